# Optimizing a Trainium2 kernel written in Bass

```python
import jax, jax.numpy as jnp
from jax import lax
import numpy as np

D_MODEL = 1024
BATCH = 32
SEQ = 256
DEPTH = 1
DEC_BATCH = 4
DEC_SEQ = 1024
PAST_LEN = 256

GRID_W = 64
MLA_HEADS = 8
QK_NOPE = 64
QK_ROPE = 32
V_HEAD = 64
Q_LORA = 256
KV_LORA = 128
MLA_WIDTH = MLA_HEADS * V_HEAD
SGU_HEADS = 8
SGU_WIDTH = D_MODEL - MLA_WIDTH
SGU_HEAD_DIM = SGU_WIDTH // SGU_HEADS
CHUNK = 128
D_FF = 4 * D_MODEL
IN_WIDTH = Q_LORA + KV_LORA + QK_ROPE + 2 * SGU_WIDTH
AXIS_PAIRS = QK_ROPE // 4
ROPE_BASE = 10000.0
EPS = 1e-6
Q_BLOCK = 128
N_MOD = 6
ATTN_SCALE = (QK_NOPE + QK_ROPE) ** -0.5

kernel_name = "hybrid_mla_sgu_prefix_dit_step"


def _rms(x, g):
    xf = x.astype(jnp.float32)
    y = xf * lax.rsqrt(jnp.mean(jnp.square(xf), axis=-1, keepdims=True) + EPS)
    return (y * g.astype(jnp.float32)).astype(x.dtype)


def _layernorm(x, g, b):
    xf = x.astype(jnp.float32)
    mu = jnp.mean(xf, axis=-1, keepdims=True)
    var = jnp.mean(jnp.square(xf - mu), axis=-1, keepdims=True)
    y = (xf - mu) * lax.rsqrt(var + EPS)
    return (y * g.astype(jnp.float32) + b.astype(jnp.float32)).astype(x.dtype)


def _axial_rope(n_tok):
    rows = n_tok // GRID_W
    row = jnp.broadcast_to(jnp.arange(rows)[:, None], (rows, GRID_W)).reshape(-1).astype(jnp.float32)
    col = jnp.broadcast_to(jnp.arange(GRID_W)[None, :], (rows, GRID_W)).reshape(-1).astype(jnp.float32)
    freqs = 1.0 / (ROPE_BASE ** (jnp.arange(AXIS_PAIRS, dtype=jnp.float32) / AXIS_PAIRS))
    ang = jnp.concatenate([row[:, None] * freqs, col[:, None] * freqs], axis=-1)
    return jnp.cos(ang), jnp.sin(ang)


def _apply_rope(x, cos, sin):
    shape = (cos.shape[0],) + (1,) * (x.ndim - 3) + (cos.shape[1],)
    c = cos.reshape(shape)
    s = sin.reshape(shape)
    xf = x.astype(jnp.float32)
    x1, x2 = xf[..., 0::2], xf[..., 1::2]
    out = jnp.stack([x1 * c - x2 * s, x1 * s + x2 * c], axis=-1).reshape(x.shape)
    return out.astype(x.dtype)


def _modulation(cond, w_mod, b_mod):
    m = jax.nn.silu(cond) @ w_mod + b_mod
    return [t[:, None, :] for t in jnp.split(m, N_MOD, axis=-1)]


def _attend(q, k, v):
    B, T, H, Dq = q.shape
    nblk = T // Q_BLOCK
    qb = q.reshape(B, nblk, Q_BLOCK, H, Dq).transpose(1, 0, 2, 3, 4)

    def one(qblk):
        s = jnp.einsum('bqhd,bkhd->bhqk', qblk, k).astype(jnp.float32) * ATTN_SCALE
        p = jax.nn.softmax(s, axis=-1).astype(v.dtype)
        return jnp.einsum('bhqk,bkhd->bqhd', p, v)

    out = lax.map(one, qb)
    return out.transpose(1, 0, 2, 3, 4).reshape(B, T, H * v.shape[-1])


def _expand_kv(ckv, kr, w_ukv):
    B, L, _ = ckv.shape
    kv = (ckv @ w_ukv).reshape(B, L, MLA_HEADS, QK_NOPE + V_HEAD)
    k_nope, val = kv[..., :QK_NOPE], kv[..., QK_NOPE:]
    k_rope = jnp.broadcast_to(kr[:, :, None, :], (B, L, MLA_HEADS, QK_ROPE))
    return jnp.concatenate([k_nope, k_rope], axis=-1), val


def _sgu(u, v, w_sgu, b_sgu, g_sgu, beta_sgu):
    B, T, _ = v.shape
    vn = _layernorm(v, g_sgu, beta_sgu).reshape(B, T // CHUNK, CHUNK, SGU_HEADS, SGU_HEAD_DIM)
    mixed = jnp.einsum('gpq,bnqgc->bnpgc', w_sgu, vn) + b_sgu.T[:, :, None]
    return u * mixed.reshape(B, T, SGU_WIDTH)


def _layer(x, mod, p, ctx_ckv=None, ctx_kr=None, rope=None):
    shift_a, scale_a, gate_a, shift_f, scale_f, gate_f = mod
    B, T, _ = x.shape
    h = _rms(x, p['g_attn_pre']) * (1.0 + scale_a) + shift_a
    proj = h @ p['w_in']
    i0 = Q_LORA
    i1 = i0 + KV_LORA
    i2 = i1 + QK_ROPE
    i3 = i2 + SGU_WIDTH
    cq, ckv, kr = proj[..., :i0], proj[..., i0:i1], proj[..., i1:i2]
    u, v = jax.nn.gelu(proj[..., i2:i3]), jax.nn.gelu(proj[..., i3:])
    q = (_rms(cq, p['g_q']) @ p['w_uq']).reshape(B, T, MLA_HEADS, QK_NOPE + QK_ROPE)
    ckv = _rms(ckv, p['g_kv'])
    if rope is None:
        k, val = _expand_kv(ckv, kr, p['w_ukv'])
    else:
        cos, sin = rope
        q = jnp.concatenate([q[..., :QK_NOPE], _apply_rope(q[..., QK_NOPE:], cos, sin)], axis=-1)
        k_lat, v_lat = _expand_kv(ckv, _apply_rope(kr, cos, sin), p['w_ukv'])
        k_ctx, v_ctx = _expand_kv(ctx_ckv, ctx_kr, p['w_ukv'])
        k = jnp.concatenate([k_ctx, k_lat], axis=1)
        val = jnp.concatenate([v_ctx, v_lat], axis=1)
    attn = _attend(q, k, val)
    sgu = _sgu(u, v, p['w_sgu'], p['b_sgu'], p['g_sgu'], p['beta_sgu'])
    mix = jnp.concatenate([attn, sgu], axis=-1) @ p['w_o']
    x = x + gate_a * _rms(mix, p['g_attn_post'])
    h = _rms(x, p['g_ffn_pre']) * (1.0 + scale_f) + shift_f
    f = jnp.square(jax.nn.relu(h @ p['w_ff1'])) @ p['w_ff2']
    x = x + gate_f * _rms(f, p['g_ffn_post'])
    return x, ckv, kr


def setup_inputs(seed: int = 0) -> dict:
    key = jax.random.key(seed)
    ks = jax.random.split(key, 24)
    n = lambda k, s, sc: jax.random.normal(k, s, jnp.float32) * sc
    L = DEPTH
    return {
        'x_prompt': n(ks[0], (BATCH, SEQ, D_MODEL), 1.0),
        'x_sample': n(ks[1], (DEC_BATCH, DEC_SEQ, D_MODEL), 1.0),
        'cache_ckv': n(ks[2], (DEC_BATCH, DEPTH, PAST_LEN, KV_LORA), 1.0),
        'cache_krope': n(ks[3], (DEC_BATCH, DEPTH, PAST_LEN, QK_ROPE), 1.0),
        'c': n(ks[4], (DEC_BATCH, D_MODEL), 1.0),
        'c_ctx': n(ks[5], (D_MODEL,), 1.0),
        'w_mod': n(ks[6], (L, D_MODEL, N_MOD * D_MODEL), 0.5 * D_MODEL ** -0.5),
        'b_mod': n(ks[7], (L, N_MOD * D_MODEL), 0.02),
        'g_attn_pre': 1.0 + n(ks[8], (L, D_MODEL), 0.05),
        'g_attn_post': 1.0 + n(ks[9], (L, D_MODEL), 0.05),
        'w_in': n(ks[10], (L, D_MODEL, IN_WIDTH), D_MODEL ** -0.5),
        'g_q': 1.0 + n(ks[11], (L, Q_LORA), 0.05),
        'w_uq': n(ks[12], (L, Q_LORA, MLA_HEADS * (QK_NOPE + QK_ROPE)), Q_LORA ** -0.5),
        'g_kv': 1.0 + n(ks[13], (L, KV_LORA), 0.05),
        'w_ukv': n(ks[14], (L, KV_LORA, MLA_HEADS * (QK_NOPE + V_HEAD)), KV_LORA ** -0.5),
        'w_sgu': n(ks[15], (L, SGU_HEADS, CHUNK, CHUNK), CHUNK ** -0.5),
        'b_sgu': 1.0 + n(ks[16], (L, SGU_HEADS, CHUNK), 0.02),
        'g_sgu': 1.0 + n(ks[17], (L, SGU_WIDTH), 0.05),
        'beta_sgu': n(ks[18], (L, SGU_WIDTH), 0.02),
        'w_o': n(ks[19], (L, D_MODEL, D_MODEL), D_MODEL ** -0.5),
        'g_ffn_pre': 1.0 + n(ks[20], (L, D_MODEL), 0.05),
        'g_ffn_post': 1.0 + n(ks[21], (L, D_MODEL), 0.05),
        'w_ff1': n(ks[22], (L, D_MODEL, D_FF), D_MODEL ** -0.5),
        'w_ff2': n(ks[23], (L, D_FF, D_MODEL), D_FF ** -0.5),
    }


def reference(x_prompt, x_sample, cache_ckv, cache_krope, c, c_ctx,
              w_mod, b_mod, g_attn_pre, g_attn_post, w_in, g_q, w_uq, g_kv, w_ukv,
              w_sgu, b_sgu, g_sgu, beta_sgu, w_o, g_ffn_pre, g_ffn_post, w_ff1, w_ff2):
    rope = _axial_rope(x_sample.shape[1])
    xp = x_prompt
    xs = x_sample
    new_ckv = []
    new_kr = []
    for l in range(DEPTH):
        p = {'g_attn_pre': g_attn_pre[l], 'g_attn_post': g_attn_post[l], 'w_in': w_in[l],
             'g_q': g_q[l], 'w_uq': w_uq[l], 'g_kv': g_kv[l], 'w_ukv': w_ukv[l],
             'w_sgu': w_sgu[l], 'b_sgu': b_sgu[l], 'g_sgu': g_sgu[l], 'beta_sgu': beta_sgu[l],
             'w_o': w_o[l], 'g_ffn_pre': g_ffn_pre[l], 'g_ffn_post': g_ffn_post[l],
             'w_ff1': w_ff1[l], 'w_ff2': w_ff2[l]}
        mod_ctx = _modulation(c_ctx[None, :], w_mod[l], b_mod[l])
        xp, ckv_l, kr_l = _layer(xp, mod_ctx, p)
        new_ckv.append(ckv_l)
        new_kr.append(kr_l)
        mod_lat = _modulation(c, w_mod[l], b_mod[l])
        xs, _, _ = _layer(xs, mod_lat, p, ctx_ckv=cache_ckv[:, l], ctx_kr=cache_krope[:, l], rope=rope)
    new_cache_ckv = jnp.stack(new_ckv, axis=1)
    new_cache_krope = jnp.stack(new_kr, axis=1)
    return (xp, xs, new_cache_ckv, new_cache_krope)
```

```python
import contextlib
import math
import numpy as np
import concourse.bass as bass
import concourse.mybir as mybir
from concourse.bass_utils import run_bass_kernel_spmd

F32 = mybir.dt.float32
BF16 = mybir.dt.bfloat16
I32 = mybir.dt.int32
AF = mybir.ActivationFunctionType
ALU = mybir.AluOpType
AX = mybir.AxisListType

D = 1024
NT_P = 8
EPS = 1e-6
ATTN_SCALE = 96.0 ** -0.5
DEBUG = False


class KB:
    def __init__(self, nc):
        self.nc = nc
        self.stack = contextlib.ExitStack()
        self.eng = {"pe": nc.tensor, "act": nc.scalar, "dve": nc.vector, "pool": nc.gpsimd, "sp": nc.sync}
        self.esem = {}
        self.ecount = {}
        for n in ("pe", "act", "dve", "pool"):
            self.esem[n] = self.stack.enter_context(nc.semaphore("s_" + n))
            self.ecount[n] = 0
        self.waited = {n: {} for n in self.eng}
        self.res = {}
        self.dsem = {}
        self.semval = {}
        self.nwaits = 0
        self.nops = 0

    def sb(self, stack, name, shape, dt):
        return stack.enter_context(self.nc.sbuf_tensor(name, list(shape), dt))

    def ps(self, stack, name, shape, dt):
        return stack.enter_context(self.nc.psum_tensor(name, list(shape), dt))

    def _r(self, name):
        if name not in self.res:
            self.res[name] = [None, []]
        return self.res[name]

    def _wait(self, engname, ev):
        if ev is None:
            return
        sem, val = ev
        w = self.waited[engname]
        if w.get(id(sem), 0) >= val:
            return
        w[id(sem)] = val
        self.eng[engname].wait_ge(sem, val)
        self.nwaits += 1

    def _deps(self, engname, reads, writes):
        for r in reads:
            st = self._r(r)
            self._wait(engname, st[0])
            if r.startswith("ps"):
                mine = id(self.esem.get(engname))
                for ev in st[1]:
                    if id(ev[0]) != mine:
                        self._wait(engname, ev)
        for w in writes:
            st = self._r(w)
            self._wait(engname, st[0])
            for ev in st[1]:
                self._wait(engname, ev)

    def _record(self, ev, reads, writes):
        for r in reads:
            lst = self._r(r)[1]
            lst.append(ev)
            if len(lst) > 64:
                best = {}
                for e in lst:
                    if id(e[0]) not in best or best[id(e[0])][1] < e[1]:
                        best[id(e[0])] = e
                lst[:] = list(best.values())
        for w in writes:
            st = self._r(w)
            st[0] = ev
            st[1] = []
        self.semval[id(ev[0])] = ev

    def op(self, engname, fn, reads=(), writes=()):
        self._deps(engname, reads, writes)
        ins = fn(self.eng[engname])
        self.ecount[engname] += 1
        ev = (self.esem[engname], self.ecount[engname])
        ins.then_inc(ev[0], 1)
        self._record(ev, reads, writes)
        self.nops += 1
        return ev

    def group(self, engname, fns, reads=(), writes=()):
        self._deps(engname, reads, writes)
        ins = None
        for fn in fns:
            ins = fn(self.eng[engname])
            self.nops += 1
        self.ecount[engname] += 1
        ev = (self.esem[engname], self.ecount[engname])
        ins.then_inc(ev[0], 1)
        self._record(ev, reads, writes)
        return ev

    def dma(self, q, out, in_, res, write, semkey=None, **kw):
        reads = [] if write else [res]
        writes = [res] if write else []
        self._deps(q, reads, writes)
        sk = semkey or res
        if sk not in self.dsem:
            self.dsem[sk] = [self.stack.enter_context(self.nc.semaphore("d_" + sk.replace(".", "_"))), 0]
        d = self.dsem[sk]
        self.eng[q].dma_start(out=out, in_=in_, **kw).then_inc(d[0], 16)
        d[1] += 16
        ev = (d[0], d[1])
        self._record(ev, reads, writes)
        return ev

    def barrier(self, exclude=()):
        skip = set()
        for name, d in self.dsem.items():
            if any(name.startswith(p) for p in exclude):
                skip.add(id(d[0]))
        for e in self.eng:
            for sid, ev in list(self.semval.items()):
                if sid in skip:
                    continue
                self._wait(e, ev)

    def finish(self, engname="sp"):
        for sid, ev in list(self.semval.items()):
            self._wait(engname, ev)

    def close(self):
        self.stack.close()


def build_program(debug=False):
    nc = bass.Bass("TRN2", target_bir_lowering=False)

    def din(name, shape):
        return nc.dram_tensor(name, list(shape), F32, kind="ExternalInput").ap()

    def dout(name, shape):
        return nc.dram_tensor(name, list(shape), F32, kind="ExternalOutput").ap()

    xp = din("xp", [1024, D])
    xs = din("xs", [1024, D])
    cckv = din("cckv", [256, 128])
    ckr = din("ckr", [256, 32])
    cond = din("cond", [2, D])
    meta = din("meta", [2])
    w_mod = din("w_mod", [D, 6 * D])
    b_mod = din("b_mod", [6 * D])
    g_attn_pre = din("g_attn_pre", [D])
    g_attn_post = din("g_attn_post", [D])
    w_in = din("w_in", [D, 1440])
    g_q = din("g_q", [256])
    w_uq = din("w_uq", [256, 768])
    g_kv = din("g_kv", [128])
    w_ukv = din("w_ukv", [128, 1024])
    w_sgu = din("w_sgu", [8, 128, 128])
    b_sgu = din("b_sgu", [8, 128])
    g_sgu = din("g_sgu", [512])
    beta_sgu = din("beta_sgu", [512])
    w_o = din("w_o", [D, D])
    g_ffn_pre = din("g_ffn_pre", [D])
    g_ffn_post = din("g_ffn_post", [D])
    w_ff1 = din("w_ff1", [D, 4 * D])
    w_ff2 = din("w_ff2", [4 * D, D])

    yp = dout("yp", [1024, D])
    ys = dout("ys", [512, D])
    nckv = dout("nckv", [1024, 128])
    nkr = dout("nkr", [1024, 32])

    kb = KB(nc)
    dbg_outs = {}

    def dbg(name, ap, res, shape, dt=F32, stk=None):
        if not debug:
            return
        o = nc.dram_tensor("dbg_" + name, list(shape), dt, kind="ExternalOutput").ap()
        dbg_outs[name] = o
        kb.dma("sp", o, ap, res, False)

    def act(out, in_, func, reads, writes, **kw):
        return kb.op("act", lambda e: e.activation(out=out, in_=in_, func=func, **kw), reads, writes)

    def tt(eng, out, in0, in1, op, reads, writes):
        return kb.op(eng, lambda e: e.tensor_tensor(out=out, in0=in0, in1=in1, op=op), reads, writes)

    def ts(eng, out, in0, s1, s2, op0, op1, reads, writes):
        if op1 is None:
            return kb.op(eng, lambda e: e.tensor_scalar(out=out, in0=in0, scalar1=s1, scalar2=None, op0=op0), reads, writes)
        return kb.op(eng, lambda e: e.tensor_scalar(out=out, in0=in0, scalar1=s1, scalar2=s2, op0=op0, op1=op1), reads, writes)

    def stt(eng, out, in0, scalar, in1, op0, op1, reads, writes):
        return kb.op(eng, lambda e: e.scalar_tensor_tensor(out=out, in0=in0, scalar=scalar, in1=in1, op0=op0, op1=op1), reads, writes)

    def cp(eng, out, in_, reads, writes):
        return kb.op(eng, lambda e: e.tensor_copy(out, in_), reads, writes)

    def mm(out, lhsT, rhs, start, stop):
        return lambda e: e.matmul(out, lhsT, rhs, start=start, stop=stop)

    def tr(out, in_, ident):
        return lambda e: e.transpose(out, in_, ident)

    def rstd_from_ss(dst, ss, n, reads, writes):
        act(dst, ss, AF.Ln, list(reads) + ["eps"], writes, scale=1.0 / n, bias=eps_col[:, 0:1])
        act(dst, dst, AF.Exp, writes, writes, scale=-0.5)

    PS = kb.stack
    psb = [kb.ps(PS, "psb%d" % i, [128, 512], F32) for i in range(8)]

    def psf(i):
        return psb[i][:, :]

    def psh(i):
        return psb[i][:, :].bitcast(BF16)

    x1 = kb.sb(PS, "x1", [128, 12, D], F32)
    ident_bf = kb.sb(PS, "ident_bf", [128, 128], BF16)
    ident_f = kb.sb(PS, "ident_f", [128, 128], F32)
    ones_bf = kb.sb(PS, "ones_bf", [128, 128], BF16)
    eps_col = kb.sb(PS, "eps_col", [128, 1], F32)
    srep = kb.sb(PS, "srep", [128, 16, 128], BF16)
    cols = kb.sb(PS, "cols", [128, 18], F32)
    modA = kb.sb(PS, "modA", [128, 2, 2, 8], F32)
    modB = kb.sb(PS, "modB", [128, 2, 2, 8], F32)
    stat = kb.sb(PS, "stat", [128, 96], F32)
    gcolF = kb.sb(PS, "gcolF", [128, 2, 8], F32)
    coltL = kb.sb(PS, "coltL", [128, 4], F32)

    kb.op("pool", lambda e: e.memset(ident_bf[:], 0.0), writes=["ident_bf"])
    kb.op("pool", lambda e: e.affine_select(out=ident_bf[:], in_=ident_bf[:], compare_op=ALU.not_equal, fill=1.0,
                                            base=0, pattern=[[-1, 128]], channel_multiplier=1), reads=["ident_bf"], writes=["ident_bf"])
    kb.op("pool", lambda e: e.memset(ident_f[:], 0.0), writes=["ident_f"])
    kb.op("pool", lambda e: e.affine_select(out=ident_f[:], in_=ident_f[:], compare_op=ALU.not_equal, fill=1.0,
                                            base=0, pattern=[[-1, 128]], channel_multiplier=1), reads=["ident_f"], writes=["ident_f"])
    kb.op("pool", lambda e: e.memset(ones_bf[:], 1.0), writes=["ones_bf"])

    kb.op("pool", lambda e: e.memset(eps_col[:], EPS), writes=["eps"])

    P1 = contextlib.ExitStack()
    w_in_bf = kb.sb(P1, "w_in_bf", [128, 8, 1440], BF16)
    wkr_pad = kb.sb(P1, "wkr_pad", [128, 8, 96], BF16)
    wkr_swp = kb.sb(P1, "wkr_swp", [128, 8, 96], BF16)
    w_uq_bf = kb.sb(P1, "w_uq_bf", [128, 2, 8, 96], BF16)
    w_uq_swp = kb.sb(P1, "w_uq_swp", [128, 2, 8, 96], BF16)
    w_ukv_bf = kb.sb(P1, "w_ukv_bf", [128, 8, 128], BF16)
    wsguT = kb.sb(P1, "wsguT", [128, 8, 128], BF16)
    w_o_bf = kb.sb(P1, "w_o_bf", [128, 8, D], BF16)
    G_a = kb.sb(P1, "G_a", [128, 2, D], F32)
    gkv_bc = kb.sb(P1, "gkv_bc", [128, 128], F32)
    gsgu_bc = kb.sb(P1, "gsgu_bc", [128, 512], F32)
    beta_bc = kb.sb(P1, "beta_bc", [128, 512], F32)
    bT_bc = kb.sb(P1, "bT_bc", [128, 8, 128], F32)
    bTb = kb.sb(P1, "bTb", [128, 8, 128], BF16)
    cst_bf = kb.sb(P1, "cst_bf", [128, 128], BF16)
    CT = kb.sb(P1, "CT", [128, 1024], F32)
    ST = kb.sb(P1, "ST", [128, 1024], F32)

    kb.op("pool", lambda e: e.memset(cst_bf[:], 1.0 / 128.0), writes=["cst_bf"])

    T0 = contextlib.ExitStack()
    rows = kb.sb(T0, "rows", [32, 128], F32)
    rows_s = kb.sb(T0, "rows_s", [32, 128], F32)
    rows_b = kb.sb(T0, "rows_b", [32, 128], BF16)
    sT = kb.sb(T0, "sT", [128, 16], BF16)
    wsg_ld = kb.sb(T0, "wsg_ld", [128, 8, 128], BF16)
    mt = kb.sb(T0, "mt", [128, 2], F32)
    pidx = kb.sb(T0, "pidx", [128, 1], I32)
    ktmp = kb.sb(T0, "ktmp", [128, 1], I32)
    kf = kb.sb(T0, "kf", [128, 1], F32)
    freq = kb.sb(T0, "freq", [128, 1], F32)
    isrow = kb.sb(T0, "isrow", [128, 1], F32)
    rc = kb.sb(T0, "rc", [128, 80], F32)
    tm80 = kb.sb(T0, "tm80", [128, 80], F32)
    m80 = kb.sb(T0, "m80", [128, 80], F32)
    ti80 = kb.sb(T0, "ti80", [128, 80], I32)
    sn80 = kb.sb(T0, "sn80", [128, 80], F32)
    cs80 = kb.sb(T0, "cs80", [128, 80], F32)
    notrow = kb.sb(T0, "notrow", [128, 1], F32)
    wm = [kb.sb(T0, "wm%d" % i, [128, 8, 512], BF16) for i in range(2)]
    bm = [kb.sb(T0, "bm%d" % i, [128, 512], F32) for i in range(2)]
    mrow = [kb.sb(T0, "mrow%d" % i, [128, 512], F32) for i in range(2)]
    scr = kb.sb(T0, "scr", [128, 4, 128], F32)
    colt = kb.sb(T0, "colt", [128, 4], F32)

    def mod_dma(j):
        b = j % 2
        kb.dma("pool", wm[b][:], w_mod[:, j * 512:(j + 1) * 512].rearrange("(kc p) n -> p kc n", p=128), "wm%d" % b, True)
        kb.dma("sp", bm[b][:], b_mod[j * 512:(j + 1) * 512].partition_broadcast(128), "bm%d" % b, True)

    kb.dma("sp", rows_s[0:16, :], cond.rearrange("c (k p) -> (c k) p", p=128), "rows_s", True)
    kb.dma("sp", rows[0:8, :], g_attn_pre.rearrange("(k p) -> k p", p=128), "rows", True)
    kb.dma("sp", rows[8:16, :], g_ffn_pre.rearrange("(k p) -> k p", p=128), "rows", True)
    kb.dma("sp", rows[16:18, :], g_q.rearrange("(k p) -> k p", p=128), "rows", True)
    mod_dma(0)
    mod_dma(1)
    kb.dma("pool", w_in_bf[:], w_in.rearrange("(kc p) n -> p kc n", p=128), "w_in", True)

    kb.dma("sp", gkv_bc[:], g_kv.partition_broadcast(128), "gkv_bc", True)
    kb.dma("sp", gsgu_bc[:], g_sgu.partition_broadcast(128), "gsgu_bc", True)
    kb.dma("sp", beta_bc[:], beta_sgu.partition_broadcast(128), "beta_bc", True)
    kb.dma("sp", bT_bc[:].rearrange("p g q -> p (g q)"), b_sgu.rearrange("g p -> (g p)").partition_broadcast(128), "bT_bc", True)
    for c in range(2):
        kb.dma("sp", G_a[:, c, :], g_attn_post.partition_broadcast(128), "G_a%d" % c, True)
    kb.group("pe", [mm(psf(0)[:, 0:18], rows[0:18, :], ident_f[0:18, 0:18], True, True)], reads=["rows", "ident_f"], writes=["ps0"])
    cp("dve", cols[:, :], psf(0)[:, 0:18], ["ps0"], ["cols"])
    act(rows_s[0:16, :], rows_s[0:16, :], AF.Silu, ["rows_s"], ["rows_s"])
    cp("dve", rows_b[0:16, :], rows_s[0:16, :], ["rows_s"], ["rows_b"])
    kb.group("pe", [mm(psf(1)[:, 0:16], rows_b[0:16, :], ident_bf[0:16, 0:16], True, True)], reads=["rows_b", "ident_bf"], writes=["ps1"])
    cp("dve", sT[:, :], psf(1)[:, 0:16], ["ps1"], ["sT"])
    for r in range(16):
        cp("dve" if r % 2 == 0 else "pool", srep[:, r, :], sT[:, r:r + 1].to_broadcast([128, 128]), ["sT"], ["srep"])

    def emit_mod(j, Gt, gname, which):
        vec, half = (j // 2) % 3, j % 2
        b = j % 2
        for c in range(2):
            pb = 4 + c
            kb.group("pe", [mm(psf(pb), srep[:, c * 8 + kc, :], wm[b][:, kc, :], kc == 0, kc == 7) for kc in range(8)],
                     reads=["srep", "wm%d" % b], writes=["ps%d" % pb])
            tt("dve", mrow[c][:], psf(pb), bm[b][:], ALU.add, ["ps%d" % pb, "bm%d" % b], ["mrow%d" % c])
            if vec == 2:
                sl = slice(half * 512, (half + 1) * 512)
                tt("dve", Gt[:, c, sl], Gt[:, c, sl], mrow[c][:], ALU.mult, ["%s%d" % (gname, c), "mrow%d" % c], ["%s%d" % (gname, c)])
            else:
                for a in range(4):
                    tt("dve", scr[:, a, :], mrow[c][:, a * 128:(a + 1) * 128], ident_f[:, :], ALU.mult, ["mrow%d" % c, "ident_f"], ["scr"])
                kb.op("dve", lambda e: e.tensor_reduce(out=colt[:, :], in_=scr[:, :, :], axis=AX.X, op=ALU.add), reads=["scr"], writes=["colt"])
                if vec == 0:
                    cp("dve", modB[:, which, c, half * 4:(half + 1) * 4], colt[:, :], ["colt"], ["modB"])
                else:
                    gc = cols[:, which * 8 + half * 4: which * 8 + half * 4 + 4]
                    stt("dve", modA[:, which, c, half * 4:(half + 1) * 4], colt[:, :], 1.0, gc, ALU.add, ALU.mult, ["colt", "cols"], ["modA"])

    def rope_gen():
        kb.dma("sp", mt[:], meta.partition_broadcast(128), "mt", True)
        yield
        kb.op("pool", lambda e: e.iota(pidx[:], pattern=[[0, 1]], base=0, channel_multiplier=1), writes=["pidx"])
        yield
        kb.op("pool", lambda e: e.iota(rc[:, 0:16], pattern=[[1, 16]], base=0, channel_multiplier=0, allow_small_or_imprecise_dtypes=True), writes=["rc"])
        yield
        kb.op("pool", lambda e: e.iota(rc[:, 16:80], pattern=[[1, 64]], base=0, channel_multiplier=0, allow_small_or_imprecise_dtypes=True), writes=["rc"])
        yield
        ts("dve", ktmp[:], pidx[:], 1, 7, ALU.arith_shift_right, ALU.bitwise_and, ["pidx"], ["ktmp"])
        yield
        cp("dve", kf[:], ktmp[:], ["ktmp"], ["kf"])
        yield
        act(freq[:], kf[:], AF.Exp, ["kf"], ["freq"], scale=-math.log(10000.0) / 8.0)
        yield
        ts("dve", ktmp[:], pidx[:], 4, 1, ALU.arith_shift_right, ALU.bitwise_and, ["pidx"], ["ktmp"])
        yield
        cp("dve", kf[:], ktmp[:], ["ktmp", "freq"], ["kf"])
        yield
        ts("dve", isrow[:], kf[:], -1.0, 1.0, ALU.mult, ALU.add, ["kf"], ["isrow"])
        yield
        cp("dve", notrow[:], kf[:], ["kf"], ["notrow"])
        yield
        ts("dve", rc[:, 0:8], rc[:, 0:8], mt[:, 0:1], None, ALU.add, None, ["rc", "mt"], ["rc"])
        yield
        ts("dve", rc[:, 8:16], rc[:, 8:16], mt[:, 1:2], -8.0, ALU.add, ALU.add, ["rc", "mt"], ["rc"])
        yield
        ts("dve", rc[:, :], rc[:, :], freq[:, 0:1], None, ALU.mult, None, ["rc", "freq"], ["rc"])
        yield

        def sin_of(dst, dname, shift):
            ts("dve", tm80[:], rc[:], shift, None, ALU.add, None, ["rc"], ["tm80"])
            yield
            ts("dve", m80[:], tm80[:], 1.0 / (2 * math.pi), None, ALU.mult, None, ["tm80"], ["m80"])
            yield
            cp("dve", ti80[:], m80[:], ["m80"], ["ti80"])
            yield
            cp("dve", m80[:], ti80[:], ["ti80"], ["m80"])
            yield
            stt("dve", tm80[:], m80[:], -2 * math.pi, tm80[:], ALU.mult, ALU.add, ["m80", "tm80"], ["tm80"])
            yield
            ts("dve", m80[:], tm80[:], math.pi, -2 * math.pi, ALU.is_gt, ALU.mult, ["tm80"], ["m80"])
            yield
            tt("dve", tm80[:], tm80[:], m80[:], ALU.add, ["tm80", "m80"], ["tm80"])
            yield
            ts("dve", m80[:], tm80[:], -math.pi, 2 * math.pi, ALU.is_lt, ALU.mult, ["tm80"], ["m80"])
            yield
            tt("dve", tm80[:], tm80[:], m80[:], ALU.add, ["tm80", "m80"], ["tm80"])
            yield
            act(dst[:], tm80[:], AF.Sin, ["tm80"], [dname])
            yield
        yield from sin_of(sn80, "sn80", 0.0)
        yield from sin_of(cs80, "cs80", 0.5 * math.pi)
        for tab, src, sname, tname in ((ST, sn80, "sn80", "ST"), (CT, cs80, "cs80", "CT")):
            tv = tab[:, :].rearrange("p (r c) -> p r c", r=16)
            rb = src[:, 0:16].rearrange("p (r o) -> p r o", o=1).broadcast_to([128, 16, 64])
            cb = src[:, 16:80].rearrange("p (o c) -> p o c", o=1).broadcast_to([128, 16, 64])
            ts("dve", tv, rb, isrow[:, 0:1], None, ALU.mult, None, [sname, "isrow"], [tname])
            yield
            stt("dve", tv, cb, notrow[:, 0:1], tv, ALU.mult, ALU.add, [sname, "notrow", tname], [tname])
            yield

    rope_it = rope_gen()

    def rope_steps(n):
        for _ in range(n):
            try:
                next(rope_it)
            except StopIteration:
                return

    emit_mod(0, G_a, "G_a", 0)
    rope_steps(12)
    mod_dma(2)
    kb.dma("pool", w_uq_bf[:].rearrange("p c h d -> p c (h d)"), w_uq.rearrange("(kc p) n -> p kc n", p=128), "w_uq", True)
    kb.dma("pool", w_ukv_bf[:].rearrange("p h d -> p (h d)"), w_ukv, "w_ukv", True)
    kb.dma("pool", wsg_ld[:], w_sgu.rearrange("g p q -> p g q"), "wsg_ld", True)
    emit_mod(1, G_a, "G_a", 0)
    rope_steps(12)
    mod_dma(3)
    emit_mod(2, G_a, "G_a", 0)
    rope_steps(14)
    emit_mod(3, G_a, "G_a", 0)
    rope_steps(100)

    cp("pool", bTb[:, :, :], bT_bc[:, :, :], ["bT_bc"], ["bTb"])
    kb.op("pool", lambda e: e.memset(wkr_pad[:], 0.0), writes=["wkr_pad"])
    kb.op("pool", lambda e: e.memset(wkr_swp[:], 0.0), writes=["wkr_swp"])
    kb.op("pool", lambda e: e.memset(w_uq_swp[:], 0.0), writes=["w_uq_swp"])
    cp("pool", wkr_pad[:, :, 64:96], w_in_bf[:, :, 384:416], ["w_in"], ["wkr_pad"])
    ts("dve", wkr_swp[:, :, 64:96:2], w_in_bf[:, :, 385:416:2], -1.0, None, ALU.mult, None, ["w_in"], ["wkr_swp"])
    cp("dve", wkr_swp[:, :, 65:96:2], w_in_bf[:, :, 384:416:2], ["w_in"], ["wkr_swp"])
    for c in range(2):
        ts("dve", w_uq_swp[:, c, :, 64:96:2], w_uq_bf[:, c, :, 65:96:2], -1.0, None, ALU.mult, None, ["w_uq"], ["w_uq_swp"])
        cp("dve", w_uq_swp[:, c, :, 65:96:2], w_uq_bf[:, c, :, 64:96:2], ["w_uq"], ["w_uq_swp"])
    for g in range(8):
        b = g % 2
        kb.group("pe", [tr(psh(b)[:, 0:128], wsg_ld[:, g, :], ident_bf[:, :])], reads=["wsg_ld", "ident_bf"], writes=["ps%d" % b])
        cp("dve", wsguT[:, g, :], psh(b)[:, 0:128], ["ps%d" % b], ["wsguT"])


    kb.barrier(exclude=("w_o",))
    T0.close()

    W1 = contextlib.ExitStack()
    hT = kb.sb(W1, "hT", [128, 8, 512], BF16)
    xq = kb.sb(W1, "xq", [128, 4096], BF16)
    uT = kb.sb(W1, "uT", [128, 4, 512], BF16)
    gv = kb.sb(W1, "gv", [128, 4, 512], F32)
    vn = kb.sb(W1, "vn", [128, 4, 512], BF16)
    cqn_bf = kb.sb(W1, "cqn_bf", [128, 4, 256], BF16)
    ckvn_f = kb.sb(W1, "ckvn_f", [128, 4, 128], F32)
    ckvn_bf = kb.sb(W1, "ckvn_bf", [128, 4, 128], BF16)
    kr_f = kb.sb(W1, "kr_f", [128, 4, 32], F32)
    cqnT = kb.sb(W1, "cqnT", [128, 2, 512], BF16)
    ckvT = kb.sb(W1, "ckvT", [128, 1280], BF16)
    krT = kb.sb(W1, "krT", [128, 1280], BF16)
    KTh = [kb.sb(W1, "KTh%d" % i, [128, 1280], BF16) for i in range(2)]
    Vt = kb.sb(W1, "Vt", [128, 10, 512], BF16)
    PT = [kb.sb(W1, "PT%d" % i, [128, 512], BF16) for i in range(3)]
    rec = kb.sb(W1, "rec", [128, 512], F32)
    junk = kb.sb(W1, "junk", [128, 1024], BF16)
    cc_ld = kb.sb(W1, "cc_ld", [128, 2, 128], BF16)
    ck_ld = kb.sb(W1, "ck_ld", [128, 2, 96], BF16)
    xnb = xq[:, :].rearrange("p (t d) -> p t d", t=4)
    qT = xq[:, :].rearrange("p (h n) -> p h n", h=8)

    wmL = x1[:, 8:10, :].rearrange("p t d -> p (t d)").bitcast(BF16).rearrange("p (k n) -> p k n", k=8)
    wmLn = ["x1.8", "x1.9"]
    bmL = x1[:, 10, 0:512]
    mrowL = [x1[:, 10, 512:1024], x1[:, 11, 0:512]]
    scrL = x1[:, 11, 512:1024].rearrange("p (a q) -> p a q", a=4)

    def late_dma(j):
        kb.dma("pool", wmL[:, 0:4, :], w_mod[0:512, j * 512:(j + 1) * 512].rearrange("(kc p) n -> p kc n", p=128), wmLn[0], True, semkey="late.a")
        kb.dma("pool", wmL[:, 4:8, :], w_mod[512:1024, j * 512:(j + 1) * 512].rearrange("(kc p) n -> p kc n", p=128), wmLn[1], True, semkey="late.b")
        kb.dma("pool", bmL, b_mod[j * 512:(j + 1) * 512].partition_broadcast(128), "x1.10", True, semkey="late.c")

    def late_mod(j):
        which = j // 6
        vec, half = (j // 2) % 3, j % 2
        for c in range(2):
            pb = 2 + c
            kb.group("pe", [mm(psf(pb), srep[:, c * 8 + kc, :], wmL[:, kc, :], kc == 0, kc == 7) for kc in range(8)],
                     reads=["srep"] + wmLn, writes=["ps%d" % pb])
            mn = "x1.10" if c == 0 else "x1.11"
            tt("dve", mrowL[c], psf(pb), bmL, ALU.add, ["ps%d" % pb, "x1.10"], [mn])
            if vec == 2 and which == 0:
                sl = slice(half * 512, (half + 1) * 512)
                tt("dve", G_a[:, c, sl], G_a[:, c, sl], mrowL[c], ALU.mult, ["G_a%d" % c, mn], ["G_a%d" % c])
            else:
                tt("dve", scrL, mrowL[c].rearrange("p (a q) -> p a q", a=4),
                   ident_f[:, :].rearrange("p (o q) -> p o q", o=1).broadcast_to([128, 4, 128]), ALU.mult, [mn, "ident_f"], ["x1.11"])
                kb.op("dve", lambda e: e.tensor_reduce(out=coltL[:, :], in_=scrL, axis=AX.X, op=ALU.add), reads=["x1.11"], writes=["coltL"])
                if vec == 0:
                    cp("dve", modB[:, which, c, half * 4:(half + 1) * 4], coltL[:, :], ["coltL"], ["modB"])
                elif vec == 1:
                    gc = cols[:, which * 8 + half * 4: which * 8 + half * 4 + 4]
                    stt("dve", modA[:, which, c, half * 4:(half + 1) * 4], coltL[:, :], 1.0, gc, ALU.add, ALU.mult, ["coltL", "cols"], ["modA"])
                else:
                    cp("dve", gcolF[:, c, half * 4:(half + 1) * 4], coltL[:, :], ["coltL"], ["gcolF"])

    late_next = [4]

    def late_step():
        j = late_next[0]
        if j >= 12:
            return
        late_mod(j)
        if j + 1 < 12:
            late_dma(j + 1)
        late_next[0] = j + 1

    stage_hooks = []
    head_hooks = []
    deferred = []

    def drain(n):
        for _ in range(n):
            if deferred:
                deferred.pop(0)()

    def front_A(x_src, slots):
        for t in range(4):
            kb.dma("sp", x1[:, slots[t], :], x_src[t * 128:(t + 1) * 128, :], "x1.%d" % slots[t], True)
            act(junk[:, :], x1[:, slots[t], :], AF.Square, ["x1.%d" % slots[t]], ["junk", "stA%d" % t], accum_out=stat[:, t:t + 1])
        rstd_from_ss(stat[:, 4:8], stat[:, 0:4], D, ["stA0", "stA1", "stA2", "stA3"], ["stA_r"])
        for t in range(4):
            ts("dve", xnb[:, t, :], x1[:, slots[t], :], stat[:, 4 + t:5 + t], None, ALU.mult, None,
               ["x1.%d" % slots[t], "stA_r"], ["xq"])

    def front_B(slots, c, full, kcol0, rope0, out_row0):
        for kc in range(8):
            b = kc % 2
            kb.group("pe", [tr(psh(b)[:, t * 128:(t + 1) * 128], xnb[:, t, kc * 128:(kc + 1) * 128], ident_bf[:, :]) for t in range(4)],
                     reads=["xq", "ident_bf"], writes=["ps%d" % b])
            if kc % 2 == 0:
                act(hT[:, kc, :], psh(b)[:, 0:512], AF.Identity, ["ps%d" % b, "modA", "modB"], ["hT.%d" % kc],
                    scale=modA[:, 0, c, kc:kc + 1], bias=modB[:, 0, c, kc:kc + 1])
            else:
                ts("dve", hT[:, kc, :], psh(b)[:, 0:512], modA[:, 0, c, kc:kc + 1], modB[:, 0, c, kc:kc + 1], ALU.mult, ALU.add,
                   ["ps%d" % b, "modA", "modB"], ["hT.%d" % kc])
        hreads = ["hT.%d" % k for k in range(8)]
        lo = 0 if full else 256
        for t in range(4):
            b = 2 + t
            kb.group("pe", [mm(psf(b)[:, lo:416], hT[:, kc, t * 128:(t + 1) * 128], w_in_bf[:, kc, lo:416], kc == 0, kc == 7) for kc in range(8)],
                     reads=hreads + ["w_in"], writes=["ps%d" % b])
            o = 64 + 4 * t
            sc, scr_ = "stC%d" % t, "stCr%d" % t
            if full:
                act(junk[:, 0:256], psf(b)[:, 0:256], AF.Square, ["ps%d" % b], ["junk", sc], accum_out=stat[:, o:o + 1], scale=1.0 / 16.0)
            act(junk[:, 256:384], psf(b)[:, 256:384], AF.Square, ["ps%d" % b], ["junk", sc], accum_out=stat[:, o + 1:o + 2], scale=128.0 ** -0.5)
            l0 = o if full else o + 1
            act(stat[:, l0 + 2:o + 4], stat[:, l0:o + 2], AF.Ln, [sc, "eps"], [scr_], scale=1.0, bias=eps_col[:, 0:1])
            act(stat[:, l0 + 2:o + 4], stat[:, l0 + 2:o + 4], AF.Exp, [scr_], [scr_], scale=-0.5)
            if full:
                ts("dve", cqn_bf[:, t, :], psf(b)[:, 0:256], stat[:, o + 2:o + 3], None, ALU.mult, None, ["ps%d" % b, scr_], ["cqn_bf"])
            stt("dve", ckvn_f[:, t, :], psf(b)[:, 256:384], stat[:, o + 3:o + 4], gkv_bc[:, :], ALU.mult, ALU.mult,
                ["ps%d" % b, scr_, "gkv_bc"], ["ckvn_f.%d" % t])
            cp("pool", ckvn_bf[:, t, :], ckvn_f[:, t, :], ["ckvn_f.%d" % t], ["ckvn_bf"])
            if out_row0 is not None:
                cp("dve", kr_f[:, t, :], psf(b)[:, 384:416], ["ps%d" % b], ["kr_f.%d" % t])
                kb.dma("sp", nckv[out_row0 + t * 128: out_row0 + (t + 1) * 128, :], ckvn_f[:, t, :], "ckvn_f.%d" % t, False)
                kb.dma("sp", nkr[out_row0 + t * 128: out_row0 + (t + 1) * 128, :], kr_f[:, t, :], "kr_f.%d" % t, False)
        kb.group("pe", [tr(psh(0)[:, t * 128:(t + 1) * 128], ckvn_bf[:, t, :], ident_bf[:, :]) for t in range(4)],
                 reads=["ckvn_bf", "ident_bf"], writes=["ps0"])
        cp("dve", ckvT[:, kcol0:kcol0 + 512], psh(0)[:, 0:512], ["ps0"], ["ckvT.%d" % (kcol0 // 256), "ckvT.%d" % (kcol0 // 256 + 1)])
        if full:
            for cc in range(2):
                kb.group("pe", [tr(psh(1)[:, t * 128:(t + 1) * 128], cqn_bf[:, t, cc * 128:(cc + 1) * 128], ident_bf[:, :]) for t in range(4)],
                         reads=["cqn_bf", "ident_bf"], writes=["ps1"])
                act(cqnT[:, cc, :], psh(1)[:, 0:512], AF.Copy, ["ps1", "cols"], ["cqnT"], scale=cols[:, 16 + cc:17 + cc])
        kres = ["krT.%d" % (kcol0 // 256), "krT.%d" % (kcol0 // 256 + 1)]
        kb.group("pe", [mm(psf(4)[0:96, :], wkr_pad[:, kc, :], hT[:, kc, :], kc == 0, kc == 7) for kc in range(8)],
                 reads=hreads + ["wkr_pad"], writes=["ps4"])
        if rope0 is None:
            cp("dve", krT[64:96, kcol0:kcol0 + 512], psf(4)[64:96, :], ["ps4"], kres)
        else:
            kb.group("pe", [mm(psf(5)[0:96, :], wkr_swp[:, kc, :], hT[:, kc, :], kc == 0, kc == 7) for kc in range(8)],
                     reads=hreads + ["wkr_swp"], writes=["ps5"])
            g0 = gv[64:96, 0, :]
            g1 = gv[64:96, 1, :]
            tt("dve", g0, psf(4)[64:96, :], CT[64:96, rope0:rope0 + 512], ALU.mult, ["ps4", "CT"], ["gv.0"])
            tt("dve", g1, psf(5)[64:96, :], ST[64:96, rope0:rope0 + 512], ALU.mult, ["ps5", "ST"], ["gv.1"])
            tt("dve", krT[64:96, kcol0:kcol0 + 512], g0, g1, ALU.add, ["gv.0", "gv.1"], kres)
        if not full:
            return
        if stage_hooks:
            stage_hooks.pop(0)()
        for cu in range(4):
            b = 4 + cu % 2
            kb.group("pe", [mm(psf(b), w_in_bf[:, kc, 416 + cu * 128: 416 + (cu + 1) * 128], hT[:, kc, :], kc == 0, kc == 7) for kc in range(8)],
                     reads=hreads + ["w_in"], writes=["ps%d" % b])
            act(uT[:, cu, :], psf(b), AF.Gelu_apprx_tanh, ["ps%d" % b], ["uT"])
        for t in range(4):
            b = 2 + t % 2
            kb.group("pe", [mm(psf(b), hT[:, kc, t * 128:(t + 1) * 128], w_in_bf[:, kc, 928:1440], kc == 0, kc == 7) for kc in range(8)],
                     reads=hreads + ["w_in"], writes=["ps%d" % b])
            act(gv[:, t, :], psf(b), AF.Gelu_apprx_tanh, ["ps%d" % b], ["gv.%d" % t, "stV"], accum_out=stat[:, 16 + t:17 + t])
            act(junk[:, 0:512], gv[:, t, :], AF.Square, ["gv.%d" % t], ["junk", "stV"], accum_out=stat[:, 20 + t:21 + t])
        if stage_hooks:
            stage_hooks.pop(0)()
        for h in range(8):
            b = 4 + h % 2
            if rope0 is None:
                kb.group("pe", [mm(psf(b)[0:96, :], w_uq_bf[:, cc, h, :], cqnT[:, cc, :], cc == 0, cc == 1) for cc in range(2)],
                         reads=["w_uq", "cqnT"], writes=["ps%d" % b])
                if h % 2 == 0:
                    act(qT[0:96, h, :], psf(b)[0:96, :], AF.Copy, ["ps%d" % b], ["xq"])
                else:
                    cp("dve", qT[0:96, h, :], psf(b)[0:96, :], ["ps%d" % b], ["xq"])
            else:
                kb.group("pe", [mm(psf(b)[0:96, :], w_uq_bf[:, cc, h, :], cqnT[:, cc, :], cc == 0, cc == 1) for cc in range(2)],
                         reads=["w_uq", "cqnT"], writes=["ps%d" % b])
                b2 = 6 + h % 2
                kb.group("pe", [mm(psf(b2)[0:96, :], w_uq_swp[:, cc, h, :], cqnT[:, cc, :], cc == 0, cc == 1) for cc in range(2)],
                         reads=["w_uq_swp", "cqnT"], writes=["ps%d" % b2])
                vnf = vn[:, :, :].rearrange("p t d -> p (t d)").bitcast(F32)
                g0 = vnf[64:96, 0:512]
                g1 = vnf[64:96, 512:1024]
                tt("dve", g0, psf(b)[64:96, :], CT[64:96, rope0:rope0 + 512], ALU.mult, ["ps%d" % b, "CT", "vn"], ["vn", "qlock%d" % b])
                act(qT[0:64, h, :], psf(b)[0:64, :], AF.Copy, ["ps%d" % b, "qlock%d" % b], ["xq"])
                tt("dve", g1, psf(b2)[64:96, :], ST[64:96, rope0:rope0 + 512], ALU.mult, ["ps%d" % b2, "ST", "vn"], ["vn"])
                tt("dve", qT[64:96, h, :], g0, g1, ALU.add, ["vn"], ["xq"])

        ts("dve", stat[:, 24:28], stat[:, 16:20], 1.0 / 512, None, ALU.mult, None, ["stV"], ["stVm"])
        tt("dve", stat[:, 28:32], stat[:, 24:28], stat[:, 24:28], ALU.mult, ["stVm"], ["stVq"])
        stt("dve", stat[:, 28:32], stat[:, 20:24], 1.0 / 512, stat[:, 28:32], ALU.mult, ALU.subtract, ["stV", "stVq"], ["stVq"])
        act(stat[:, 28:32], stat[:, 28:32], AF.Ln, ["stVq", "eps"], ["stVq"], scale=1.0, bias=eps_col[:, 0:1])
        act(stat[:, 28:32], stat[:, 28:32], AF.Exp, ["stVq"], ["stVq"], scale=-0.5)
        for t in range(4):
            deferred.append(lambda t=t: ts("dve", gv[:, t, :], gv[:, t, :], stat[:, 24 + t:25 + t], stat[:, 28 + t:29 + t], ALU.subtract, ALU.mult,
                                           ["gv.%d" % t, "stVm", "stVq"], ["gv.%d" % t]))
            deferred.append(lambda t=t: tt("dve", gv[:, t, :], gv[:, t, :], gsgu_bc[:, :], ALU.mult, ["gv.%d" % t, "gsgu_bc"], ["gv.%d" % t]))
            deferred.append(lambda t=t: tt("dve", vn[:, t, :], gv[:, t, :], beta_bc[:, :], ALU.add, ["gv.%d" % t, "beta_bc"], ["vn"]))

    def build_V(kt_list, kcol_of):
        for i, kt in enumerate(kt_list):
            b = 2 + i % 2
            k0 = kcol_of(kt)
            kb.group("pe", [mm(psf(b), ckvT[:, k0:k0 + 128], w_ukv_bf[:, :, 64:128], True, True)],
                     reads=["ckvT.%d" % (k0 // 256), "w_ukv"], writes=["ps%d" % b])
            vdst = Vt[:, kt, :]
            vsrc = psf(b)
            if i % 2 == 0:
                cp("dve", vdst, vsrc, ["ps%d" % b], ["Vt.%d" % (kt // 2)])
            else:
                act(vdst, vsrc, AF.Copy, ["ps%d" % b], ["Vt.%d" % (kt // 2)])

    def build_K(h, k0, nk):
        kt_buf = KTh[h % 2]
        kname = "KTh%d" % (h % 2)
        kslots = ["KT.%d" % i for i in (range(4) if h % 2 == 0 else range(4, 8))]
        nkt = nk // 128
        kblocks = sorted(set((k0 + i * 128) // 256 for i in range(nkt)))
        nchunks = (nk + 511) // 512
        for ci in range(nchunks):
            c0 = ci * 512
            n = min(512, nk - c0)
            b = 2 + ci % 2
            kb.group("pe", [mm(psf(b)[:, 0:n], w_ukv_bf[:, h, :], ckvT[:, k0 + c0:k0 + c0 + n], True, True)],
                     reads=["w_ukv"] + ["ckvT.%d" % kbk for kbk in kblocks], writes=["ps%d" % b])
            if ci % 2 == 0:
                cp("dve", kt_buf[0:64, c0:c0 + n], psf(b)[0:64, 0:n], ["ps%d" % b], [kname] + kslots)
            else:
                act(kt_buf[0:64, c0:c0 + n], psf(b)[0:64, 0:n], AF.Copy, ["ps%d" % b], [kname] + kslots)
        cp("pool", kt_buf[64:96, 0:nk], krT[64:96, k0:k0 + nk], ["krT.%d" % kbk for kbk in kblocks], [kname] + kslots)

    def attention(q0, nq, k0, nk, vt0, cat_col0):
        nkt = nk // 128

        def normalise(h):
            pacc, pden = 4 + h % 2, h % 2
            po = (h % 2) * 64
            den = psf(pden)[po:po + 64, 0:nq]
            if nq <= 256 and h % 2 == 1:
                act(rec[po:po + 64, 0:nq], den, AF.Ln, ["ps%d" % pden], ["rec%d" % (h % 2)])
                act(rec[po:po + 64, 0:nq], rec[po:po + 64, 0:nq], AF.Exp, ["rec%d" % (h % 2)], ["rec%d" % (h % 2)], scale=-1.0)
            else:
                kb.op("dve", lambda e: e.reciprocal(out=rec[po:po + 64, 0:nq], in_=den), reads=["ps%d" % pden], writes=["rec%d" % (h % 2)])
            tt("dve", hT[po:po + 64, h // 2, cat_col0:cat_col0 + nq], psf(pacc)[po:po + 64, 0:nq], rec[po:po + 64, 0:nq], ALU.mult,
               ["ps%d" % pacc, "rec%d" % (h % 2)], ["hT.%d" % (h // 2)])

        def lw_of(h, kt):
            hp = h - (h % 2)
            return Vt[:, vt0 + kt, hp * 64:(hp + 2) * 64]

        if nkt * nq <= 512:
            kblk = ["ckvT.%d" % kbk for kbk in sorted(set((k0 + i * 128) // 256 for i in range(nkt)))]
            krblk = ["krT.%d" % kbk for kbk in sorted(set((k0 + i * 128) // 256 for i in range(nkt)))]

            def kslot(h):
                return KTh[h // 4], (h % 4) * nk
            for h_ in range(8):
                kbuf, kc0 = kslot(h_)
                cp("pool", kbuf[64:96, kc0:kc0 + nk], krT[64:96, k0:k0 + nk], krblk, ["KT.%d" % h_])
            for hp in range(4):
                b = 2 + hp % 2
                kb.group("pe", [mm(psf(b)[:, i * nk:(i + 1) * nk], w_ukv_bf[:, 2 * hp + i, :], ckvT[:, k0:k0 + nk], True, True) for i in range(2)],
                         reads=["w_ukv"] + kblk, writes=["ps%d" % b])
                kbuf, kc0 = kslot(2 * hp)
                if hp % 2 == 0:
                    cp("dve", kbuf[0:64, kc0:kc0 + 2 * nk], psf(b)[0:64, 0:2 * nk], ["ps%d" % b], ["KT.%d" % (2 * hp), "KT.%d" % (2 * hp + 1)])
                else:
                    act(kbuf[0:64, kc0:kc0 + 2 * nk], psf(b)[0:64, 0:2 * nk], AF.Copy, ["ps%d" % b], ["KT.%d" % (2 * hp), "KT.%d" % (2 * hp + 1)])

            def s_exp(h):
                b = 6 + h % 2
                kbuf, kc0 = kslot(h)
                kb.group("pe", [mm(psf(b)[:, kt * nq:(kt + 1) * nq], kbuf[0:96, kc0 + kt * 128:kc0 + (kt + 1) * 128], qT[0:96, h, q0:q0 + nq], True, True)
                                for kt in range(nkt)], reads=["KT.%d" % h, "xq"], writes=["ps%d" % b])
                act(PT[h % 3][:, 0:nkt * nq], psf(b)[:, 0:nkt * nq], AF.Exp, ["ps%d" % b], ["PT%d" % (h % 3)], scale=ATTN_SCALE)
            s_exp(0)
            for h in range(8):
                if h + 1 < 8:
                    s_exp(h + 1)
                pacc, pden = 4 + h % 2, h % 2
                pt = PT[h % 3]
                fns = []
                for kt in range(nkt):
                    fns.append(mm(psf(pacc)[:, 0:nq], lw_of(h, kt), pt[:, kt * nq:(kt + 1) * nq], kt == 0, kt == nkt - 1))
                    fns.append(mm(psf(pden)[:, 0:nq], ones_bf[:, :], pt[:, kt * nq:(kt + 1) * nq], kt == 0, kt == nkt - 1))
                kb.group("pe", fns, reads=["PT%d" % (h % 3), "ones_bf"] + ["Vt.%d" % ((vt0 + kt) // 2) for kt in range(nkt)],
                         writes=["ps%d" % pacc, "ps%d" % pden])
                normalise(h)
                drain(1)
            return
        def vaug(hh):
            return x1[:, 4 + hh % 2, :].bitcast(BF16)[:, 0:nkt * 128].rearrange("p (t n) -> p t n", t=nkt)

        def build_vaug(hh):
            act(vaug(hh)[:, :, 0:64], Vt[:, vt0:vt0 + nkt, hh * 64:(hh + 1) * 64], AF.Copy,
                ["Vt.%d" % i for i in sorted(set((vt0 + k_) // 2 for k_ in range(nkt)))], ["x1.%d" % (4 + hh % 2)])
        for hh in range(2):
            kb.op("pool", lambda e: e.memset(vaug(hh)[:, :, 64:128], 1.0), writes=["x1.%d" % (4 + hh)])
        build_K(0, k0, nk)
        build_K(1, k0, nk)
        build_vaug(0)
        build_vaug(1)
        for h in range(8):
            kt_buf = KTh[h % 2]
            kname = "KTh%d" % (h % 2)
            pacc = 4 + h % 2
            vname = "x1.%d" % (4 + h % 2)

            def s_mm(kt):
                b = 6 + kt % 2
                kb.group("pe", [mm(psf(b)[:, 0:nq], kt_buf[0:96, kt * 128:(kt + 1) * 128], qT[0:96, h, q0:q0 + nq], True, True)],
                         reads=[kname, "xq"] + ["KT.%d" % i for i in (range(4) if h % 2 == 0 else range(4, 8))], writes=["ps%d" % b])
            s_mm(0)
            for kt in range(nkt):
                if kt + 1 < nkt:
                    s_mm(kt + 1)
                b = 6 + kt % 2
                pt = PT[kt % 3]
                ptn = "PT%d" % (kt % 3)
                act(pt[:, 0:nq], psf(b)[:, 0:nq], AF.Exp, ["ps%d" % b], [ptn], scale=ATTN_SCALE)
                kb.group("pe", [mm(psf(pacc)[:, 0:nq], vaug(h)[:, kt, :], pt[:, 0:nq], kt == 0, kt == nkt - 1)],
                         reads=[ptn, vname], writes=(["ps%d" % pacc] if kt in (0, nkt - 1) else []))
            if h + 2 < 8:
                build_K(h + 2, k0, nk)
                build_vaug(h + 2)
            po = (h % 2) * 64
            if h % 2 == 1:
                act(rec[0:64, 0:nq], psf(pacc)[64:128, 0:nq], AF.Ln, ["ps%d" % pacc], ["rec0"])
                act(rec[0:64, 0:nq], rec[0:64, 0:nq], AF.Exp, ["rec0"], ["rec0"], scale=-1.0)
            else:
                kb.op("dve", lambda e: e.reciprocal(out=rec[0:64, 0:nq], in_=psf(pacc)[64:128, 0:nq]), reads=["ps%d" % pacc], writes=["rec0"])
            tt("dve", hT[po:po + 64, h // 2, cat_col0:cat_col0 + nq], psf(pacc)[0:64, 0:nq], rec[0:64, 0:nq], ALU.mult,
               ["ps%d" % pacc, "rec0"], ["hT.%d" % (h // 2)])
            drain(2)
            if head_hooks:
                head_hooks.pop(0)()

    def sgu_and_out(slots, c, ydst):
        drain(100)
        for k in range(4):
            pa, pb_ = (4, 5) if k % 2 == 0 else (6, 7)
            fns = []
            for j in range(4):
                cs = slice(j * 128, (j + 1) * 128)
                fns.append(mm(psf(pa)[:, cs], vn[:, j, k * 128:(k + 1) * 128], wsguT[:, 2 * k, :], True, False))
                fns.append(mm(psf(pa)[:, cs], cst_bf[:, :], bTb[:, 2 * k, :], False, True))
                fns.append(mm(psf(pb_)[:, cs], vn[:, j, k * 128:(k + 1) * 128], wsguT[:, 2 * k + 1, :], True, False))
                fns.append(mm(psf(pb_)[:, cs], cst_bf[:, :], bTb[:, 2 * k + 1, :], False, True))
            kb.group("pe", fns, reads=["vn", "wsguT", "cst_bf", "bTb"], writes=["ps%d" % pa, "ps%d" % pb_])
            tt("dve", hT[0:64, 4 + k, :], psf(pa)[0:64, :], uT[0:64, k, :], ALU.mult, ["ps%d" % pa, "uT"], ["hT.%d" % (4 + k)])
            tt("dve", hT[64:128, 4 + k, :], psf(pb_)[64:128, :], uT[64:128, k, :], ALU.mult, ["ps%d" % pb_, "uT"], ["hT.%d" % (4 + k)])
        creads = ["hT.%d" % k for k in range(8)]
        for t in range(4):
            o = 32 + 4 * (t % 2)
            sm, smr = "stM%d" % (t % 2), "stMr%d" % (t % 2)
            for half in range(2):
                b = 2 + 2 * (t % 2) + half
                kb.group("pe", [mm(psf(b), hT[:, kc, t * 128:(t + 1) * 128], w_o_bf[:, kc, half * 512:(half + 1) * 512], kc == 0, kc == 7) for kc in range(8)],
                         reads=creads + ["w_o"], writes=["ps%d" % b])
                act(junk[:, 0:512], psf(b), AF.Square, ["ps%d" % b], ["junk", sm], accum_out=stat[:, o + half:o + half + 1])
            tt("dve", stat[:, o + 2:o + 3], stat[:, o:o + 1], stat[:, o + 1:o + 2], ALU.add, [sm], [smr])
            rstd_from_ss(stat[:, o + 3:o + 4], stat[:, o + 2:o + 3], D, [smr], [smr])
            s = slots[t]
            for half in range(2):
                b = 2 + 2 * (t % 2) + half
                sl = slice(half * 512, (half + 1) * 512)
                sc = 1 + half
                stt("dve", gv[:, sc, :], psf(b), stat[:, o + 3:o + 4], G_a[:, c, sl], ALU.mult, ALU.mult,
                    ["ps%d" % b, smr, "G_a%d" % c], ["gv.%d" % sc])
                tt("dve" if half == 0 else "pool", x1[:, s, sl], x1[:, s, sl], gv[:, sc, :], ALU.add, ["gv.%d" % sc, "x1.%d" % s], ["x1.%d" % s])

    kb.dma("pool", cc_ld[:], cckv.rearrange("(t p) d -> p t d", p=128), "cc_ld", True)
    kb.op("pool", lambda e: e.memset(ck_ld[:], 0.0), writes=["ck_ld"])
    kb.dma("pool", ck_ld[:, :, 64:96], ckr.rearrange("(t p) d -> p t d", p=128), "ck_ld", True)
    kb.group("pe", [tr(psh(0)[:, t * 128:(t + 1) * 128], cc_ld[:, t, :], ident_bf[:, :]) for t in range(2)], reads=["cc_ld", "ident_bf"], writes=["ps0"])
    cp("dve", ckvT[:, 0:256], psh(0)[:, 0:256], ["ps0"], ["ckvT.0"])
    kb.group("pe", [tr(psh(1)[0:96, t * 128:(t + 1) * 128], ck_ld[:, t, :], ident_bf[:, :]) for t in range(2)], reads=["ck_ld", "ident_bf"], writes=["ps1"])
    cp("dve", krT[64:96, 0:256], psh(1)[64:96, 0:256], ["ps1"], ["krT.0"])
    late_dma(4)
    front_A(xs[512:1024, :], [4, 5, 6, 7])
    front_B([4, 5, 6, 7], 1, False, 768, 512, None)
    late_step()
    front_A(xs[0:512, :], [0, 1, 2, 3])
    stage_hooks.append(late_step)
    stage_hooks.append(late_step)
    front_B([0, 1, 2, 3], 1, True, 256, 0, None)
    del stage_hooks[:]
    kb.dma("pool", w_o_bf[:], w_o.rearrange("(kc p) n -> p kc n", p=128), "w_o", True)
    build_V(list(range(10)), lambda kt: kt * 128)
    for i_ in range(8):
        head_hooks.append(late_step if i_ in (2, 5) else (lambda: None))
    attention(0, 512, 0, 1280, 0, 0)
    del head_hooks[:]
    front_A(xp[0:512, :], [4, 5, 6, 7])
    sgu_and_out([0, 1, 2, 3], 1, None)
    late_step()
    for g in range(2):
        slots = [4 + 4 * g + t for t in range(4)]
        front_B(slots, 0, True, 0, None, g * 512)
        if g == 0:
            late_step()
        build_V([0, 1, 2, 3], lambda kt: kt * 128)
        for j in range(2):
            attention(j * 256, 256, j * 256, 256, 2 * j, j * 256)
            if g == 0 and j == 0:
                late_step()
        if g == 0:
            while late_next[0] < 12:
                late_step()
            front_A(xp[512:1024, :], [8, 9, 10, 11])
        sgu_and_out(slots, 0, None)

    kb.barrier()
    W1.close()
    P1.close()

    P2 = contextlib.ExitStack()
    G_f = kb.sb(P2, "G_f", [128, 2, D], F32)
    w1_bf = kb.sb(P2, "w1_bf", [128, 8, 4 * D], BF16)
    w2_bf = kb.sb(P2, "w2_bf", [128, 32, D], BF16)
    for c in range(2):
        kb.dma("sp", G_f[:, c, :], g_ffn_post.partition_broadcast(128), "G_f%d" % c, True)
    def wgroup_dma(g):
        kb.dma("pool", w1_bf[:, :, g * 512:(g + 1) * 512], w_ff1[:, g * 512:(g + 1) * 512].rearrange("(kc p) n -> p kc n", p=128), "w1.%d" % g, True)
        kb.dma("pool", w2_bf[:, g * 4:(g + 1) * 4, :], w_ff2[g * 512:(g + 1) * 512, :].rearrange("(kc p) n -> p kc n", p=128), "w2.%d" % g, True)

    for g in range(8):
        wgroup_dma(g)
    T2 = contextlib.ExitStack()
    diagT = kb.sb(T2, "diagT", [128, 4, 128], F32)
    ones_f = kb.sb(T2, "ones_f", [128, 128], F32)
    kb.op("dve", lambda e: e.memset(ones_f[:], 1.0), writes=["ones_f"])
    for c in range(2):
        for half in range(2):
            pb = 4 + half
            for kq in range(4):
                ts("dve", diagT[:, kq, :], ident_f[:, :], gcolF[:, c, half * 4 + kq:half * 4 + kq + 1], None, ALU.mult, None,
                   ["ident_f", "gcolF"], ["diagT"])
            kb.group("pe", [mm(psf(pb)[:, kq * 128:(kq + 1) * 128], ones_f[:, :], diagT[:, kq, :], True, True) for kq in range(4)],
                     reads=["ones_f", "diagT"], writes=["ps%d" % pb])
            sl = slice(half * 512, (half + 1) * 512)
            tt("dve", G_f[:, c, sl], G_f[:, c, sl], psf(pb), ALU.mult, ["G_f%d" % c, "ps%d" % pb], ["G_f%d" % c])
    kb.barrier(exclude=("w1.", "w2."))
    T2.close()

    W2 = contextlib.ExitStack()
    h2T = [kb.sb(W2, "h2T%d" % i, [128, 8, 256], BF16) for i in range(2)]
    xn2 = kb.sb(W2, "xn2", [128, 2, D], BF16)
    NH = 6
    hid = [kb.sb(W2, "hid%d" % i, [128, 256], BF16) for i in range(NH)]
    rl = [kb.sb(W2, "rl%d" % i, [128, 256], F32) for i in range(2)]
    junk2 = kb.sb(W2, "junk2", [128, 512], BF16)
    xn2f = xn2[:, :, :].rearrange("p t d -> p (t d)").bitcast(F32)
    ytmp = [xn2f[:, 0:512], xn2f[:, 512:1024]]

    def prep_stats(sg):
        tiles = [2 * sg, 2 * sg + 1]
        for i, s_ in enumerate(tiles):
            act(xn2[:, i, :], x1[:, s_, :], AF.Square, ["x1.%d" % s_], ["xn2", "yt0", "yt1", "stH"], accum_out=stat[:, 40 + i:41 + i])
        rstd_from_ss(stat[:, 42:44], stat[:, 40:42], D, ["stH"], ["stHr"])
        for i, s_ in enumerate(tiles):
            ts("dve", xn2[:, i, :], x1[:, s_, :], stat[:, 42 + i:43 + i], None, ALU.mult, None, ["x1.%d" % s_, "stHr", "xn2"], ["xn2"])

    def prep_tr(sg, kc):
        c_ = 1 if sg < 2 else 0
        b = kc % 2
        hb = h2T[sg % 2]
        kb.group("pe", [tr(psh(b)[:, i * 128:(i + 1) * 128], xn2[:, i, kc * 128:(kc + 1) * 128], ident_bf[:, :]) for i in range(2)],
                 reads=["xn2", "ident_bf"], writes=["ps%d" % b])
        act(hb[:, kc, :], psh(b)[:, 0:256], AF.Identity, ["ps%d" % b, "modA", "modB"], ["h2T%d.%d" % (sg % 2, kc)],
            scale=modA[:, 1, c_, kc:kc + 1], bias=modB[:, 1, c_, kc:kc + 1])

    prep_stats(0)
    for kc in range(8):
        prep_tr(0, kc)

    LA = NH - 1
    NFC = 32

    def ff1(sg, fc):
        hb_ = h2T[sg % 2]
        b = 2 + fc % 2
        r = fc % 2
        r4 = (sg * NFC + fc) % NH
        kb.group("pe", [mm(psf(b)[:, 0:256], w1_bf[:, kc, fc * 128:(fc + 1) * 128], hb_[:, kc, :], kc == 0, kc == 7) for kc in range(8)],
                 reads=["h2T%d.%d" % (sg % 2, kc) for kc in range(8)] + ["w1.%d" % (fc // 4)], writes=["ps%d" % b])
        act(rl[r][:, :], psf(b)[:, 0:256], AF.Relu, ["ps%d" % b], ["rl%d" % r])
        tt("dve", hid[r4][:, :], rl[r][:, :], rl[r][:, :], ALU.mult, ["rl%d" % r], ["hid%d" % r4])

    def ff2(sg, fc):
        r = (sg * NFC + fc) % NH
        for i in range(2):
            fns = [mm(psf(4 + 2 * i + half), hid[r][:, i * 128:(i + 1) * 128], w2_bf[:, fc, half * 512:(half + 1) * 512], fc == 0, fc == NFC - 1)
                   for half in range(2)]
            kb.group("pe", fns, reads=["hid%d" % r, "w2.%d" % (fc // 4)],
                     writes=(["ps%d" % (4 + 2 * i), "ps%d" % (5 + 2 * i)] if fc in (0, NFC - 1) else []))

    def tail(sg):
        tiles = [2 * sg, 2 * sg + 1]
        c = 1 if sg < 2 else 0
        hdead = h2T[sg % 2][:, :, :].rearrange("p k n -> p (k n)").bitcast(F32)
        hnames = ["h2T%d.%d" % (sg % 2, kc) for kc in range(8)]
        tmps = [(ytmp[0], ["xn2", "yt0"], ["yt0"]), (ytmp[1], ["xn2", "yt1"], ["yt1"]),
                (hdead[:, 0:512], hnames, hnames), (hdead[:, 512:1024], hnames, hnames)]
        for i in range(2):
            for half in range(2):
                b = 4 + 2 * i + half
                sl = slice(half * 512, (half + 1) * 512)
                tb, wn, rn = tmps[2 * i + half]
                act(junk2[:, :], psf(b), AF.Square, ["ps%d" % b], ["junk2", "stJ", "pslock%d" % b], accum_out=stat[:, 44 + 2 * i + half:45 + 2 * i + half])
                tt("dve", tb, psf(b), G_f[:, c, sl], ALU.mult, ["ps%d" % b, "pslock%d" % b, "G_f%d" % c], wn)
        kb.op("dve", lambda e: e.tensor_reduce(out=stat[:, 48:50], in_=stat[:, 44:48].rearrange("p (i h) -> p i h", h=2), axis=AX.X, op=ALU.add),
              reads=["stJ"], writes=["stJr"])
        rstd_from_ss(stat[:, 50:52], stat[:, 48:50], D, ["stJr"], ["stJr"])
        for i, s_ in enumerate(tiles):
            for half in range(2):
                sl = slice(half * 512, (half + 1) * 512)
                tb, wn, rn = tmps[2 * i + half]
                stt("dve", x1[:, s_, sl], tb, stat[:, 50 + i:51 + i], x1[:, s_, sl], ALU.mult, ALU.add,
                    rn + ["stJr", "x1.%d" % s_], ["x1.%d" % s_])
            if s_ < 4:
                kb.dma("sp", ys[s_ * 128:(s_ + 1) * 128, :], x1[:, s_, :], "x1.%d" % s_, False)
            else:
                kb.dma("sp", yp[(s_ - 4) * 128:(s_ - 3) * 128, :], x1[:, s_, :], "x1.%d" % s_, False)

    NS = 6 * NFC
    for n in range(NS + LA):
        if n < NS:
            sg, fc = divmod(n, NFC)
            ff1(sg, fc)
            if sg + 1 < 6:
                if fc == 10:
                    prep_stats(sg + 1)
                if 14 <= fc < 22:
                    prep_tr(sg + 1, fc - 14)
        m = n - LA
        if m >= 0:
            sg2, fc2 = divmod(m, NFC)
            ff2(sg2, fc2)
            if fc2 == NFC - 1:
                tail(sg2)

    kb.finish("sp")
    W2.close()
    P2.close()
    kb.close()
    return nc, dbg_outs


_CACHE = {}


def _get_program():
    if "nc" not in _CACHE:
        _CACHE["nc"] = build_program(DEBUG)
    return _CACHE["nc"]


def make_in_maps(inputs):
    f = lambda a: np.ascontiguousarray(np.asarray(a, dtype=np.float32))
    x_prompt = f(inputs["x_prompt"])
    x_sample = f(inputs["x_sample"])
    cache_ckv = f(inputs["cache_ckv"])
    cache_krope = f(inputs["cache_krope"])
    c = f(inputs["c"])
    c_ctx = f(inputs["c_ctx"])
    shared = {
        "w_mod": f(inputs["w_mod"])[0], "b_mod": f(inputs["b_mod"])[0],
        "g_attn_pre": f(inputs["g_attn_pre"])[0], "g_attn_post": f(inputs["g_attn_post"])[0],
        "w_in": f(inputs["w_in"])[0], "g_q": f(inputs["g_q"])[0], "w_uq": f(inputs["w_uq"])[0],
        "g_kv": f(inputs["g_kv"])[0], "w_ukv": f(inputs["w_ukv"])[0], "w_sgu": f(inputs["w_sgu"])[0],
        "b_sgu": f(inputs["b_sgu"])[0], "g_sgu": f(inputs["g_sgu"])[0], "beta_sgu": f(inputs["beta_sgu"])[0],
        "w_o": f(inputs["w_o"])[0], "g_ffn_pre": f(inputs["g_ffn_pre"])[0], "g_ffn_post": f(inputs["g_ffn_post"])[0],
        "w_ff1": f(inputs["w_ff1"])[0], "w_ff2": f(inputs["w_ff2"])[0],
    }
    in_maps = []
    for i in range(8):
        b, hf = i // 2, i % 2
        own = x_sample[b, hf * 512:(hf + 1) * 512]
        oth = x_sample[b, (1 - hf) * 512:(2 - hf) * 512]
        m = dict(shared)
        m["xp"] = np.ascontiguousarray(x_prompt[4 * i:4 * i + 4].reshape(1024, D))
        m["xs"] = np.ascontiguousarray(np.concatenate([own, oth], axis=0))
        m["cckv"] = np.ascontiguousarray(cache_ckv[b, 0])
        m["ckr"] = np.ascontiguousarray(cache_krope[b, 0])
        m["cond"] = np.ascontiguousarray(np.stack([c_ctx, c[b]], axis=0))
        m["meta"] = np.array([8.0 * hf, 8.0 * (1 - hf)], dtype=np.float32)
        in_maps.append(m)
    return in_maps


def kernel(**inputs):
    nc, _ = _get_program()
    in_maps = make_in_maps(inputs)
    res = run_bass_kernel_spmd(nc, in_maps, core_ids=list(range(8)))
    y_prompt = np.zeros((32, 256, D), np.float32)
    y_sample = np.zeros((4, 1024, D), np.float32)
    new_ckv = np.zeros((32, 1, 256, 128), np.float32)
    new_kr = np.zeros((32, 1, 256, 32), np.float32)
    for i in range(8):
        r = res.results[i]
        b, hf = i // 2, i % 2
        y_prompt[4 * i:4 * i + 4] = np.asarray(r["yp"]).reshape(4, 256, D)
        y_sample[b, hf * 512:(hf + 1) * 512] = np.asarray(r["ys"])
        new_ckv[4 * i:4 * i + 4, 0] = np.asarray(r["nckv"]).reshape(4, 256, 128)
        new_kr[4 * i:4 * i + 4, 0] = np.asarray(r["nkr"]).reshape(4, 256, 32)
    return (y_prompt, y_sample, new_ckv, new_kr)
```

```python
import contextlib
import math
import numpy as np
import concourse.bass as bass
import concourse.mybir as mybir
from concourse.bass_utils import run_bass_kernel_spmd

F32 = mybir.dt.float32
BF16 = mybir.dt.bfloat16
I32 = mybir.dt.int32
AF = mybir.ActivationFunctionType
ALU = mybir.AluOpType
AX = mybir.AxisListType

D = 1024
NT_P = 8
EPS = 1e-6
ATTN_SCALE = 96.0 ** -0.5
DEBUG = False


class KB:
    def __init__(self, nc):
        self.nc = nc
        self.stack = contextlib.ExitStack()
        self.eng = {"pe": nc.tensor, "act": nc.scalar, "dve": nc.vector, "pool": nc.gpsimd, "sp": nc.sync}
        self.esem = {}
        self.ecount = {}
        for n in ("pe", "act", "dve", "pool"):
            self.esem[n] = self.stack.enter_context(nc.semaphore("s_" + n))
            self.ecount[n] = 0
        self.waited = {n: {} for n in self.eng}
        self.res = {}
        self.dsem = {}
        self.semval = {}
        self.nwaits = 0
        self.nops = 0

    def sb(self, stack, name, shape, dt):
        return stack.enter_context(self.nc.sbuf_tensor(name, list(shape), dt))

    def ps(self, stack, name, shape, dt):
        return stack.enter_context(self.nc.psum_tensor(name, list(shape), dt))

    def _r(self, name):
        if name not in self.res:
            self.res[name] = [None, []]
        return self.res[name]

    def _wait(self, engname, ev):
        if ev is None:
            return
        sem, val = ev
        w = self.waited[engname]
        if w.get(id(sem), 0) >= val:
            return
        w[id(sem)] = val
        self.eng[engname].wait_ge(sem, val)
        self.nwaits += 1

    def _deps(self, engname, reads, writes):
        for r in reads:
            st = self._r(r)
            self._wait(engname, st[0])
            if r.startswith("ps"):
                mine = id(self.esem.get(engname))
                for ev in st[1]:
                    if id(ev[0]) != mine:
                        self._wait(engname, ev)
        for w in writes:
            st = self._r(w)
            self._wait(engname, st[0])
            for ev in st[1]:
                self._wait(engname, ev)

    def _record(self, ev, reads, writes):
        for r in reads:
            lst = self._r(r)[1]
            lst.append(ev)
            if len(lst) > 64:
                best = {}
                for e in lst:
                    if id(e[0]) not in best or best[id(e[0])][1] < e[1]:
                        best[id(e[0])] = e
                lst[:] = list(best.values())
        for w in writes:
            st = self._r(w)
            st[0] = ev
            st[1] = []
        self.semval[id(ev[0])] = ev

    def op(self, engname, fn, reads=(), writes=()):
        self._deps(engname, reads, writes)
        ins = fn(self.eng[engname])
        self.ecount[engname] += 1
        ev = (self.esem[engname], self.ecount[engname])
        ins.then_inc(ev[0], 1)
        self._record(ev, reads, writes)
        self.nops += 1
        return ev

    def group(self, engname, fns, reads=(), writes=()):
        self._deps(engname, reads, writes)
        ins = None
        for fn in fns:
            ins = fn(self.eng[engname])
            self.nops += 1
        self.ecount[engname] += 1
        ev = (self.esem[engname], self.ecount[engname])
        ins.then_inc(ev[0], 1)
        self._record(ev, reads, writes)
        return ev

    def dma(self, q, out, in_, res, write, semkey=None, **kw):
        reads = [] if write else [res]
        writes = [res] if write else []
        self._deps(q, reads, writes)
        sk = semkey or res
        if sk not in self.dsem:
            self.dsem[sk] = [self.stack.enter_context(self.nc.semaphore("d_" + sk.replace(".", "_"))), 0]
        d = self.dsem[sk]
        self.eng[q].dma_start(out=out, in_=in_, **kw).then_inc(d[0], 16)
        d[1] += 16
        ev = (d[0], d[1])
        self._record(ev, reads, writes)
        return ev

    def barrier(self, exclude=()):
        skip = set()
        for name, d in self.dsem.items():
            if any(name.startswith(p) for p in exclude):
                skip.add(id(d[0]))
        for e in self.eng:
            for sid, ev in list(self.semval.items()):
                if sid in skip:
                    continue
                self._wait(e, ev)

    def finish(self, engname="sp"):
        for sid, ev in list(self.semval.items()):
            self._wait(engname, ev)

    def close(self):
        self.stack.close()


def build_program(debug=False):
    nc = bass.Bass("TRN2", target_bir_lowering=False)

    def din(name, shape):
        return nc.dram_tensor(name, list(shape), F32, kind="ExternalInput").ap()

    def dout(name, shape):
        return nc.dram_tensor(name, list(shape), F32, kind="ExternalOutput").ap()

    xp = din("xp", [1024, D])
    xs = din("xs", [1024, D])
    cckv = din("cckv", [256, 128])
    ckr = din("ckr", [256, 32])
    cond = din("cond", [2, D])
    meta = din("meta", [2])
    w_mod = din("w_mod", [D, 6 * D])
    b_mod = din("b_mod", [6 * D])
    g_attn_pre = din("g_attn_pre", [D])
    g_attn_post = din("g_attn_post", [D])
    w_in = din("w_in", [D, 1440])
    g_q = din("g_q", [256])
    w_uq = din("w_uq", [256, 768])
    g_kv = din("g_kv", [128])
    w_ukv = din("w_ukv", [128, 1024])
    w_sgu = din("w_sgu", [8, 128, 128])
    b_sgu = din("b_sgu", [8, 128])
    g_sgu = din("g_sgu", [512])
    beta_sgu = din("beta_sgu", [512])
    w_o = din("w_o", [D, D])
    g_ffn_pre = din("g_ffn_pre", [D])
    g_ffn_post = din("g_ffn_post", [D])
    w_ff1 = din("w_ff1", [D, 4 * D])
    w_ff2 = din("w_ff2", [4 * D, D])

    yp = dout("yp", [1024, D])
    ys = dout("ys", [512, D])
    nckv = dout("nckv", [1024, 128])
    nkr = dout("nkr", [1024, 32])

    kb = KB(nc)
    dbg_outs = {}

    def dbg(name, ap, res, shape, dt=F32, stk=None):
        if not debug:
            return
        o = nc.dram_tensor("dbg_" + name, list(shape), dt, kind="ExternalOutput").ap()
        dbg_outs[name] = o
        kb.dma("sp", o, ap, res, False)

    def act(out, in_, func, reads, writes, **kw):
        return kb.op("act", lambda e: e.activation(out=out, in_=in_, func=func, **kw), reads, writes)

    def tt(eng, out, in0, in1, op, reads, writes):
        return kb.op(eng, lambda e: e.tensor_tensor(out=out, in0=in0, in1=in1, op=op), reads, writes)

    def ts(eng, out, in0, s1, s2, op0, op1, reads, writes):
        if op1 is None:
            return kb.op(eng, lambda e: e.tensor_scalar(out=out, in0=in0, scalar1=s1, scalar2=None, op0=op0), reads, writes)
        return kb.op(eng, lambda e: e.tensor_scalar(out=out, in0=in0, scalar1=s1, scalar2=s2, op0=op0, op1=op1), reads, writes)

    def stt(eng, out, in0, scalar, in1, op0, op1, reads, writes):
        return kb.op(eng, lambda e: e.scalar_tensor_tensor(out=out, in0=in0, scalar=scalar, in1=in1, op0=op0, op1=op1), reads, writes)

    def cp(eng, out, in_, reads, writes):
        return kb.op(eng, lambda e: e.tensor_copy(out, in_), reads, writes)

    def mm(out, lhsT, rhs, start, stop):
        return lambda e: e.matmul(out, lhsT, rhs, start=start, stop=stop)

    def tr(out, in_, ident):
        return lambda e: e.transpose(out, in_, ident)

    def rstd_from_ss(dst, ss, n, reads, writes):
        act(dst, ss, AF.Ln, list(reads) + ["eps"], writes, scale=1.0 / n, bias=eps_col[:, 0:1])
        act(dst, dst, AF.Exp, writes, writes, scale=-0.5)

    PS = kb.stack
    psb = [kb.ps(PS, "psb%d" % i, [128, 512], F32) for i in range(8)]

    def psf(i):
        return psb[i][:, :]

    def psh(i):
        return psb[i][:, :].bitcast(BF16)

    x1 = kb.sb(PS, "x1", [128, 12, D], F32)
    ident_bf = kb.sb(PS, "ident_bf", [128, 128], BF16)
    ident_f = kb.sb(PS, "ident_f", [128, 128], F32)
    ones_bf = kb.sb(PS, "ones_bf", [128, 128], BF16)
    eps_col = kb.sb(PS, "eps_col", [128, 1], F32)
    srep = kb.sb(PS, "srep", [128, 16, 128], BF16)
    cols = kb.sb(PS, "cols", [128, 18], F32)
    modA = kb.sb(PS, "modA", [128, 2, 2, 8], F32)
    modB = kb.sb(PS, "modB", [128, 2, 2, 8], F32)
    stat = kb.sb(PS, "stat", [128, 96], F32)
    gcolF = kb.sb(PS, "gcolF", [128, 2, 8], F32)
    coltL = kb.sb(PS, "coltL", [128, 4], F32)

    kb.op("pool", lambda e: e.memset(ident_bf[:], 0.0), writes=["ident_bf"])
    kb.op("pool", lambda e: e.affine_select(out=ident_bf[:], in_=ident_bf[:], compare_op=ALU.not_equal, fill=1.0,
                                            base=0, pattern=[[-1, 128]], channel_multiplier=1), reads=["ident_bf"], writes=["ident_bf"])
    kb.op("pool", lambda e: e.memset(ident_f[:], 0.0), writes=["ident_f"])
    kb.op("pool", lambda e: e.affine_select(out=ident_f[:], in_=ident_f[:], compare_op=ALU.not_equal, fill=1.0,
                                            base=0, pattern=[[-1, 128]], channel_multiplier=1), reads=["ident_f"], writes=["ident_f"])
    kb.op("pool", lambda e: e.memset(ones_bf[:], 1.0), writes=["ones_bf"])

    kb.op("pool", lambda e: e.memset(eps_col[:], EPS), writes=["eps"])

    P1 = contextlib.ExitStack()
    w_in_bf = kb.sb(P1, "w_in_bf", [128, 8, 1440], BF16)
    wkr_pad = kb.sb(P1, "wkr_pad", [128, 8, 96], BF16)
    wkr_swp = kb.sb(P1, "wkr_swp", [128, 8, 96], BF16)
    w_uq_bf = kb.sb(P1, "w_uq_bf", [128, 2, 8, 96], BF16)
    w_uq_swp = kb.sb(P1, "w_uq_swp", [128, 2, 8, 96], BF16)
    w_ukv_bf = kb.sb(P1, "w_ukv_bf", [128, 8, 128], BF16)
    wsguT = kb.sb(P1, "wsguT", [128, 8, 128], BF16)
    w_o_bf = kb.sb(P1, "w_o_bf", [128, 8, D], BF16)
    G_a = kb.sb(P1, "G_a", [128, 2, D], F32)
    gkv_bc = kb.sb(P1, "gkv_bc", [128, 128], F32)
    gsgu_bc = kb.sb(P1, "gsgu_bc", [128, 512], F32)
    beta_bc = kb.sb(P1, "beta_bc", [128, 512], F32)
    bT_bc = kb.sb(P1, "bT_bc", [128, 8, 128], F32)
    bTb = kb.sb(P1, "bTb", [128, 8, 128], BF16)
    cst_bf = kb.sb(P1, "cst_bf", [128, 128], BF16)
    CT = kb.sb(P1, "CT", [128, 1024], F32)
    ST = kb.sb(P1, "ST", [128, 1024], F32)

    kb.op("pool", lambda e: e.memset(cst_bf[:], 1.0 / 128.0), writes=["cst_bf"])

    T0 = contextlib.ExitStack()
    rows = kb.sb(T0, "rows", [32, 128], F32)
    rows_s = kb.sb(T0, "rows_s", [32, 128], F32)
    rows_b = kb.sb(T0, "rows_b", [32, 128], BF16)
    sT = kb.sb(T0, "sT", [128, 16], BF16)
    wsg_ld = kb.sb(T0, "wsg_ld", [128, 8, 128], BF16)
    mt = kb.sb(T0, "mt", [128, 2], F32)
    pidx = kb.sb(T0, "pidx", [128, 1], I32)
    ktmp = kb.sb(T0, "ktmp", [128, 1], I32)
    kf = kb.sb(T0, "kf", [128, 1], F32)
    freq = kb.sb(T0, "freq", [128, 1], F32)
    isrow = kb.sb(T0, "isrow", [128, 1], F32)
    rc = kb.sb(T0, "rc", [128, 80], F32)
    tm80 = kb.sb(T0, "tm80", [128, 80], F32)
    m80 = kb.sb(T0, "m80", [128, 80], F32)
    ti80 = kb.sb(T0, "ti80", [128, 80], I32)
    sn80 = kb.sb(T0, "sn80", [128, 80], F32)
    cs80 = kb.sb(T0, "cs80", [128, 80], F32)
    notrow = kb.sb(T0, "notrow", [128, 1], F32)
    wm = [kb.sb(T0, "wm%d" % i, [128, 8, 512], BF16) for i in range(2)]
    bm = [kb.sb(T0, "bm%d" % i, [128, 512], F32) for i in range(2)]
    mrow = [kb.sb(T0, "mrow%d" % i, [128, 512], F32) for i in range(2)]
    scr = kb.sb(T0, "scr", [128, 4, 128], F32)
    colt = kb.sb(T0, "colt", [128, 4], F32)

    def mod_dma(j):
        b = j % 2
        kb.dma("pool", wm[b][:], w_mod[:, j * 512:(j + 1) * 512].rearrange("(kc p) n -> p kc n", p=128), "wm%d" % b, True)
        kb.dma("sp", bm[b][:], b_mod[j * 512:(j + 1) * 512].partition_broadcast(128), "bm%d" % b, True)

    kb.dma("sp", rows_s[0:16, :], cond.rearrange("c (k p) -> (c k) p", p=128), "rows_s", True)
    kb.dma("sp", rows[0:8, :], g_attn_pre.rearrange("(k p) -> k p", p=128), "rows", True)
    kb.dma("sp", rows[8:16, :], g_ffn_pre.rearrange("(k p) -> k p", p=128), "rows", True)
    kb.dma("sp", rows[16:18, :], g_q.rearrange("(k p) -> k p", p=128), "rows", True)
    mod_dma(0)
    mod_dma(1)
    kb.dma("pool", w_in_bf[:], w_in.rearrange("(kc p) n -> p kc n", p=128), "w_in", True)

    kb.dma("sp", gkv_bc[:], g_kv.partition_broadcast(128), "gkv_bc", True)
    kb.dma("sp", gsgu_bc[:], g_sgu.partition_broadcast(128), "gsgu_bc", True)
    kb.dma("sp", beta_bc[:], beta_sgu.partition_broadcast(128), "beta_bc", True)
    kb.dma("sp", bT_bc[:].rearrange("p g q -> p (g q)"), b_sgu.rearrange("g p -> (g p)").partition_broadcast(128), "bT_bc", True)
    for c in range(2):
        kb.dma("sp", G_a[:, c, :], g_attn_post.partition_broadcast(128), "G_a%d" % c, True)
    kb.group("pe", [mm(psf(0)[:, 0:18], rows[0:18, :], ident_f[0:18, 0:18], True, True)], reads=["rows", "ident_f"], writes=["ps0"])
    cp("dve", cols[:, :], psf(0)[:, 0:18], ["ps0"], ["cols"])
    act(rows_s[0:16, :], rows_s[0:16, :], AF.Silu, ["rows_s"], ["rows_s"])
    cp("dve", rows_b[0:16, :], rows_s[0:16, :], ["rows_s"], ["rows_b"])
    kb.group("pe", [mm(psf(1)[:, 0:16], rows_b[0:16, :], ident_bf[0:16, 0:16], True, True)], reads=["rows_b", "ident_bf"], writes=["ps1"])
    cp("dve", sT[:, :], psf(1)[:, 0:16], ["ps1"], ["sT"])
    for r in range(16):
        cp("dve" if r % 2 == 0 else "pool", srep[:, r, :], sT[:, r:r + 1].to_broadcast([128, 128]), ["sT"], ["srep"])

    def emit_mod(j, Gt, gname, which):
        vec, half = (j // 2) % 3, j % 2
        b = j % 2
        for c in range(2):
            pb = 4 + c
            kb.group("pe", [mm(psf(pb), srep[:, c * 8 + kc, :], wm[b][:, kc, :], kc == 0, kc == 7) for kc in range(8)],
                     reads=["srep", "wm%d" % b], writes=["ps%d" % pb])
            tt("dve", mrow[c][:], psf(pb), bm[b][:], ALU.add, ["ps%d" % pb, "bm%d" % b], ["mrow%d" % c])
            if vec == 2:
                sl = slice(half * 512, (half + 1) * 512)
                tt("dve", Gt[:, c, sl], Gt[:, c, sl], mrow[c][:], ALU.mult, ["%s%d" % (gname, c), "mrow%d" % c], ["%s%d" % (gname, c)])
            else:
                for a in range(4):
                    tt("dve", scr[:, a, :], mrow[c][:, a * 128:(a + 1) * 128], ident_f[:, :], ALU.mult, ["mrow%d" % c, "ident_f"], ["scr"])
                kb.op("dve", lambda e: e.tensor_reduce(out=colt[:, :], in_=scr[:, :, :], axis=AX.X, op=ALU.add), reads=["scr"], writes=["colt"])
                if vec == 0:
                    cp("dve", modB[:, which, c, half * 4:(half + 1) * 4], colt[:, :], ["colt"], ["modB"])
                else:
                    gc = cols[:, which * 8 + half * 4: which * 8 + half * 4 + 4]
                    stt("dve", modA[:, which, c, half * 4:(half + 1) * 4], colt[:, :], 1.0, gc, ALU.add, ALU.mult, ["colt", "cols"], ["modA"])

    def rope_gen():
        kb.dma("sp", mt[:], meta.partition_broadcast(128), "mt", True)
        yield
        kb.op("pool", lambda e: e.iota(pidx[:], pattern=[[0, 1]], base=0, channel_multiplier=1), writes=["pidx"])
        yield
        kb.op("pool", lambda e: e.iota(rc[:, 0:16], pattern=[[1, 16]], base=0, channel_multiplier=0, allow_small_or_imprecise_dtypes=True), writes=["rc"])
        yield
        kb.op("pool", lambda e: e.iota(rc[:, 16:80], pattern=[[1, 64]], base=0, channel_multiplier=0, allow_small_or_imprecise_dtypes=True), writes=["rc"])
        yield
        ts("dve", ktmp[:], pidx[:], 1, 7, ALU.arith_shift_right, ALU.bitwise_and, ["pidx"], ["ktmp"])
        yield
        cp("dve", kf[:], ktmp[:], ["ktmp"], ["kf"])
        yield
        act(freq[:], kf[:], AF.Exp, ["kf"], ["freq"], scale=-math.log(10000.0) / 8.0)
        yield
        ts("dve", ktmp[:], pidx[:], 4, 1, ALU.arith_shift_right, ALU.bitwise_and, ["pidx"], ["ktmp"])
        yield
        cp("dve", kf[:], ktmp[:], ["ktmp", "freq"], ["kf"])
        yield
        ts("dve", isrow[:], kf[:], -1.0, 1.0, ALU.mult, ALU.add, ["kf"], ["isrow"])
        yield
        cp("dve", notrow[:], kf[:], ["kf"], ["notrow"])
        yield
        ts("dve", rc[:, 0:8], rc[:, 0:8], mt[:, 0:1], None, ALU.add, None, ["rc", "mt"], ["rc"])
        yield
        ts("dve", rc[:, 8:16], rc[:, 8:16], mt[:, 1:2], -8.0, ALU.add, ALU.add, ["rc", "mt"], ["rc"])
        yield
        ts("dve", rc[:, :], rc[:, :], freq[:, 0:1], None, ALU.mult, None, ["rc", "freq"], ["rc"])
        yield

        def sin_of(dst, dname, shift):
            ts("dve", tm80[:], rc[:], shift, None, ALU.add, None, ["rc"], ["tm80"])
            yield
            ts("dve", m80[:], tm80[:], 1.0 / (2 * math.pi), None, ALU.mult, None, ["tm80"], ["m80"])
            yield
            cp("dve", ti80[:], m80[:], ["m80"], ["ti80"])
            yield
            cp("dve", m80[:], ti80[:], ["ti80"], ["m80"])
            yield
            stt("dve", tm80[:], m80[:], -2 * math.pi, tm80[:], ALU.mult, ALU.add, ["m80", "tm80"], ["tm80"])
            yield
            ts("dve", m80[:], tm80[:], math.pi, -2 * math.pi, ALU.is_gt, ALU.mult, ["tm80"], ["m80"])
            yield
            tt("dve", tm80[:], tm80[:], m80[:], ALU.add, ["tm80", "m80"], ["tm80"])
            yield
            ts("dve", m80[:], tm80[:], -math.pi, 2 * math.pi, ALU.is_lt, ALU.mult, ["tm80"], ["m80"])
            yield
            tt("dve", tm80[:], tm80[:], m80[:], ALU.add, ["tm80", "m80"], ["tm80"])
            yield
            act(dst[:], tm80[:], AF.Sin, ["tm80"], [dname])
            yield
        yield from sin_of(sn80, "sn80", 0.0)
        yield from sin_of(cs80, "cs80", 0.5 * math.pi)
        for tab, src, sname, tname in ((ST, sn80, "sn80", "ST"), (CT, cs80, "cs80", "CT")):
            tv = tab[:, :].rearrange("p (r c) -> p r c", r=16)
            rb = src[:, 0:16].rearrange("p (r o) -> p r o", o=1).broadcast_to([128, 16, 64])
            cb = src[:, 16:80].rearrange("p (o c) -> p o c", o=1).broadcast_to([128, 16, 64])
            ts("dve", tv, rb, isrow[:, 0:1], None, ALU.mult, None, [sname, "isrow"], [tname])
            yield
            stt("dve", tv, cb, notrow[:, 0:1], tv, ALU.mult, ALU.add, [sname, "notrow", tname], [tname])
            yield

    rope_it = rope_gen()

    def rope_steps(n):
        for _ in range(n):
            try:
                next(rope_it)
            except StopIteration:
                return

    emit_mod(0, G_a, "G_a", 0)
    rope_steps(12)
    mod_dma(2)
    kb.dma("pool", w_uq_bf[:].rearrange("p c h d -> p c (h d)"), w_uq.rearrange("(kc p) n -> p kc n", p=128), "w_uq", True)
    kb.dma("pool", w_ukv_bf[:].rearrange("p h d -> p (h d)"), w_ukv, "w_ukv", True)
    kb.dma("pool", wsg_ld[:], w_sgu.rearrange("g p q -> p g q"), "wsg_ld", True)
    emit_mod(1, G_a, "G_a", 0)
    rope_steps(12)
    mod_dma(3)
    emit_mod(2, G_a, "G_a", 0)
    rope_steps(14)
    emit_mod(3, G_a, "G_a", 0)
    rope_steps(100)

    cp("pool", bTb[:, :, :], bT_bc[:, :, :], ["bT_bc"], ["bTb"])
    kb.op("pool", lambda e: e.memset(wkr_pad[:], 0.0), writes=["wkr_pad"])
    kb.op("pool", lambda e: e.memset(wkr_swp[:], 0.0), writes=["wkr_swp"])
    kb.op("pool", lambda e: e.memset(w_uq_swp[:], 0.0), writes=["w_uq_swp"])
    cp("pool", wkr_pad[:, :, 64:96], w_in_bf[:, :, 384:416], ["w_in"], ["wkr_pad"])
    ts("dve", wkr_swp[:, :, 64:96:2], w_in_bf[:, :, 385:416:2], -1.0, None, ALU.mult, None, ["w_in"], ["wkr_swp"])
    cp("dve", wkr_swp[:, :, 65:96:2], w_in_bf[:, :, 384:416:2], ["w_in"], ["wkr_swp"])
    for c in range(2):
        ts("dve", w_uq_swp[:, c, :, 64:96:2], w_uq_bf[:, c, :, 65:96:2], -1.0, None, ALU.mult, None, ["w_uq"], ["w_uq_swp"])
        cp("dve", w_uq_swp[:, c, :, 65:96:2], w_uq_bf[:, c, :, 64:96:2], ["w_uq"], ["w_uq_swp"])
    for g in range(8):
        b = g % 2
        kb.group("pe", [tr(psh(b)[:, 0:128], wsg_ld[:, g, :], ident_bf[:, :])], reads=["wsg_ld", "ident_bf"], writes=["ps%d" % b])
        cp("dve", wsguT[:, g, :], psh(b)[:, 0:128], ["ps%d" % b], ["wsguT"])


    kb.barrier(exclude=("w_o",))
    T0.close()

    W1 = contextlib.ExitStack()
    hT = kb.sb(W1, "hT", [128, 8, 512], BF16)
    xq = kb.sb(W1, "xq", [128, 4096], BF16)
    uT = kb.sb(W1, "uT", [128, 4, 512], BF16)
    gv = kb.sb(W1, "gv", [128, 4, 512], F32)
    vn = kb.sb(W1, "vn", [128, 4, 512], BF16)
    cqn_bf = kb.sb(W1, "cqn_bf", [128, 4, 256], BF16)
    ckvn_f = kb.sb(W1, "ckvn_f", [128, 4, 128], F32)
    ckvn_bf = kb.sb(W1, "ckvn_bf", [128, 4, 128], BF16)
    kr_f = kb.sb(W1, "kr_f", [128, 4, 32], F32)
    cqnT = kb.sb(W1, "cqnT", [128, 2, 512], BF16)
    ckvT = kb.sb(W1, "ckvT", [128, 1280], BF16)
    krT = kb.sb(W1, "krT", [128, 1280], BF16)
    KTh = [kb.sb(W1, "KTh%d" % i, [128, 1280], BF16) for i in range(2)]
    Vt = kb.sb(W1, "Vt", [128, 10, 512], BF16)
    PT = [kb.sb(W1, "PT%d" % i, [128, 512], BF16) for i in range(3)]
    rec = kb.sb(W1, "rec", [128, 512], F32)
    junk = kb.sb(W1, "junk", [128, 1024], BF16)
    cc_ld = kb.sb(W1, "cc_ld", [128, 2, 128], BF16)
    ck_ld = kb.sb(W1, "ck_ld", [128, 2, 96], BF16)
    xnb = xq[:, :].rearrange("p (t d) -> p t d", t=4)
    qT = xq[:, :].rearrange("p (h n) -> p h n", h=8)

    wmL = x1[:, 8:10, :].rearrange("p t d -> p (t d)").bitcast(BF16).rearrange("p (k n) -> p k n", k=8)
    wmLn = ["x1.8", "x1.9"]
    bmL = x1[:, 10, 0:512]
    mrowL = [x1[:, 10, 512:1024], x1[:, 11, 0:512]]
    scrL = x1[:, 11, 512:1024].rearrange("p (a q) -> p a q", a=4)

    def late_dma(j):
        kb.dma("pool", wmL[:, 0:4, :], w_mod[0:512, j * 512:(j + 1) * 512].rearrange("(kc p) n -> p kc n", p=128), wmLn[0], True, semkey="late.a")
        kb.dma("pool", wmL[:, 4:8, :], w_mod[512:1024, j * 512:(j + 1) * 512].rearrange("(kc p) n -> p kc n", p=128), wmLn[1], True, semkey="late.b")
        kb.dma("pool", bmL, b_mod[j * 512:(j + 1) * 512].partition_broadcast(128), "x1.10", True, semkey="late.c")

    def late_mod(j):
        which = j // 6
        vec, half = (j // 2) % 3, j % 2
        for c in range(2):
            pb = 2 + c
            kb.group("pe", [mm(psf(pb), srep[:, c * 8 + kc, :], wmL[:, kc, :], kc == 0, kc == 7) for kc in range(8)],
                     reads=["srep"] + wmLn, writes=["ps%d" % pb])
            mn = "x1.10" if c == 0 else "x1.11"
            tt("dve", mrowL[c], psf(pb), bmL, ALU.add, ["ps%d" % pb, "x1.10"], [mn])
            if vec == 2 and which == 0:
                sl = slice(half * 512, (half + 1) * 512)
                tt("dve", G_a[:, c, sl], G_a[:, c, sl], mrowL[c], ALU.mult, ["G_a%d" % c, mn], ["G_a%d" % c])
            else:
                tt("dve", scrL, mrowL[c].rearrange("p (a q) -> p a q", a=4),
                   ident_f[:, :].rearrange("p (o q) -> p o q", o=1).broadcast_to([128, 4, 128]), ALU.mult, [mn, "ident_f"], ["x1.11"])
                kb.op("dve", lambda e: e.tensor_reduce(out=coltL[:, :], in_=scrL, axis=AX.X, op=ALU.add), reads=["x1.11"], writes=["coltL"])
                if vec == 0:
                    cp("dve", modB[:, which, c, half * 4:(half + 1) * 4], coltL[:, :], ["coltL"], ["modB"])
                elif vec == 1:
                    gc = cols[:, which * 8 + half * 4: which * 8 + half * 4 + 4]
                    stt("dve", modA[:, which, c, half * 4:(half + 1) * 4], coltL[:, :], 1.0, gc, ALU.add, ALU.mult, ["coltL", "cols"], ["modA"])
                else:
                    cp("dve", gcolF[:, c, half * 4:(half + 1) * 4], coltL[:, :], ["coltL"], ["gcolF"])

    late_next = [4]

    def late_step():
        j = late_next[0]
        if j >= 12:
            return
        late_mod(j)
        if j + 1 < 12:
            late_dma(j + 1)
        late_next[0] = j + 1

    stage_hooks = []
    head_hooks = []
    deferred = []

    def drain(n):
        for _ in range(n):
            if deferred:
                deferred.pop(0)()

    def front_A(x_src, slots):
        for t in range(4):
            kb.dma("sp", x1[:, slots[t], :], x_src[t * 128:(t + 1) * 128, :], "x1.%d" % slots[t], True)
            act(junk[:, :], x1[:, slots[t], :], AF.Square, ["x1.%d" % slots[t]], ["junk", "stA%d" % t], accum_out=stat[:, t:t + 1])
        rstd_from_ss(stat[:, 4:8], stat[:, 0:4], D, ["stA0", "stA1", "stA2", "stA3"], ["stA_r"])
        for t in range(4):
            ts("dve", xnb[:, t, :], x1[:, slots[t], :], stat[:, 4 + t:5 + t], None, ALU.mult, None,
               ["x1.%d" % slots[t], "stA_r"], ["xq"])

    def front_B(slots, c, full, kcol0, rope0, out_row0):
        for kc in range(8):
            b = kc % 2
            kb.group("pe", [tr(psh(b)[:, t * 128:(t + 1) * 128], xnb[:, t, kc * 128:(kc + 1) * 128], ident_bf[:, :]) for t in range(4)],
                     reads=["xq", "ident_bf"], writes=["ps%d" % b])
            if kc % 2 == 0:
                act(hT[:, kc, :], psh(b)[:, 0:512], AF.Identity, ["ps%d" % b, "modA", "modB"], ["hT.%d" % kc],
                    scale=modA[:, 0, c, kc:kc + 1], bias=modB[:, 0, c, kc:kc + 1])
            else:
                ts("dve", hT[:, kc, :], psh(b)[:, 0:512], modA[:, 0, c, kc:kc + 1], modB[:, 0, c, kc:kc + 1], ALU.mult, ALU.add,
                   ["ps%d" % b, "modA", "modB"], ["hT.%d" % kc])
        hreads = ["hT.%d" % k for k in range(8)]
        lo = 0 if full else 256
        for t in range(4):
            b = 2 + t
            kb.group("pe", [mm(psf(b)[:, lo:416], hT[:, kc, t * 128:(t + 1) * 128], w_in_bf[:, kc, lo:416], kc == 0, kc == 7) for kc in range(8)],
                     reads=hreads + ["w_in"], writes=["ps%d" % b])
            o = 64 + 4 * t
            sc, scr_ = "stC%d" % t, "stCr%d" % t
            if full:
                act(junk[:, 0:256], psf(b)[:, 0:256], AF.Square, ["ps%d" % b], ["junk", sc], accum_out=stat[:, o:o + 1], scale=1.0 / 16.0)
            act(junk[:, 256:384], psf(b)[:, 256:384], AF.Square, ["ps%d" % b], ["junk", sc], accum_out=stat[:, o + 1:o + 2], scale=128.0 ** -0.5)
            l0 = o if full else o + 1
            act(stat[:, l0 + 2:o + 4], stat[:, l0:o + 2], AF.Ln, [sc, "eps"], [scr_], scale=1.0, bias=eps_col[:, 0:1])
            act(stat[:, l0 + 2:o + 4], stat[:, l0 + 2:o + 4], AF.Exp, [scr_], [scr_], scale=-0.5)
            if full:
                ts("dve", cqn_bf[:, t, :], psf(b)[:, 0:256], stat[:, o + 2:o + 3], None, ALU.mult, None, ["ps%d" % b, scr_], ["cqn_bf"])
            stt("dve", ckvn_f[:, t, :], psf(b)[:, 256:384], stat[:, o + 3:o + 4], gkv_bc[:, :], ALU.mult, ALU.mult,
                ["ps%d" % b, scr_, "gkv_bc"], ["ckvn_f.%d" % t])
            cp("pool", ckvn_bf[:, t, :], ckvn_f[:, t, :], ["ckvn_f.%d" % t], ["ckvn_bf"])
            if out_row0 is not None:
                cp("dve", kr_f[:, t, :], psf(b)[:, 384:416], ["ps%d" % b], ["kr_f.%d" % t])
                kb.dma("sp", nckv[out_row0 + t * 128: out_row0 + (t + 1) * 128, :], ckvn_f[:, t, :], "ckvn_f.%d" % t, False)
                kb.dma("sp", nkr[out_row0 + t * 128: out_row0 + (t + 1) * 128, :], kr_f[:, t, :], "kr_f.%d" % t, False)
        kb.group("pe", [tr(psh(0)[:, t * 128:(t + 1) * 128], ckvn_bf[:, t, :], ident_bf[:, :]) for t in range(4)],
                 reads=["ckvn_bf", "ident_bf"], writes=["ps0"])
        cp("dve", ckvT[:, kcol0:kcol0 + 512], psh(0)[:, 0:512], ["ps0"], ["ckvT.%d" % (kcol0 // 256), "ckvT.%d" % (kcol0 // 256 + 1)])
        if full:
            for cc in range(2):
                kb.group("pe", [tr(psh(1)[:, t * 128:(t + 1) * 128], cqn_bf[:, t, cc * 128:(cc + 1) * 128], ident_bf[:, :]) for t in range(4)],
                         reads=["cqn_bf", "ident_bf"], writes=["ps1"])
                act(cqnT[:, cc, :], psh(1)[:, 0:512], AF.Copy, ["ps1", "cols"], ["cqnT"], scale=cols[:, 16 + cc:17 + cc])
        kres = ["krT.%d" % (kcol0 // 256), "krT.%d" % (kcol0 // 256 + 1)]
        kb.group("pe", [mm(psf(4)[0:96, :], wkr_pad[:, kc, :], hT[:, kc, :], kc == 0, kc == 7) for kc in range(8)],
                 reads=hreads + ["wkr_pad"], writes=["ps4"])
        if rope0 is None:
            cp("dve", krT[64:96, kcol0:kcol0 + 512], psf(4)[64:96, :], ["ps4"], kres)
        else:
            kb.group("pe", [mm(psf(5)[0:96, :], wkr_swp[:, kc, :], hT[:, kc, :], kc == 0, kc == 7) for kc in range(8)],
                     reads=hreads + ["wkr_swp"], writes=["ps5"])
            g0 = gv[64:96, 0, :]
            g1 = gv[64:96, 1, :]
            tt("dve", g0, psf(4)[64:96, :], CT[64:96, rope0:rope0 + 512], ALU.mult, ["ps4", "CT"], ["gv.0"])
            tt("dve", g1, psf(5)[64:96, :], ST[64:96, rope0:rope0 + 512], ALU.mult, ["ps5", "ST"], ["gv.1"])
            tt("dve", krT[64:96, kcol0:kcol0 + 512], g0, g1, ALU.add, ["gv.0", "gv.1"], kres)
        if not full:
            return
        if stage_hooks:
            stage_hooks.pop(0)()
        for cu in range(4):
            b = 4 + cu % 2
            kb.group("pe", [mm(psf(b), w_in_bf[:, kc, 416 + cu * 128: 416 + (cu + 1) * 128], hT[:, kc, :], kc == 0, kc == 7) for kc in range(8)],
                     reads=hreads + ["w_in"], writes=["ps%d" % b])
            act(uT[:, cu, :], psf(b), AF.Gelu_apprx_tanh, ["ps%d" % b], ["uT"])
        for t in range(4):
            b = 2 + t % 2
            kb.group("pe", [mm(psf(b), hT[:, kc, t * 128:(t + 1) * 128], w_in_bf[:, kc, 928:1440], kc == 0, kc == 7) for kc in range(8)],
                     reads=hreads + ["w_in"], writes=["ps%d" % b])
            act(gv[:, t, :], psf(b), AF.Gelu_apprx_tanh, ["ps%d" % b], ["gv.%d" % t, "stV"], accum_out=stat[:, 16 + t:17 + t])
            act(junk[:, 0:512], gv[:, t, :], AF.Square, ["gv.%d" % t], ["junk", "stV"], accum_out=stat[:, 20 + t:21 + t])
        if stage_hooks:
            stage_hooks.pop(0)()
        for h in range(8):
            b = 4 + h % 2
            if rope0 is None:
                kb.group("pe", [mm(psf(b)[0:96, :], w_uq_bf[:, cc, h, :], cqnT[:, cc, :], cc == 0, cc == 1) for cc in range(2)],
                         reads=["w_uq", "cqnT"], writes=["ps%d" % b])
                if h % 2 == 0:
                    act(qT[0:96, h, :], psf(b)[0:96, :], AF.Copy, ["ps%d" % b], ["xq"])
                else:
                    cp("dve", qT[0:96, h, :], psf(b)[0:96, :], ["ps%d" % b], ["xq"])
            else:
                kb.group("pe", [mm(psf(b)[0:96, :], w_uq_bf[:, cc, h, :], cqnT[:, cc, :], cc == 0, cc == 1) for cc in range(2)],
                         reads=["w_uq", "cqnT"], writes=["ps%d" % b])
                b2 = 6 + h % 2
                kb.group("pe", [mm(psf(b2)[0:96, :], w_uq_swp[:, cc, h, :], cqnT[:, cc, :], cc == 0, cc == 1) for cc in range(2)],
                         reads=["w_uq_swp", "cqnT"], writes=["ps%d" % b2])
                vnf = vn[:, :, :].rearrange("p t d -> p (t d)").bitcast(F32)
                g0 = vnf[64:96, 0:512]
                g1 = vnf[64:96, 512:1024]
                tt("dve", g0, psf(b)[64:96, :], CT[64:96, rope0:rope0 + 512], ALU.mult, ["ps%d" % b, "CT", "vn"], ["vn", "qlock%d" % b])
                act(qT[0:64, h, :], psf(b)[0:64, :], AF.Copy, ["ps%d" % b, "qlock%d" % b], ["xq"])
                tt("dve", g1, psf(b2)[64:96, :], ST[64:96, rope0:rope0 + 512], ALU.mult, ["ps%d" % b2, "ST", "vn"], ["vn"])
                tt("dve", qT[64:96, h, :], g0, g1, ALU.add, ["vn"], ["xq"])

        ts("dve", stat[:, 24:28], stat[:, 16:20], 1.0 / 512, None, ALU.mult, None, ["stV"], ["stVm"])
        tt("dve", stat[:, 28:32], stat[:, 24:28], stat[:, 24:28], ALU.mult, ["stVm"], ["stVq"])
        stt("dve", stat[:, 28:32], stat[:, 20:24], 1.0 / 512, stat[:, 28:32], ALU.mult, ALU.subtract, ["stV", "stVq"], ["stVq"])
        act(stat[:, 28:32], stat[:, 28:32], AF.Ln, ["stVq", "eps"], ["stVq"], scale=1.0, bias=eps_col[:, 0:1])
        act(stat[:, 28:32], stat[:, 28:32], AF.Exp, ["stVq"], ["stVq"], scale=-0.5)
        for t in range(4):
            deferred.append(lambda t=t: ts("dve", gv[:, t, :], gv[:, t, :], stat[:, 24 + t:25 + t], stat[:, 28 + t:29 + t], ALU.subtract, ALU.mult,
                                           ["gv.%d" % t, "stVm", "stVq"], ["gv.%d" % t]))
            deferred.append(lambda t=t: tt("dve", gv[:, t, :], gv[:, t, :], gsgu_bc[:, :], ALU.mult, ["gv.%d" % t, "gsgu_bc"], ["gv.%d" % t]))
            deferred.append(lambda t=t: tt("dve", vn[:, t, :], gv[:, t, :], beta_bc[:, :], ALU.add, ["gv.%d" % t, "beta_bc"], ["vn"]))

    def build_V(kt_list, kcol_of):
        for i, kt in enumerate(kt_list):
            b = 2 + i % 2
            k0 = kcol_of(kt)
            kb.group("pe", [mm(psf(b), ckvT[:, k0:k0 + 128], w_ukv_bf[:, :, 64:128], True, True)],
                     reads=["ckvT.%d" % (k0 // 256), "w_ukv"], writes=["ps%d" % b])
            vdst = Vt[:, kt, :]
            vsrc = psf(b)
            if i % 2 == 0:
                cp("dve", vdst, vsrc, ["ps%d" % b], ["Vt.%d" % (kt // 2)])
            else:
                act(vdst, vsrc, AF.Copy, ["ps%d" % b], ["Vt.%d" % (kt // 2)])

    def build_K(h, k0, nk):
        kt_buf = KTh[h % 2]
        kname = "KTh%d" % (h % 2)
        kslots = ["KT.%d" % i for i in (range(4) if h % 2 == 0 else range(4, 8))]
        nkt = nk // 128
        kblocks = sorted(set((k0 + i * 128) // 256 for i in range(nkt)))
        nchunks = (nk + 511) // 512
        for ci in range(nchunks):
            c0 = ci * 512
            n = min(512, nk - c0)
            b = 2 + ci % 2
            kb.group("pe", [mm(psf(b)[:, 0:n], w_ukv_bf[:, h, :], ckvT[:, k0 + c0:k0 + c0 + n], True, True)],
                     reads=["w_ukv"] + ["ckvT.%d" % kbk for kbk in kblocks], writes=["ps%d" % b])
            cp("dve", kt_buf[0:64, c0:c0 + n], psf(b)[0:64, 0:n], ["ps%d" % b], [kname] + kslots)
        cp("pool", kt_buf[64:96, 0:nk], krT[64:96, k0:k0 + nk], ["krT.%d" % kbk for kbk in kblocks], [kname] + kslots)

    def attention(q0, nq, k0, nk, vt0, cat_col0):
        nkt = nk // 128

        def normalise(h):
            pacc, pden = 4 + h % 2, h % 2
            po = (h % 2) * 64
            den = psf(pden)[po:po + 64, 0:nq]
            if nq <= 256 and h % 2 == 1:
                act(rec[po:po + 64, 0:nq], den, AF.Ln, ["ps%d" % pden], ["rec%d" % (h % 2)])
                act(rec[po:po + 64, 0:nq], rec[po:po + 64, 0:nq], AF.Exp, ["rec%d" % (h % 2)], ["rec%d" % (h % 2)], scale=-1.0)
            else:
                kb.op("dve", lambda e: e.reciprocal(out=rec[po:po + 64, 0:nq], in_=den), reads=["ps%d" % pden], writes=["rec%d" % (h % 2)])
            tt("dve", hT[po:po + 64, h // 2, cat_col0:cat_col0 + nq], psf(pacc)[po:po + 64, 0:nq], rec[po:po + 64, 0:nq], ALU.mult,
               ["ps%d" % pacc, "rec%d" % (h % 2)], ["hT.%d" % (h // 2)])

        def lw_of(h, kt):
            hp = h - (h % 2)
            return Vt[:, vt0 + kt, hp * 64:(hp + 2) * 64]

        if nkt * nq <= 512:
            kblk = ["ckvT.%d" % kbk for kbk in sorted(set((k0 + i * 128) // 256 for i in range(nkt)))]
            krblk = ["krT.%d" % kbk for kbk in sorted(set((k0 + i * 128) // 256 for i in range(nkt)))]

            def kslot(h):
                return KTh[h // 4], (h % 4) * nk
            for h_ in range(8):
                kbuf, kc0 = kslot(h_)
                cp("pool", kbuf[64:96, kc0:kc0 + nk], krT[64:96, k0:k0 + nk], krblk, ["KT.%d" % h_])
            for hp in range(4):
                b = 2 + hp % 2
                kb.group("pe", [mm(psf(b)[:, i * nk:(i + 1) * nk], w_ukv_bf[:, 2 * hp + i, :], ckvT[:, k0:k0 + nk], True, True) for i in range(2)],
                         reads=["w_ukv"] + kblk, writes=["ps%d" % b])
                kbuf, kc0 = kslot(2 * hp)
                if hp % 2 == 0:
                    cp("dve", kbuf[0:64, kc0:kc0 + 2 * nk], psf(b)[0:64, 0:2 * nk], ["ps%d" % b], ["KT.%d" % (2 * hp), "KT.%d" % (2 * hp + 1)])
                else:
                    act(kbuf[0:64, kc0:kc0 + 2 * nk], psf(b)[0:64, 0:2 * nk], AF.Copy, ["ps%d" % b], ["KT.%d" % (2 * hp), "KT.%d" % (2 * hp + 1)])

            def s_exp(h):
                b = 6 + h % 2
                kbuf, kc0 = kslot(h)
                kb.group("pe", [mm(psf(b)[:, kt * nq:(kt + 1) * nq], kbuf[0:96, kc0 + kt * 128:kc0 + (kt + 1) * 128], qT[0:96, h, q0:q0 + nq], True, True)
                                for kt in range(nkt)], reads=["KT.%d" % h, "xq"], writes=["ps%d" % b])
                act(PT[h % 3][:, 0:nkt * nq], psf(b)[:, 0:nkt * nq], AF.Exp, ["ps%d" % b], ["PT%d" % (h % 3)], scale=ATTN_SCALE)
            s_exp(0)
            for h in range(8):
                if h + 1 < 8:
                    s_exp(h + 1)
                pacc, pden = 4 + h % 2, h % 2
                pt = PT[h % 3]
                fns = []
                for kt in range(nkt):
                    fns.append(mm(psf(pacc)[:, 0:nq], lw_of(h, kt), pt[:, kt * nq:(kt + 1) * nq], kt == 0, kt == nkt - 1))
                    fns.append(mm(psf(pden)[:, 0:nq], ones_bf[:, :], pt[:, kt * nq:(kt + 1) * nq], kt == 0, kt == nkt - 1))
                kb.group("pe", fns, reads=["PT%d" % (h % 3), "ones_bf"] + ["Vt.%d" % ((vt0 + kt) // 2) for kt in range(nkt)],
                         writes=["ps%d" % pacc, "ps%d" % pden])
                normalise(h)
                drain(1)
            return
        build_K(0, k0, nk)
        build_K(1, k0, nk)
        for h in range(8):
            kt_buf = KTh[h % 2]
            kname = "KTh%d" % (h % 2)
            pacc = 4 + h % 2
            pden = h % 2

            def s_mm(kt):
                b = 6 + kt % 2
                kb.group("pe", [mm(psf(b)[:, 0:nq], kt_buf[0:96, kt * 128:(kt + 1) * 128], qT[0:96, h, q0:q0 + nq], True, True)],
                         reads=[kname, "xq"] + ["KT.%d" % i for i in (range(4) if h % 2 == 0 else range(4, 8))], writes=["ps%d" % b])
            s_mm(0)
            for kt in range(nkt):
                if kt + 1 < nkt:
                    s_mm(kt + 1)
                b = 6 + kt % 2
                pt = PT[kt % 3]
                ptn = "PT%d" % (kt % 3)
                act(pt[:, 0:nq], psf(b)[:, 0:nq], AF.Exp, ["ps%d" % b], [ptn], scale=ATTN_SCALE)
                kb.group("pe", [mm(psf(pacc)[:, 0:nq], lw_of(h, kt), pt[:, 0:nq], kt == 0, kt == nkt - 1),
                                mm(psf(pden)[:, 0:nq], ones_bf[:, :], pt[:, 0:nq], kt == 0, kt == nkt - 1)],
                         reads=[ptn, "Vt.%d" % ((vt0 + kt) // 2), "ones_bf"],
                         writes=(["ps%d" % pacc, "ps%d" % pden] if kt in (0, nkt - 1) else []))
            if h + 2 < 8:
                build_K(h + 2, k0, nk)
            normalise(h)
            drain(2)
            if head_hooks:
                head_hooks.pop(0)()

    def sgu_and_out(slots, c, ydst):
        drain(100)
        for k in range(4):
            pa, pb_ = (4, 5) if k % 2 == 0 else (6, 7)
            fns = []
            for j in range(4):
                cs = slice(j * 128, (j + 1) * 128)
                fns.append(mm(psf(pa)[:, cs], vn[:, j, k * 128:(k + 1) * 128], wsguT[:, 2 * k, :], True, False))
                fns.append(mm(psf(pa)[:, cs], cst_bf[:, :], bTb[:, 2 * k, :], False, True))
                fns.append(mm(psf(pb_)[:, cs], vn[:, j, k * 128:(k + 1) * 128], wsguT[:, 2 * k + 1, :], True, False))
                fns.append(mm(psf(pb_)[:, cs], cst_bf[:, :], bTb[:, 2 * k + 1, :], False, True))
            kb.group("pe", fns, reads=["vn", "wsguT", "cst_bf", "bTb"], writes=["ps%d" % pa, "ps%d" % pb_])
            tt("dve", hT[0:64, 4 + k, :], psf(pa)[0:64, :], uT[0:64, k, :], ALU.mult, ["ps%d" % pa, "uT"], ["hT.%d" % (4 + k)])
            tt("dve", hT[64:128, 4 + k, :], psf(pb_)[64:128, :], uT[64:128, k, :], ALU.mult, ["ps%d" % pb_, "uT"], ["hT.%d" % (4 + k)])
        creads = ["hT.%d" % k for k in range(8)]
        for t in range(4):
            o = 32 + 4 * (t % 2)
            sm, smr = "stM%d" % (t % 2), "stMr%d" % (t % 2)
            for half in range(2):
                b = 2 + 2 * (t % 2) + half
                kb.group("pe", [mm(psf(b), hT[:, kc, t * 128:(t + 1) * 128], w_o_bf[:, kc, half * 512:(half + 1) * 512], kc == 0, kc == 7) for kc in range(8)],
                         reads=creads + ["w_o"], writes=["ps%d" % b])
                act(junk[:, 0:512], psf(b), AF.Square, ["ps%d" % b], ["junk", sm], accum_out=stat[:, o + half:o + half + 1])
            tt("dve", stat[:, o + 2:o + 3], stat[:, o:o + 1], stat[:, o + 1:o + 2], ALU.add, [sm], [smr])
            rstd_from_ss(stat[:, o + 3:o + 4], stat[:, o + 2:o + 3], D, [smr], [smr])
            s = slots[t]
            for half in range(2):
                b = 2 + 2 * (t % 2) + half
                sl = slice(half * 512, (half + 1) * 512)
                sc = 1 + half
                stt("dve", gv[:, sc, :], psf(b), stat[:, o + 3:o + 4], G_a[:, c, sl], ALU.mult, ALU.mult,
                    ["ps%d" % b, smr, "G_a%d" % c], ["gv.%d" % sc])
                tt("dve" if half == 0 else "pool", x1[:, s, sl], x1[:, s, sl], gv[:, sc, :], ALU.add, ["gv.%d" % sc, "x1.%d" % s], ["x1.%d" % s])

    kb.dma("pool", cc_ld[:], cckv.rearrange("(t p) d -> p t d", p=128), "cc_ld", True)
    kb.op("pool", lambda e: e.memset(ck_ld[:], 0.0), writes=["ck_ld"])
    kb.dma("pool", ck_ld[:, :, 64:96], ckr.rearrange("(t p) d -> p t d", p=128), "ck_ld", True)
    kb.group("pe", [tr(psh(0)[:, t * 128:(t + 1) * 128], cc_ld[:, t, :], ident_bf[:, :]) for t in range(2)], reads=["cc_ld", "ident_bf"], writes=["ps0"])
    cp("dve", ckvT[:, 0:256], psh(0)[:, 0:256], ["ps0"], ["ckvT.0"])
    kb.group("pe", [tr(psh(1)[0:96, t * 128:(t + 1) * 128], ck_ld[:, t, :], ident_bf[:, :]) for t in range(2)], reads=["ck_ld", "ident_bf"], writes=["ps1"])
    cp("dve", krT[64:96, 0:256], psh(1)[64:96, 0:256], ["ps1"], ["krT.0"])
    late_dma(4)
    front_A(xs[512:1024, :], [4, 5, 6, 7])
    front_B([4, 5, 6, 7], 1, False, 768, 512, None)
    late_step()
    front_A(xs[0:512, :], [0, 1, 2, 3])
    stage_hooks.append(late_step)
    stage_hooks.append(late_step)
    front_B([0, 1, 2, 3], 1, True, 256, 0, None)
    del stage_hooks[:]
    kb.dma("pool", w_o_bf[:], w_o.rearrange("(kc p) n -> p kc n", p=128), "w_o", True)
    build_V(list(range(10)), lambda kt: kt * 128)
    for i_ in range(8):
        head_hooks.append(late_step if i_ in (2, 5) else (lambda: None))
    attention(0, 512, 0, 1280, 0, 0)
    del head_hooks[:]
    front_A(xp[0:512, :], [4, 5, 6, 7])
    sgu_and_out([0, 1, 2, 3], 1, None)
    late_step()
    for g in range(2):
        slots = [4 + 4 * g + t for t in range(4)]
        front_B(slots, 0, True, 0, None, g * 512)
        if g == 0:
            late_step()
        build_V([0, 1, 2, 3], lambda kt: kt * 128)
        for j in range(2):
            attention(j * 256, 256, j * 256, 256, 2 * j, j * 256)
            if g == 0 and j == 0:
                late_step()
        if g == 0:
            while late_next[0] < 12:
                late_step()
            front_A(xp[512:1024, :], [8, 9, 10, 11])
        sgu_and_out(slots, 0, None)

    kb.barrier()
    W1.close()
    P1.close()

    P2 = contextlib.ExitStack()
    G_f = kb.sb(P2, "G_f", [128, 2, D], F32)
    w1_bf = kb.sb(P2, "w1_bf", [128, 8, 4 * D], BF16)
    w2_bf = kb.sb(P2, "w2_bf", [128, 32, D], BF16)
    for c in range(2):
        kb.dma("sp", G_f[:, c, :], g_ffn_post.partition_broadcast(128), "G_f%d" % c, True)
    def wgroup_dma(g):
        kb.dma("pool", w1_bf[:, :, g * 512:(g + 1) * 512], w_ff1[:, g * 512:(g + 1) * 512].rearrange("(kc p) n -> p kc n", p=128), "w1.%d" % g, True)
        kb.dma("pool", w2_bf[:, g * 4:(g + 1) * 4, :], w_ff2[g * 512:(g + 1) * 512, :].rearrange("(kc p) n -> p kc n", p=128), "w2.%d" % g, True)

    for g in range(8):
        wgroup_dma(g)
    W2 = contextlib.ExitStack()
    h2T = [kb.sb(W2, "h2T%d" % i, [128, 8, 256], BF16) for i in range(2)]
    xn2 = kb.sb(W2, "xn2", [128, 2, D], BF16)
    NH = 6
    hid = [kb.sb(W2, "hid%d" % i, [128, 256], BF16) for i in range(NH)]
    rl = [kb.sb(W2, "rl%d" % i, [128, 256], F32) for i in range(2)]
    junk2 = kb.sb(W2, "junk2", [128, 512], BF16)
    xn2f = xn2[:, :, :].rearrange("p t d -> p (t d)").bitcast(F32)
    ytmp = [xn2f[:, 0:512], xn2f[:, 512:1024]]

    def prep_stats(sg):
        tiles = [2 * sg, 2 * sg + 1]
        for i, s_ in enumerate(tiles):
            act(xn2[:, i, :], x1[:, s_, :], AF.Square, ["x1.%d" % s_], ["xn2", "yt0", "yt1", "stH"], accum_out=stat[:, 40 + i:41 + i])
        rstd_from_ss(stat[:, 42:44], stat[:, 40:42], D, ["stH"], ["stHr"])
        for i, s_ in enumerate(tiles):
            ts("dve", xn2[:, i, :], x1[:, s_, :], stat[:, 42 + i:43 + i], None, ALU.mult, None, ["x1.%d" % s_, "stHr", "xn2"], ["xn2"])

    def prep_tr(sg, kc):
        c_ = 1 if sg < 2 else 0
        b = kc % 2
        hb = h2T[sg % 2]
        kb.group("pe", [tr(psh(b)[:, i * 128:(i + 1) * 128], xn2[:, i, kc * 128:(kc + 1) * 128], ident_bf[:, :]) for i in range(2)],
                 reads=["xn2", "ident_bf"], writes=["ps%d" % b])
        act(hb[:, kc, :], psh(b)[:, 0:256], AF.Identity, ["ps%d" % b, "modA", "modB"], ["h2T%d.%d" % (sg % 2, kc)],
            scale=modA[:, 1, c_, kc:kc + 1], bias=modB[:, 1, c_, kc:kc + 1])

    prep_stats(0)
    for kc in range(8):
        prep_tr(0, kc)
    ones_f = junk2[:, :].bitcast(F32)[:, 0:128]
    kb.op("dve", lambda e: e.memset(ones_f, 1.0), writes=["junk2"])
    for c in range(2):
        for half in range(2):
            pb = 4 + half
            for kq in range(4):
                dg = rl[kq // 2][:, (kq % 2) * 128:(kq % 2 + 1) * 128]
                ts("dve", dg, ident_f[:, :], gcolF[:, c, half * 4 + kq:half * 4 + kq + 1], None, ALU.mult, None,
                   ["ident_f", "gcolF"], ["rl%d" % (kq // 2)])
            kb.group("pe", [mm(psf(pb)[:, kq * 128:(kq + 1) * 128], ones_f, rl[kq // 2][:, (kq % 2) * 128:(kq % 2 + 1) * 128], True, True)
                            for kq in range(4)], reads=["junk2", "rl0", "rl1"], writes=["ps%d" % pb])
            sl = slice(half * 512, (half + 1) * 512)
            tt("dve", G_f[:, c, sl], G_f[:, c, sl], psf(pb), ALU.mult, ["G_f%d" % c, "ps%d" % pb], ["G_f%d" % c])

    LA = NH - 1
    NFC = 32

    def ff1(sg, fc):
        hb_ = h2T[sg % 2]
        b = 2 + fc % 2
        r = fc % 2
        r4 = (sg * NFC + fc) % NH
        kb.group("pe", [mm(psf(b)[:, 0:256], w1_bf[:, kc, fc * 128:(fc + 1) * 128], hb_[:, kc, :], kc == 0, kc == 7) for kc in range(8)],
                 reads=["h2T%d.%d" % (sg % 2, kc) for kc in range(8)] + ["w1.%d" % (fc // 4)], writes=["ps%d" % b])
        act(rl[r][:, :], psf(b)[:, 0:256], AF.Relu, ["ps%d" % b], ["rl%d" % r])
        tt("dve", hid[r4][:, :], rl[r][:, :], rl[r][:, :], ALU.mult, ["rl%d" % r], ["hid%d" % r4])

    def ff2(sg, fc):
        r = (sg * NFC + fc) % NH
        for i in range(2):
            fns = [mm(psf(4 + 2 * i + half), hid[r][:, i * 128:(i + 1) * 128], w2_bf[:, fc, half * 512:(half + 1) * 512], fc == 0, fc == NFC - 1)
                   for half in range(2)]
            kb.group("pe", fns, reads=["hid%d" % r, "w2.%d" % (fc // 4)],
                     writes=(["ps%d" % (4 + 2 * i), "ps%d" % (5 + 2 * i)] if fc in (0, NFC - 1) else []))

    def tail(sg):
        tiles = [2 * sg, 2 * sg + 1]
        c = 1 if sg < 2 else 0
        hdead = h2T[sg % 2][:, :, :].rearrange("p k n -> p (k n)").bitcast(F32)
        hnames = ["h2T%d.%d" % (sg % 2, kc) for kc in range(8)]
        tmps = [(ytmp[0], ["xn2", "yt0"], ["yt0"]), (ytmp[1], ["xn2", "yt1"], ["yt1"]),
                (hdead[:, 0:512], hnames, hnames), (hdead[:, 512:1024], hnames, hnames)]
        for i in range(2):
            for half in range(2):
                b = 4 + 2 * i + half
                sl = slice(half * 512, (half + 1) * 512)
                tb, wn, rn = tmps[2 * i + half]
                act(junk2[:, :], psf(b), AF.Square, ["ps%d" % b], ["junk2", "stJ", "pslock%d" % b], accum_out=stat[:, 44 + 2 * i + half:45 + 2 * i + half])
                tt("dve", tb, psf(b), G_f[:, c, sl], ALU.mult, ["ps%d" % b, "pslock%d" % b, "G_f%d" % c], wn)
        kb.op("dve", lambda e: e.tensor_reduce(out=stat[:, 48:50], in_=stat[:, 44:48].rearrange("p (i h) -> p i h", h=2), axis=AX.X, op=ALU.add),
              reads=["stJ"], writes=["stJr"])
        rstd_from_ss(stat[:, 50:52], stat[:, 48:50], D, ["stJr"], ["stJr"])
        for i, s_ in enumerate(tiles):
            for half in range(2):
                sl = slice(half * 512, (half + 1) * 512)
                tb, wn, rn = tmps[2 * i + half]
                stt("dve", x1[:, s_, sl], tb, stat[:, 50 + i:51 + i], x1[:, s_, sl], ALU.mult, ALU.add,
                    rn + ["stJr", "x1.%d" % s_], ["x1.%d" % s_])
            if s_ < 4:
                kb.dma("sp", ys[s_ * 128:(s_ + 1) * 128, :], x1[:, s_, :], "x1.%d" % s_, False)
            else:
                kb.dma("sp", yp[(s_ - 4) * 128:(s_ - 3) * 128, :], x1[:, s_, :], "x1.%d" % s_, False)

    NS = 6 * NFC
    for n in range(NS + LA):
        if n < NS:
            sg, fc = divmod(n, NFC)
            ff1(sg, fc)
            if sg + 1 < 6:
                if fc == 10:
                    prep_stats(sg + 1)
                if 14 <= fc < 22:
                    prep_tr(sg + 1, fc - 14)
        m = n - LA
        if m >= 0:
            sg2, fc2 = divmod(m, NFC)
            ff2(sg2, fc2)
            if fc2 == NFC - 1:
                tail(sg2)

    kb.finish("sp")
    W2.close()
    P2.close()
    kb.close()
    return nc, dbg_outs


_CACHE = {}


def _get_program():
    if "nc" not in _CACHE:
        _CACHE["nc"] = build_program(DEBUG)
    return _CACHE["nc"]


def make_in_maps(inputs):
    f = lambda a: np.ascontiguousarray(np.asarray(a, dtype=np.float32))
    x_prompt = f(inputs["x_prompt"])
    x_sample = f(inputs["x_sample"])
    cache_ckv = f(inputs["cache_ckv"])
    cache_krope = f(inputs["cache_krope"])
    c = f(inputs["c"])
    c_ctx = f(inputs["c_ctx"])
    shared = {
        "w_mod": f(inputs["w_mod"])[0], "b_mod": f(inputs["b_mod"])[0],
        "g_attn_pre": f(inputs["g_attn_pre"])[0], "g_attn_post": f(inputs["g_attn_post"])[0],
        "w_in": f(inputs["w_in"])[0], "g_q": f(inputs["g_q"])[0], "w_uq": f(inputs["w_uq"])[0],
        "g_kv": f(inputs["g_kv"])[0], "w_ukv": f(inputs["w_ukv"])[0], "w_sgu": f(inputs["w_sgu"])[0],
        "b_sgu": f(inputs["b_sgu"])[0], "g_sgu": f(inputs["g_sgu"])[0], "beta_sgu": f(inputs["beta_sgu"])[0],
        "w_o": f(inputs["w_o"])[0], "g_ffn_pre": f(inputs["g_ffn_pre"])[0], "g_ffn_post": f(inputs["g_ffn_post"])[0],
        "w_ff1": f(inputs["w_ff1"])[0], "w_ff2": f(inputs["w_ff2"])[0],
    }
    in_maps = []
    for i in range(8):
        b, hf = i // 2, i % 2
        own = x_sample[b, hf * 512:(hf + 1) * 512]
        oth = x_sample[b, (1 - hf) * 512:(2 - hf) * 512]
        m = dict(shared)
        m["xp"] = np.ascontiguousarray(x_prompt[4 * i:4 * i + 4].reshape(1024, D))
        m["xs"] = np.ascontiguousarray(np.concatenate([own, oth], axis=0))
        m["cckv"] = np.ascontiguousarray(cache_ckv[b, 0])
        m["ckr"] = np.ascontiguousarray(cache_krope[b, 0])
        m["cond"] = np.ascontiguousarray(np.stack([c_ctx, c[b]], axis=0))
        m["meta"] = np.array([8.0 * hf, 8.0 * (1 - hf)], dtype=np.float32)
        in_maps.append(m)
    return in_maps


def kernel(**inputs):
    nc, _ = _get_program()
    in_maps = make_in_maps(inputs)
    res = run_bass_kernel_spmd(nc, in_maps, core_ids=list(range(8)))
    y_prompt = np.zeros((32, 256, D), np.float32)
    y_sample = np.zeros((4, 1024, D), np.float32)
    new_ckv = np.zeros((32, 1, 256, 128), np.float32)
    new_kr = np.zeros((32, 1, 256, 32), np.float32)
    for i in range(8):
        r = res.results[i]
        b, hf = i // 2, i % 2
        y_prompt[4 * i:4 * i + 4] = np.asarray(r["yp"]).reshape(4, 256, D)
        y_sample[b, hf * 512:(hf + 1) * 512] = np.asarray(r["ys"])
        new_ckv[4 * i:4 * i + 4, 0] = np.asarray(r["nckv"]).reshape(4, 256, 128)
        new_kr[4 * i:4 * i + 4, 0] = np.asarray(r["nkr"]).reshape(4, 256, 32)
    return (y_prompt, y_sample, new_ckv, new_kr)
```

```python
import contextlib
import math
import numpy as np
import concourse.bass as bass
import concourse.mybir as mybir
from concourse.bass_utils import run_bass_kernel_spmd

F32 = mybir.dt.float32
BF16 = mybir.dt.bfloat16
I32 = mybir.dt.int32
AF = mybir.ActivationFunctionType
ALU = mybir.AluOpType
AX = mybir.AxisListType

D = 1024
NT_P = 8
EPS = 1e-6
ATTN_SCALE = 96.0 ** -0.5
DEBUG = False


class KB:
    def __init__(self, nc):
        self.nc = nc
        self.stack = contextlib.ExitStack()
        self.eng = {"pe": nc.tensor, "act": nc.scalar, "dve": nc.vector, "pool": nc.gpsimd, "sp": nc.sync}
        self.esem = {}
        self.ecount = {}
        for n in ("pe", "act", "dve", "pool"):
            self.esem[n] = self.stack.enter_context(nc.semaphore("s_" + n))
            self.ecount[n] = 0
        self.waited = {n: {} for n in self.eng}
        self.res = {}
        self.dsem = {}
        self.semval = {}
        self.nwaits = 0
        self.nops = 0

    def sb(self, stack, name, shape, dt):
        return stack.enter_context(self.nc.sbuf_tensor(name, list(shape), dt))

    def ps(self, stack, name, shape, dt):
        return stack.enter_context(self.nc.psum_tensor(name, list(shape), dt))

    def _r(self, name):
        if name not in self.res:
            self.res[name] = [None, []]
        return self.res[name]

    def _wait(self, engname, ev):
        if ev is None:
            return
        sem, val = ev
        w = self.waited[engname]
        if w.get(id(sem), 0) >= val:
            return
        w[id(sem)] = val
        self.eng[engname].wait_ge(sem, val)
        self.nwaits += 1

    def _deps(self, engname, reads, writes):
        for r in reads:
            st = self._r(r)
            self._wait(engname, st[0])
            if r.startswith("ps"):
                mine = id(self.esem.get(engname))
                for ev in st[1]:
                    if id(ev[0]) != mine:
                        self._wait(engname, ev)
        for w in writes:
            st = self._r(w)
            self._wait(engname, st[0])
            for ev in st[1]:
                self._wait(engname, ev)

    def _record(self, ev, reads, writes):
        for r in reads:
            lst = self._r(r)[1]
            lst.append(ev)
            if len(lst) > 64:
                best = {}
                for e in lst:
                    if id(e[0]) not in best or best[id(e[0])][1] < e[1]:
                        best[id(e[0])] = e
                lst[:] = list(best.values())
        for w in writes:
            st = self._r(w)
            st[0] = ev
            st[1] = []
        self.semval[id(ev[0])] = ev

    def op(self, engname, fn, reads=(), writes=()):
        self._deps(engname, reads, writes)
        ins = fn(self.eng[engname])
        self.ecount[engname] += 1
        ev = (self.esem[engname], self.ecount[engname])
        ins.then_inc(ev[0], 1)
        self._record(ev, reads, writes)
        self.nops += 1
        return ev

    def group(self, engname, fns, reads=(), writes=()):
        self._deps(engname, reads, writes)
        ins = None
        for fn in fns:
            ins = fn(self.eng[engname])
            self.nops += 1
        self.ecount[engname] += 1
        ev = (self.esem[engname], self.ecount[engname])
        ins.then_inc(ev[0], 1)
        self._record(ev, reads, writes)
        return ev

    def dma(self, q, out, in_, res, write, semkey=None, **kw):
        reads = [] if write else [res]
        writes = [res] if write else []
        self._deps(q, reads, writes)
        sk = semkey or res
        if sk not in self.dsem:
            self.dsem[sk] = [self.stack.enter_context(self.nc.semaphore("d_" + sk.replace(".", "_"))), 0]
        d = self.dsem[sk]
        self.eng[q].dma_start(out=out, in_=in_, **kw).then_inc(d[0], 16)
        d[1] += 16
        ev = (d[0], d[1])
        self._record(ev, reads, writes)
        return ev

    def barrier(self, exclude=()):
        skip = set()
        for name, d in self.dsem.items():
            if any(name.startswith(p) for p in exclude):
                skip.add(id(d[0]))
        for e in self.eng:
            for sid, ev in list(self.semval.items()):
                if sid in skip:
                    continue
                self._wait(e, ev)

    def finish(self, engname="sp"):
        for sid, ev in list(self.semval.items()):
            self._wait(engname, ev)

    def close(self):
        self.stack.close()


def build_program(debug=False):
    nc = bass.Bass("TRN2", target_bir_lowering=False)

    def din(name, shape):
        return nc.dram_tensor(name, list(shape), F32, kind="ExternalInput").ap()

    def dout(name, shape):
        return nc.dram_tensor(name, list(shape), F32, kind="ExternalOutput").ap()

    xp = din("xp", [1024, D])
    xs = din("xs", [1024, D])
    cckv = din("cckv", [256, 128])
    ckr = din("ckr", [256, 32])
    cond = din("cond", [2, D])
    meta = din("meta", [2])
    w_mod = din("w_mod", [D, 6 * D])
    b_mod = din("b_mod", [6 * D])
    g_attn_pre = din("g_attn_pre", [D])
    g_attn_post = din("g_attn_post", [D])
    w_in = din("w_in", [D, 1440])
    g_q = din("g_q", [256])
    w_uq = din("w_uq", [256, 768])
    g_kv = din("g_kv", [128])
    w_ukv = din("w_ukv", [128, 1024])
    w_sgu = din("w_sgu", [8, 128, 128])
    b_sgu = din("b_sgu", [8, 128])
    g_sgu = din("g_sgu", [512])
    beta_sgu = din("beta_sgu", [512])
    w_o = din("w_o", [D, D])
    g_ffn_pre = din("g_ffn_pre", [D])
    g_ffn_post = din("g_ffn_post", [D])
    w_ff1 = din("w_ff1", [D, 4 * D])
    w_ff2 = din("w_ff2", [4 * D, D])

    yp = dout("yp", [1024, D])
    ys = dout("ys", [512, D])
    nckv = dout("nckv", [1024, 128])
    nkr = dout("nkr", [1024, 32])

    kb = KB(nc)
    dbg_outs = {}

    def dbg(name, ap, res, shape, dt=F32, stk=None):
        if not debug:
            return
        o = nc.dram_tensor("dbg_" + name, list(shape), dt, kind="ExternalOutput").ap()
        dbg_outs[name] = o
        kb.dma("sp", o, ap, res, False)

    def act(out, in_, func, reads, writes, **kw):
        return kb.op("act", lambda e: e.activation(out=out, in_=in_, func=func, **kw), reads, writes)

    def tt(eng, out, in0, in1, op, reads, writes):
        return kb.op(eng, lambda e: e.tensor_tensor(out=out, in0=in0, in1=in1, op=op), reads, writes)

    def ts(eng, out, in0, s1, s2, op0, op1, reads, writes):
        if op1 is None:
            return kb.op(eng, lambda e: e.tensor_scalar(out=out, in0=in0, scalar1=s1, scalar2=None, op0=op0), reads, writes)
        return kb.op(eng, lambda e: e.tensor_scalar(out=out, in0=in0, scalar1=s1, scalar2=s2, op0=op0, op1=op1), reads, writes)

    def stt(eng, out, in0, scalar, in1, op0, op1, reads, writes):
        return kb.op(eng, lambda e: e.scalar_tensor_tensor(out=out, in0=in0, scalar=scalar, in1=in1, op0=op0, op1=op1), reads, writes)

    def cp(eng, out, in_, reads, writes):
        return kb.op(eng, lambda e: e.tensor_copy(out, in_), reads, writes)

    def mm(out, lhsT, rhs, start, stop):
        return lambda e: e.matmul(out, lhsT, rhs, start=start, stop=stop)

    def tr(out, in_, ident):
        return lambda e: e.transpose(out, in_, ident)

    def rstd_from_ss(dst, ss, n, reads, writes):
        act(dst, ss, AF.Ln, list(reads) + ["eps"], writes, scale=1.0 / n, bias=eps_col[:, 0:1])
        act(dst, dst, AF.Exp, writes, writes, scale=-0.5)

    PS = kb.stack
    psb = [kb.ps(PS, "psb%d" % i, [128, 512], F32) for i in range(8)]

    def psf(i):
        return psb[i][:, :]

    def psh(i):
        return psb[i][:, :].bitcast(BF16)

    x1 = kb.sb(PS, "x1", [128, 12, D], F32)
    ident_bf = kb.sb(PS, "ident_bf", [128, 128], BF16)
    ident_f = kb.sb(PS, "ident_f", [128, 128], F32)
    ones_bf = kb.sb(PS, "ones_bf", [128, 128], BF16)
    eps_col = kb.sb(PS, "eps_col", [128, 1], F32)
    srep = kb.sb(PS, "srep", [128, 16, 128], BF16)
    cols = kb.sb(PS, "cols", [128, 18], F32)
    modA = kb.sb(PS, "modA", [128, 2, 2, 8], F32)
    modB = kb.sb(PS, "modB", [128, 2, 2, 8], F32)
    stat = kb.sb(PS, "stat", [128, 96], F32)
    gcolF = kb.sb(PS, "gcolF", [128, 2, 8], F32)
    coltL = kb.sb(PS, "coltL", [128, 4], F32)

    kb.op("pool", lambda e: e.memset(ident_bf[:], 0.0), writes=["ident_bf"])
    kb.op("pool", lambda e: e.affine_select(out=ident_bf[:], in_=ident_bf[:], compare_op=ALU.not_equal, fill=1.0,
                                            base=0, pattern=[[-1, 128]], channel_multiplier=1), reads=["ident_bf"], writes=["ident_bf"])
    kb.op("pool", lambda e: e.memset(ident_f[:], 0.0), writes=["ident_f"])
    kb.op("pool", lambda e: e.affine_select(out=ident_f[:], in_=ident_f[:], compare_op=ALU.not_equal, fill=1.0,
                                            base=0, pattern=[[-1, 128]], channel_multiplier=1), reads=["ident_f"], writes=["ident_f"])
    kb.op("pool", lambda e: e.memset(ones_bf[:], 1.0), writes=["ones_bf"])

    kb.op("pool", lambda e: e.memset(eps_col[:], EPS), writes=["eps"])

    P1 = contextlib.ExitStack()
    w_in_bf = kb.sb(P1, "w_in_bf", [128, 8, 1440], BF16)
    wkr_pad = kb.sb(P1, "wkr_pad", [128, 8, 96], BF16)
    wkr_swp = kb.sb(P1, "wkr_swp", [128, 8, 96], BF16)
    w_uq_bf = kb.sb(P1, "w_uq_bf", [128, 2, 8, 96], BF16)
    w_uq_swp = kb.sb(P1, "w_uq_swp", [128, 2, 8, 96], BF16)
    w_ukv_bf = kb.sb(P1, "w_ukv_bf", [128, 8, 128], BF16)
    wsguT = kb.sb(P1, "wsguT", [128, 8, 128], BF16)
    w_o_bf = kb.sb(P1, "w_o_bf", [128, 8, D], BF16)
    G_a = kb.sb(P1, "G_a", [128, 2, D], F32)
    gkv_bc = kb.sb(P1, "gkv_bc", [128, 128], F32)
    gsgu_bc = kb.sb(P1, "gsgu_bc", [128, 512], F32)
    beta_bc = kb.sb(P1, "beta_bc", [128, 512], F32)
    bT_bc = kb.sb(P1, "bT_bc", [128, 8, 128], F32)
    bTb = kb.sb(P1, "bTb", [128, 8, 128], BF16)
    cst_bf = kb.sb(P1, "cst_bf", [128, 128], BF16)
    CT = kb.sb(P1, "CT", [128, 1024], F32)
    ST = kb.sb(P1, "ST", [128, 1024], F32)

    kb.op("pool", lambda e: e.memset(cst_bf[:], 1.0 / 128.0), writes=["cst_bf"])

    T0 = contextlib.ExitStack()
    rows = kb.sb(T0, "rows", [32, 128], F32)
    rows_s = kb.sb(T0, "rows_s", [32, 128], F32)
    rows_b = kb.sb(T0, "rows_b", [32, 128], BF16)
    sT = kb.sb(T0, "sT", [128, 16], BF16)
    wsg_ld = kb.sb(T0, "wsg_ld", [128, 8, 128], BF16)
    mt = kb.sb(T0, "mt", [128, 2], F32)
    pidx = kb.sb(T0, "pidx", [128, 1], I32)
    ktmp = kb.sb(T0, "ktmp", [128, 1], I32)
    kf = kb.sb(T0, "kf", [128, 1], F32)
    freq = kb.sb(T0, "freq", [128, 1], F32)
    isrow = kb.sb(T0, "isrow", [128, 1], F32)
    rc = kb.sb(T0, "rc", [128, 80], F32)
    tm80 = kb.sb(T0, "tm80", [128, 80], F32)
    m80 = kb.sb(T0, "m80", [128, 80], F32)
    ti80 = kb.sb(T0, "ti80", [128, 80], I32)
    sn80 = kb.sb(T0, "sn80", [128, 80], F32)
    cs80 = kb.sb(T0, "cs80", [128, 80], F32)
    notrow = kb.sb(T0, "notrow", [128, 1], F32)
    wm = [kb.sb(T0, "wm%d" % i, [128, 8, 512], BF16) for i in range(2)]
    bm = [kb.sb(T0, "bm%d" % i, [128, 512], F32) for i in range(2)]
    mrow = [kb.sb(T0, "mrow%d" % i, [128, 512], F32) for i in range(2)]
    scr = kb.sb(T0, "scr", [128, 4, 128], F32)
    colt = kb.sb(T0, "colt", [128, 4], F32)

    def mod_dma(j):
        b = j % 2
        kb.dma("pool", wm[b][:], w_mod[:, j * 512:(j + 1) * 512].rearrange("(kc p) n -> p kc n", p=128), "wm%d" % b, True)
        kb.dma("sp", bm[b][:], b_mod[j * 512:(j + 1) * 512].partition_broadcast(128), "bm%d" % b, True)

    kb.dma("sp", rows_s[0:16, :], cond.rearrange("c (k p) -> (c k) p", p=128), "rows_s", True)
    kb.dma("sp", rows[0:8, :], g_attn_pre.rearrange("(k p) -> k p", p=128), "rows", True)
    kb.dma("sp", rows[8:16, :], g_ffn_pre.rearrange("(k p) -> k p", p=128), "rows", True)
    kb.dma("sp", rows[16:18, :], g_q.rearrange("(k p) -> k p", p=128), "rows", True)
    mod_dma(0)
    mod_dma(1)
    kb.dma("pool", w_in_bf[:], w_in.rearrange("(kc p) n -> p kc n", p=128), "w_in", True)

    kb.dma("sp", gkv_bc[:], g_kv.partition_broadcast(128), "gkv_bc", True)
    kb.dma("sp", gsgu_bc[:], g_sgu.partition_broadcast(128), "gsgu_bc", True)
    kb.dma("sp", beta_bc[:], beta_sgu.partition_broadcast(128), "beta_bc", True)
    kb.dma("sp", bT_bc[:].rearrange("p g q -> p (g q)"), b_sgu.rearrange("g p -> (g p)").partition_broadcast(128), "bT_bc", True)
    for c in range(2):
        kb.dma("sp", G_a[:, c, :], g_attn_post.partition_broadcast(128), "G_a%d" % c, True)
    kb.group("pe", [mm(psf(0)[:, 0:18], rows[0:18, :], ident_f[0:18, 0:18], True, True)], reads=["rows", "ident_f"], writes=["ps0"])
    cp("dve", cols[:, :], psf(0)[:, 0:18], ["ps0"], ["cols"])
    act(rows_s[0:16, :], rows_s[0:16, :], AF.Silu, ["rows_s"], ["rows_s"])
    cp("dve", rows_b[0:16, :], rows_s[0:16, :], ["rows_s"], ["rows_b"])
    kb.group("pe", [mm(psf(1)[:, 0:16], rows_b[0:16, :], ident_bf[0:16, 0:16], True, True)], reads=["rows_b", "ident_bf"], writes=["ps1"])
    cp("dve", sT[:, :], psf(1)[:, 0:16], ["ps1"], ["sT"])
    for r in range(16):
        cp("dve" if r % 2 == 0 else "pool", srep[:, r, :], sT[:, r:r + 1].to_broadcast([128, 128]), ["sT"], ["srep"])

    def emit_mod(j, Gt, gname, which):
        vec, half = (j // 2) % 3, j % 2
        b = j % 2
        for c in range(2):
            pb = 4 + c
            kb.group("pe", [mm(psf(pb), srep[:, c * 8 + kc, :], wm[b][:, kc, :], kc == 0, kc == 7) for kc in range(8)],
                     reads=["srep", "wm%d" % b], writes=["ps%d" % pb])
            tt("dve", mrow[c][:], psf(pb), bm[b][:], ALU.add, ["ps%d" % pb, "bm%d" % b], ["mrow%d" % c])
            if vec == 2:
                sl = slice(half * 512, (half + 1) * 512)
                tt("dve", Gt[:, c, sl], Gt[:, c, sl], mrow[c][:], ALU.mult, ["%s%d" % (gname, c), "mrow%d" % c], ["%s%d" % (gname, c)])
            else:
                for a in range(4):
                    tt("dve", scr[:, a, :], mrow[c][:, a * 128:(a + 1) * 128], ident_f[:, :], ALU.mult, ["mrow%d" % c, "ident_f"], ["scr"])
                kb.op("dve", lambda e: e.tensor_reduce(out=colt[:, :], in_=scr[:, :, :], axis=AX.X, op=ALU.add), reads=["scr"], writes=["colt"])
                if vec == 0:
                    cp("dve", modB[:, which, c, half * 4:(half + 1) * 4], colt[:, :], ["colt"], ["modB"])
                else:
                    gc = cols[:, which * 8 + half * 4: which * 8 + half * 4 + 4]
                    stt("dve", modA[:, which, c, half * 4:(half + 1) * 4], colt[:, :], 1.0, gc, ALU.add, ALU.mult, ["colt", "cols"], ["modA"])

    def rope_gen():
        kb.dma("sp", mt[:], meta.partition_broadcast(128), "mt", True)
        yield
        kb.op("pool", lambda e: e.iota(pidx[:], pattern=[[0, 1]], base=0, channel_multiplier=1), writes=["pidx"])
        yield
        kb.op("pool", lambda e: e.iota(rc[:, 0:16], pattern=[[1, 16]], base=0, channel_multiplier=0, allow_small_or_imprecise_dtypes=True), writes=["rc"])
        yield
        kb.op("pool", lambda e: e.iota(rc[:, 16:80], pattern=[[1, 64]], base=0, channel_multiplier=0, allow_small_or_imprecise_dtypes=True), writes=["rc"])
        yield
        ts("dve", ktmp[:], pidx[:], 1, 7, ALU.arith_shift_right, ALU.bitwise_and, ["pidx"], ["ktmp"])
        yield
        cp("dve", kf[:], ktmp[:], ["ktmp"], ["kf"])
        yield
        act(freq[:], kf[:], AF.Exp, ["kf"], ["freq"], scale=-math.log(10000.0) / 8.0)
        yield
        ts("dve", ktmp[:], pidx[:], 4, 1, ALU.arith_shift_right, ALU.bitwise_and, ["pidx"], ["ktmp"])
        yield
        cp("dve", kf[:], ktmp[:], ["ktmp", "freq"], ["kf"])
        yield
        ts("dve", isrow[:], kf[:], -1.0, 1.0, ALU.mult, ALU.add, ["kf"], ["isrow"])
        yield
        cp("dve", notrow[:], kf[:], ["kf"], ["notrow"])
        yield
        ts("dve", rc[:, 0:8], rc[:, 0:8], mt[:, 0:1], None, ALU.add, None, ["rc", "mt"], ["rc"])
        yield
        ts("dve", rc[:, 8:16], rc[:, 8:16], mt[:, 1:2], -8.0, ALU.add, ALU.add, ["rc", "mt"], ["rc"])
        yield
        ts("dve", rc[:, :], rc[:, :], freq[:, 0:1], None, ALU.mult, None, ["rc", "freq"], ["rc"])
        yield

        def sin_of(dst, dname, shift):
            ts("dve", tm80[:], rc[:], shift, None, ALU.add, None, ["rc"], ["tm80"])
            yield
            ts("dve", m80[:], tm80[:], 1.0 / (2 * math.pi), None, ALU.mult, None, ["tm80"], ["m80"])
            yield
            cp("dve", ti80[:], m80[:], ["m80"], ["ti80"])
            yield
            cp("dve", m80[:], ti80[:], ["ti80"], ["m80"])
            yield
            stt("dve", tm80[:], m80[:], -2 * math.pi, tm80[:], ALU.mult, ALU.add, ["m80", "tm80"], ["tm80"])
            yield
            ts("dve", m80[:], tm80[:], math.pi, -2 * math.pi, ALU.is_gt, ALU.mult, ["tm80"], ["m80"])
            yield
            tt("dve", tm80[:], tm80[:], m80[:], ALU.add, ["tm80", "m80"], ["tm80"])
            yield
            ts("dve", m80[:], tm80[:], -math.pi, 2 * math.pi, ALU.is_lt, ALU.mult, ["tm80"], ["m80"])
            yield
            tt("dve", tm80[:], tm80[:], m80[:], ALU.add, ["tm80", "m80"], ["tm80"])
            yield
            act(dst[:], tm80[:], AF.Sin, ["tm80"], [dname])
            yield
        yield from sin_of(sn80, "sn80", 0.0)
        yield from sin_of(cs80, "cs80", 0.5 * math.pi)
        for tab, src, sname, tname in ((ST, sn80, "sn80", "ST"), (CT, cs80, "cs80", "CT")):
            tv = tab[:, :].rearrange("p (r c) -> p r c", r=16)
            rb = src[:, 0:16].rearrange("p (r o) -> p r o", o=1).broadcast_to([128, 16, 64])
            cb = src[:, 16:80].rearrange("p (o c) -> p o c", o=1).broadcast_to([128, 16, 64])
            ts("dve", tv, rb, isrow[:, 0:1], None, ALU.mult, None, [sname, "isrow"], [tname])
            yield
            stt("dve", tv, cb, notrow[:, 0:1], tv, ALU.mult, ALU.add, [sname, "notrow", tname], [tname])
            yield

    rope_it = rope_gen()

    def rope_steps(n):
        for _ in range(n):
            try:
                next(rope_it)
            except StopIteration:
                return

    emit_mod(0, G_a, "G_a", 0)
    rope_steps(12)
    mod_dma(2)
    kb.dma("pool", w_uq_bf[:].rearrange("p c h d -> p c (h d)"), w_uq.rearrange("(kc p) n -> p kc n", p=128), "w_uq", True)
    kb.dma("pool", w_ukv_bf[:].rearrange("p h d -> p (h d)"), w_ukv, "w_ukv", True)
    kb.dma("pool", wsg_ld[:], w_sgu.rearrange("g p q -> p g q"), "wsg_ld", True)
    emit_mod(1, G_a, "G_a", 0)
    rope_steps(12)
    mod_dma(3)
    emit_mod(2, G_a, "G_a", 0)
    rope_steps(14)
    emit_mod(3, G_a, "G_a", 0)
    rope_steps(100)

    cp("pool", bTb[:, :, :], bT_bc[:, :, :], ["bT_bc"], ["bTb"])
    kb.op("pool", lambda e: e.memset(wkr_pad[:], 0.0), writes=["wkr_pad"])
    kb.op("pool", lambda e: e.memset(wkr_swp[:], 0.0), writes=["wkr_swp"])
    kb.op("pool", lambda e: e.memset(w_uq_swp[:], 0.0), writes=["w_uq_swp"])
    cp("pool", wkr_pad[:, :, 64:96], w_in_bf[:, :, 384:416], ["w_in"], ["wkr_pad"])
    ts("dve", wkr_swp[:, :, 64:96:2], w_in_bf[:, :, 385:416:2], -1.0, None, ALU.mult, None, ["w_in"], ["wkr_swp"])
    cp("dve", wkr_swp[:, :, 65:96:2], w_in_bf[:, :, 384:416:2], ["w_in"], ["wkr_swp"])
    for c in range(2):
        ts("dve", w_uq_swp[:, c, :, 64:96:2], w_uq_bf[:, c, :, 65:96:2], -1.0, None, ALU.mult, None, ["w_uq"], ["w_uq_swp"])
        cp("dve", w_uq_swp[:, c, :, 65:96:2], w_uq_bf[:, c, :, 64:96:2], ["w_uq"], ["w_uq_swp"])
    for g in range(8):
        b = g % 2
        kb.group("pe", [tr(psh(b)[:, 0:128], wsg_ld[:, g, :], ident_bf[:, :])], reads=["wsg_ld", "ident_bf"], writes=["ps%d" % b])
        cp("dve", wsguT[:, g, :], psh(b)[:, 0:128], ["ps%d" % b], ["wsguT"])


    kb.barrier(exclude=("w_o",))
    T0.close()

    W1 = contextlib.ExitStack()
    hT = kb.sb(W1, "hT", [128, 8, 512], BF16)
    xq = kb.sb(W1, "xq", [128, 4096], BF16)
    uT = kb.sb(W1, "uT", [128, 4, 512], BF16)
    gv = kb.sb(W1, "gv", [128, 4, 512], F32)
    vn = kb.sb(W1, "vn", [128, 4, 512], BF16)
    cqn_bf = kb.sb(W1, "cqn_bf", [128, 4, 256], BF16)
    ckvn_f = kb.sb(W1, "ckvn_f", [128, 4, 128], F32)
    ckvn_bf = kb.sb(W1, "ckvn_bf", [128, 4, 128], BF16)
    kr_f = kb.sb(W1, "kr_f", [128, 4, 32], F32)
    cqnT = kb.sb(W1, "cqnT", [128, 2, 512], BF16)
    ckvT = kb.sb(W1, "ckvT", [128, 1280], BF16)
    krT = kb.sb(W1, "krT", [128, 1280], BF16)
    KTh = [kb.sb(W1, "KTh%d" % i, [128, 1280], BF16) for i in range(2)]
    Vt = kb.sb(W1, "Vt", [128, 10, 512], BF16)
    PT = [kb.sb(W1, "PT%d" % i, [128, 512], BF16) for i in range(3)]
    rec = kb.sb(W1, "rec", [128, 512], F32)
    junk = kb.sb(W1, "junk", [128, 1024], BF16)
    cc_ld = kb.sb(W1, "cc_ld", [128, 2, 128], BF16)
    ck_ld = kb.sb(W1, "ck_ld", [128, 2, 96], BF16)
    xnb = xq[:, :].rearrange("p (t d) -> p t d", t=4)
    qT = xq[:, :].rearrange("p (h n) -> p h n", h=8)

    wmL = x1[:, 8:10, :].rearrange("p t d -> p (t d)").bitcast(BF16).rearrange("p (k n) -> p k n", k=8)
    wmLn = ["x1.8", "x1.9"]
    bmL = x1[:, 10, 0:512]
    mrowL = [x1[:, 10, 512:1024], x1[:, 11, 0:512]]
    scrL = x1[:, 11, 512:1024].rearrange("p (a q) -> p a q", a=4)

    def late_dma(j):
        kb._deps("pool", [], [wmLn[1]])
        ev_ = kb.dma("pool", wmL[:, :, :], w_mod[:, j * 512:(j + 1) * 512].rearrange("(kc p) n -> p kc n", p=128), wmLn[0], True, semkey="late.a")
        st_ = kb._r(wmLn[1])
        st_[0] = ev_
        st_[1] = []
        kb.dma("pool", bmL, b_mod[j * 512:(j + 1) * 512].partition_broadcast(128), "x1.10", True, semkey="late.c")

    def late_mod(j):
        which = j // 6
        vec, half = (j // 2) % 3, j % 2
        for c in range(2):
            pb = 2 + c
            kb.group("pe", [mm(psf(pb), srep[:, c * 8 + kc, :], wmL[:, kc, :], kc == 0, kc == 7) for kc in range(8)],
                     reads=["srep"] + wmLn, writes=["ps%d" % pb])
            mn = "x1.10" if c == 0 else "x1.11"
            tt("dve", mrowL[c], psf(pb), bmL, ALU.add, ["ps%d" % pb, "x1.10"], [mn])
            if vec == 2 and which == 0:
                sl = slice(half * 512, (half + 1) * 512)
                tt("dve", G_a[:, c, sl], G_a[:, c, sl], mrowL[c], ALU.mult, ["G_a%d" % c, mn], ["G_a%d" % c])
            else:
                tt("dve", scrL, mrowL[c].rearrange("p (a q) -> p a q", a=4),
                   ident_f[:, :].rearrange("p (o q) -> p o q", o=1).broadcast_to([128, 4, 128]), ALU.mult, [mn, "ident_f"], ["x1.11"])
                kb.op("dve", lambda e: e.tensor_reduce(out=coltL[:, :], in_=scrL, axis=AX.X, op=ALU.add), reads=["x1.11"], writes=["coltL"])
                if vec == 0:
                    cp("dve", modB[:, which, c, half * 4:(half + 1) * 4], coltL[:, :], ["coltL"], ["modB"])
                elif vec == 1:
                    gc = cols[:, which * 8 + half * 4: which * 8 + half * 4 + 4]
                    stt("dve", modA[:, which, c, half * 4:(half + 1) * 4], coltL[:, :], 1.0, gc, ALU.add, ALU.mult, ["coltL", "cols"], ["modA"])
                else:
                    cp("dve", gcolF[:, c, half * 4:(half + 1) * 4], coltL[:, :], ["coltL"], ["gcolF"])

    late_next = [4]

    def late_step():
        j = late_next[0]
        if j >= 12:
            return
        late_mod(j)
        if j + 1 < 12:
            late_dma(j + 1)
        late_next[0] = j + 1

    stage_hooks = []
    head_hooks = []
    deferred = []

    def drain(n):
        for _ in range(n):
            if deferred:
                deferred.pop(0)()

    def front_A(x_src, slots):
        for t in range(4):
            kb.dma("sp", x1[:, slots[t], :], x_src[t * 128:(t + 1) * 128, :], "x1.%d" % slots[t], True)
            act(junk[:, :], x1[:, slots[t], :], AF.Square, ["x1.%d" % slots[t]], ["junk", "stA%d" % t], accum_out=stat[:, t:t + 1])
        rstd_from_ss(stat[:, 4:8], stat[:, 0:4], D, ["stA0", "stA1", "stA2", "stA3"], ["stA_r"])
        for t in range(4):
            ts("dve", xnb[:, t, :], x1[:, slots[t], :], stat[:, 4 + t:5 + t], None, ALU.mult, None,
               ["x1.%d" % slots[t], "stA_r"], ["xq"])

    def front_B(slots, c, full, kcol0, rope0, out_row0):
        for kc in range(8):
            b = kc % 2
            kb.group("pe", [tr(psh(b)[:, t * 128:(t + 1) * 128], xnb[:, t, kc * 128:(kc + 1) * 128], ident_bf[:, :]) for t in range(4)],
                     reads=["xq", "ident_bf"], writes=["ps%d" % b])
            if kc % 2 == 0:
                act(hT[:, kc, :], psh(b)[:, 0:512], AF.Identity, ["ps%d" % b, "modA", "modB"], ["hT.%d" % kc],
                    scale=modA[:, 0, c, kc:kc + 1], bias=modB[:, 0, c, kc:kc + 1])
            else:
                ts("dve", hT[:, kc, :], psh(b)[:, 0:512], modA[:, 0, c, kc:kc + 1], modB[:, 0, c, kc:kc + 1], ALU.mult, ALU.add,
                   ["ps%d" % b, "modA", "modB"], ["hT.%d" % kc])
        hreads = ["hT.%d" % k for k in range(8)]
        lo = 0 if full else 256
        for t in range(4):
            b = 2 + t
            kb.group("pe", [mm(psf(b)[:, lo:416], hT[:, kc, t * 128:(t + 1) * 128], w_in_bf[:, kc, lo:416], kc == 0, kc == 7) for kc in range(8)],
                     reads=hreads + ["w_in"], writes=["ps%d" % b])
            o = 64 + 4 * t
            sc, scr_ = "stC%d" % t, "stCr%d" % t
            if full:
                act(junk[:, 0:256], psf(b)[:, 0:256], AF.Square, ["ps%d" % b], ["junk", sc], accum_out=stat[:, o:o + 1], scale=1.0 / 16.0)
            act(junk[:, 256:384], psf(b)[:, 256:384], AF.Square, ["ps%d" % b], ["junk", sc], accum_out=stat[:, o + 1:o + 2], scale=128.0 ** -0.5)
            l0 = o if full else o + 1
            act(stat[:, l0 + 2:o + 4], stat[:, l0:o + 2], AF.Ln, [sc, "eps"], [scr_], scale=1.0, bias=eps_col[:, 0:1])
            act(stat[:, l0 + 2:o + 4], stat[:, l0 + 2:o + 4], AF.Exp, [scr_], [scr_], scale=-0.5)
            if full:
                ts("dve", cqn_bf[:, t, :], psf(b)[:, 0:256], stat[:, o + 2:o + 3], None, ALU.mult, None, ["ps%d" % b, scr_], ["cqn_bf"])
            stt("dve", ckvn_f[:, t, :], psf(b)[:, 256:384], stat[:, o + 3:o + 4], gkv_bc[:, :], ALU.mult, ALU.mult,
                ["ps%d" % b, scr_, "gkv_bc"], ["ckvn_f.%d" % t])
            cp("pool", ckvn_bf[:, t, :], ckvn_f[:, t, :], ["ckvn_f.%d" % t], ["ckvn_bf"])
            if out_row0 is not None:
                cp("dve", kr_f[:, t, :], psf(b)[:, 384:416], ["ps%d" % b], ["kr_f.%d" % t])
                kb.dma("sp", nckv[out_row0 + t * 128: out_row0 + (t + 1) * 128, :], ckvn_f[:, t, :], "ckvn_f.%d" % t, False)
                kb.dma("sp", nkr[out_row0 + t * 128: out_row0 + (t + 1) * 128, :], kr_f[:, t, :], "kr_f.%d" % t, False)
        kb.group("pe", [tr(psh(0)[:, t * 128:(t + 1) * 128], ckvn_bf[:, t, :], ident_bf[:, :]) for t in range(4)],
                 reads=["ckvn_bf", "ident_bf"], writes=["ps0"])
        cp("dve", ckvT[:, kcol0:kcol0 + 512], psh(0)[:, 0:512], ["ps0"], ["ckvT.%d" % (kcol0 // 256), "ckvT.%d" % (kcol0 // 256 + 1)])
        if full:
            for cc in range(2):
                kb.group("pe", [tr(psh(1)[:, t * 128:(t + 1) * 128], cqn_bf[:, t, cc * 128:(cc + 1) * 128], ident_bf[:, :]) for t in range(4)],
                         reads=["cqn_bf", "ident_bf"], writes=["ps1"])
                act(cqnT[:, cc, :], psh(1)[:, 0:512], AF.Copy, ["ps1", "cols"], ["cqnT"], scale=cols[:, 16 + cc:17 + cc])
        kres = ["krT.%d" % (kcol0 // 256), "krT.%d" % (kcol0 // 256 + 1)]
        kb.group("pe", [mm(psf(4)[0:96, :], wkr_pad[:, kc, :], hT[:, kc, :], kc == 0, kc == 7) for kc in range(8)],
                 reads=hreads + ["wkr_pad"], writes=["ps4"])
        if rope0 is None:
            cp("dve", krT[64:96, kcol0:kcol0 + 512], psf(4)[64:96, :], ["ps4"], kres)
        else:
            kb.group("pe", [mm(psf(5)[0:96, :], wkr_swp[:, kc, :], hT[:, kc, :], kc == 0, kc == 7) for kc in range(8)],
                     reads=hreads + ["wkr_swp"], writes=["ps5"])
            g0 = gv[64:96, 0, :]
            g1 = gv[64:96, 1, :]
            tt("dve", g0, psf(4)[64:96, :], CT[64:96, rope0:rope0 + 512], ALU.mult, ["ps4", "CT"], ["gv.0"])
            tt("dve", g1, psf(5)[64:96, :], ST[64:96, rope0:rope0 + 512], ALU.mult, ["ps5", "ST"], ["gv.1"])
            tt("dve", krT[64:96, kcol0:kcol0 + 512], g0, g1, ALU.add, ["gv.0", "gv.1"], kres)
        if not full:
            return
        if stage_hooks:
            stage_hooks.pop(0)()
        for cu in range(4):
            b = 4 + cu % 2
            kb.group("pe", [mm(psf(b), w_in_bf[:, kc, 416 + cu * 128: 416 + (cu + 1) * 128], hT[:, kc, :], kc == 0, kc == 7) for kc in range(8)],
                     reads=hreads + ["w_in"], writes=["ps%d" % b])
            act(uT[:, cu, :], psf(b), AF.Gelu_apprx_tanh, ["ps%d" % b], ["uT"])
        for t in range(4):
            b = 2 + t % 2
            kb.group("pe", [mm(psf(b), hT[:, kc, t * 128:(t + 1) * 128], w_in_bf[:, kc, 928:1440], kc == 0, kc == 7) for kc in range(8)],
                     reads=hreads + ["w_in"], writes=["ps%d" % b])
            act(gv[:, t, :], psf(b), AF.Gelu_apprx_tanh, ["ps%d" % b], ["gv.%d" % t, "stV"], accum_out=stat[:, 16 + t:17 + t])
            act(junk[:, 0:512], gv[:, t, :], AF.Square, ["gv.%d" % t], ["junk", "stV"], accum_out=stat[:, 20 + t:21 + t])
        if stage_hooks:
            stage_hooks.pop(0)()
        for h in range(8):
            b = 4 + h % 2
            if rope0 is None:
                kb.group("pe", [mm(psf(b)[0:96, :], w_uq_bf[:, cc, h, :], cqnT[:, cc, :], cc == 0, cc == 1) for cc in range(2)],
                         reads=["w_uq", "cqnT"], writes=["ps%d" % b])
                if h % 2 == 0:
                    act(qT[0:96, h, :], psf(b)[0:96, :], AF.Copy, ["ps%d" % b], ["xq"])
                else:
                    cp("dve", qT[0:96, h, :], psf(b)[0:96, :], ["ps%d" % b], ["xq"])
            else:
                kb.group("pe", [mm(psf(b)[0:96, :], w_uq_bf[:, cc, h, :], cqnT[:, cc, :], cc == 0, cc == 1) for cc in range(2)],
                         reads=["w_uq", "cqnT"], writes=["ps%d" % b])
                b2 = 6 + h % 2
                kb.group("pe", [mm(psf(b2)[0:96, :], w_uq_swp[:, cc, h, :], cqnT[:, cc, :], cc == 0, cc == 1) for cc in range(2)],
                         reads=["w_uq_swp", "cqnT"], writes=["ps%d" % b2])
                vnf = vn[:, :, :].rearrange("p t d -> p (t d)").bitcast(F32)
                g0 = vnf[64:96, 0:512]
                g1 = vnf[64:96, 512:1024]
                tt("dve", g0, psf(b)[64:96, :], CT[64:96, rope0:rope0 + 512], ALU.mult, ["ps%d" % b, "CT", "vn"], ["vn", "qlock%d" % b])
                act(qT[0:64, h, :], psf(b)[0:64, :], AF.Copy, ["ps%d" % b, "qlock%d" % b], ["xq"])
                tt("dve", g1, psf(b2)[64:96, :], ST[64:96, rope0:rope0 + 512], ALU.mult, ["ps%d" % b2, "ST", "vn"], ["vn"])
                tt("dve", qT[64:96, h, :], g0, g1, ALU.add, ["vn"], ["xq"])

        ts("dve", stat[:, 24:28], stat[:, 16:20], 1.0 / 512, None, ALU.mult, None, ["stV"], ["stVm"])
        tt("dve", stat[:, 28:32], stat[:, 24:28], stat[:, 24:28], ALU.mult, ["stVm"], ["stVq"])
        stt("dve", stat[:, 28:32], stat[:, 20:24], 1.0 / 512, stat[:, 28:32], ALU.mult, ALU.subtract, ["stV", "stVq"], ["stVq"])
        act(stat[:, 28:32], stat[:, 28:32], AF.Ln, ["stVq", "eps"], ["stVq"], scale=1.0, bias=eps_col[:, 0:1])
        act(stat[:, 28:32], stat[:, 28:32], AF.Exp, ["stVq"], ["stVq"], scale=-0.5)
        for t in range(4):
            deferred.append(lambda t=t: ts("dve", gv[:, t, :], gv[:, t, :], stat[:, 24 + t:25 + t], stat[:, 28 + t:29 + t], ALU.subtract, ALU.mult,
                                           ["gv.%d" % t, "stVm", "stVq"], ["gv.%d" % t]))
            deferred.append(lambda t=t: tt("dve", gv[:, t, :], gv[:, t, :], gsgu_bc[:, :], ALU.mult, ["gv.%d" % t, "gsgu_bc"], ["gv.%d" % t]))
            deferred.append(lambda t=t: tt("dve", vn[:, t, :], gv[:, t, :], beta_bc[:, :], ALU.add, ["gv.%d" % t, "beta_bc"], ["vn"]))

    def build_V(kt_list, kcol_of):
        for i, kt in enumerate(kt_list):
            b = 2 + i % 2
            k0 = kcol_of(kt)
            kb.group("pe", [mm(psf(b), ckvT[:, k0:k0 + 128], w_ukv_bf[:, :, 64:128], True, True)],
                     reads=["ckvT.%d" % (k0 // 256), "w_ukv"], writes=["ps%d" % b])
            vdst = Vt[:, kt, :]
            vsrc = psf(b)
            if i % 2 == 0:
                cp("dve", vdst, vsrc, ["ps%d" % b], ["Vt.%d" % (kt // 2)])
            else:
                act(vdst, vsrc, AF.Copy, ["ps%d" % b], ["Vt.%d" % (kt // 2)])

    def build_K(h, k0, nk):
        kt_buf = KTh[h % 2]
        kname = "KTh%d" % (h % 2)
        kslots = ["KT.%d" % i for i in (range(4) if h % 2 == 0 else range(4, 8))]
        nkt = nk // 128
        kblocks = sorted(set((k0 + i * 128) // 256 for i in range(nkt)))
        nchunks = (nk + 511) // 512
        for ci in range(nchunks):
            c0 = ci * 512
            n = min(512, nk - c0)
            b = 2 + ci % 2
            kb.group("pe", [mm(psf(b)[:, 0:n], w_ukv_bf[:, h, :], ckvT[:, k0 + c0:k0 + c0 + n], True, True)],
                     reads=["w_ukv"] + ["ckvT.%d" % kbk for kbk in kblocks], writes=["ps%d" % b])
            cp("dve", kt_buf[0:64, c0:c0 + n], psf(b)[0:64, 0:n], ["ps%d" % b], [kname] + kslots)
        cp("pool", kt_buf[64:96, 0:nk], krT[64:96, k0:k0 + nk], ["krT.%d" % kbk for kbk in kblocks], [kname] + kslots)

    def attention(q0, nq, k0, nk, vt0, cat_col0):
        nkt = nk // 128

        def normalise(h):
            pacc, pden = 4 + h % 2, h % 2
            po = (h % 2) * 64
            den = psf(pden)[po:po + 64, 0:nq]
            if nq <= 256 and h % 2 == 1:
                act(rec[po:po + 64, 0:nq], den, AF.Ln, ["ps%d" % pden], ["rec%d" % (h % 2)])
                act(rec[po:po + 64, 0:nq], rec[po:po + 64, 0:nq], AF.Exp, ["rec%d" % (h % 2)], ["rec%d" % (h % 2)], scale=-1.0)
            else:
                kb.op("dve", lambda e: e.reciprocal(out=rec[po:po + 64, 0:nq], in_=den), reads=["ps%d" % pden], writes=["rec%d" % (h % 2)])
            tt("dve", hT[po:po + 64, h // 2, cat_col0:cat_col0 + nq], psf(pacc)[po:po + 64, 0:nq], rec[po:po + 64, 0:nq], ALU.mult,
               ["ps%d" % pacc, "rec%d" % (h % 2)], ["hT.%d" % (h // 2)])

        def lw_of(h, kt):
            hp = h - (h % 2)
            return Vt[:, vt0 + kt, hp * 64:(hp + 2) * 64]

        if nkt * nq <= 512:
            kblk = ["ckvT.%d" % kbk for kbk in sorted(set((k0 + i * 128) // 256 for i in range(nkt)))]
            krblk = ["krT.%d" % kbk for kbk in sorted(set((k0 + i * 128) // 256 for i in range(nkt)))]

            def kslot(h):
                return KTh[h // 4], (h % 4) * nk
            for h_ in range(8):
                kbuf, kc0 = kslot(h_)
                cp("pool", kbuf[64:96, kc0:kc0 + nk], krT[64:96, k0:k0 + nk], krblk, ["KT.%d" % h_])
            for hp in range(4):
                b = 2 + hp % 2
                kb.group("pe", [mm(psf(b)[:, i * nk:(i + 1) * nk], w_ukv_bf[:, 2 * hp + i, :], ckvT[:, k0:k0 + nk], True, True) for i in range(2)],
                         reads=["w_ukv"] + kblk, writes=["ps%d" % b])
                kbuf, kc0 = kslot(2 * hp)
                if hp % 2 == 0:
                    cp("dve", kbuf[0:64, kc0:kc0 + 2 * nk], psf(b)[0:64, 0:2 * nk], ["ps%d" % b], ["KT.%d" % (2 * hp), "KT.%d" % (2 * hp + 1)])
                else:
                    act(kbuf[0:64, kc0:kc0 + 2 * nk], psf(b)[0:64, 0:2 * nk], AF.Copy, ["ps%d" % b], ["KT.%d" % (2 * hp), "KT.%d" % (2 * hp + 1)])

            def s_exp(h):
                b = 6 + h % 2
                kbuf, kc0 = kslot(h)
                kb.group("pe", [mm(psf(b)[:, kt * nq:(kt + 1) * nq], kbuf[0:96, kc0 + kt * 128:kc0 + (kt + 1) * 128], qT[0:96, h, q0:q0 + nq], True, True)
                                for kt in range(nkt)], reads=["KT.%d" % h, "xq"], writes=["ps%d" % b])
                act(PT[h % 3][:, 0:nkt * nq], psf(b)[:, 0:nkt * nq], AF.Exp, ["ps%d" % b], ["PT%d" % (h % 3)], scale=ATTN_SCALE)
            s_exp(0)
            for h in range(8):
                if h + 1 < 8:
                    s_exp(h + 1)
                pacc, pden = 4 + h % 2, h % 2
                pt = PT[h % 3]
                fns = []
                for kt in range(nkt):
                    fns.append(mm(psf(pacc)[:, 0:nq], lw_of(h, kt), pt[:, kt * nq:(kt + 1) * nq], kt == 0, kt == nkt - 1))
                    fns.append(mm(psf(pden)[:, 0:nq], ones_bf[:, :], pt[:, kt * nq:(kt + 1) * nq], kt == 0, kt == nkt - 1))
                kb.group("pe", fns, reads=["PT%d" % (h % 3), "ones_bf"] + ["Vt.%d" % ((vt0 + kt) // 2) for kt in range(nkt)],
                         writes=["ps%d" % pacc, "ps%d" % pden])
                normalise(h)
                drain(1)
            return
        build_K(0, k0, nk)
        build_K(1, k0, nk)
        for h in range(8):
            kt_buf = KTh[h % 2]
            kname = "KTh%d" % (h % 2)
            pacc = 4 + h % 2
            pden = h % 2

            def s_mm(kt):
                b = 6 + kt % 2
                kb.group("pe", [mm(psf(b)[:, 0:nq], kt_buf[0:96, kt * 128:(kt + 1) * 128], qT[0:96, h, q0:q0 + nq], True, True)],
                         reads=[kname, "xq"] + ["KT.%d" % i for i in (range(4) if h % 2 == 0 else range(4, 8))], writes=["ps%d" % b])
            s_mm(0)
            for kt in range(nkt):
                if kt + 1 < nkt:
                    s_mm(kt + 1)
                b = 6 + kt % 2
                pt = PT[kt % 3]
                ptn = "PT%d" % (kt % 3)
                act(pt[:, 0:nq], psf(b)[:, 0:nq], AF.Exp, ["ps%d" % b], [ptn], scale=ATTN_SCALE)
                kb.group("pe", [mm(psf(pacc)[:, 0:nq], lw_of(h, kt), pt[:, 0:nq], kt == 0, kt == nkt - 1),
                                mm(psf(pden)[:, 0:nq], ones_bf[:, :], pt[:, 0:nq], kt == 0, kt == nkt - 1)],
                         reads=[ptn, "Vt.%d" % ((vt0 + kt) // 2), "ones_bf"],
                         writes=(["ps%d" % pacc, "ps%d" % pden] if kt in (0, nkt - 1) else []))
            if h + 2 < 8:
                build_K(h + 2, k0, nk)
            normalise(h)
            drain(2)
            if head_hooks:
                head_hooks.pop(0)()

    def sgu_and_out(slots, c, ydst):
        drain(100)
        for k in range(4):
            pa, pb_ = (4, 5) if k % 2 == 0 else (6, 7)
            fns = []
            for j in range(4):
                cs = slice(j * 128, (j + 1) * 128)
                fns.append(mm(psf(pa)[:, cs], vn[:, j, k * 128:(k + 1) * 128], wsguT[:, 2 * k, :], True, False))
                fns.append(mm(psf(pa)[:, cs], cst_bf[:, :], bTb[:, 2 * k, :], False, True))
                fns.append(mm(psf(pb_)[:, cs], vn[:, j, k * 128:(k + 1) * 128], wsguT[:, 2 * k + 1, :], True, False))
                fns.append(mm(psf(pb_)[:, cs], cst_bf[:, :], bTb[:, 2 * k + 1, :], False, True))
            kb.group("pe", fns, reads=["vn", "wsguT", "cst_bf", "bTb"], writes=["ps%d" % pa, "ps%d" % pb_])
            tt("dve", hT[0:64, 4 + k, :], psf(pa)[0:64, :], uT[0:64, k, :], ALU.mult, ["ps%d" % pa, "uT"], ["hT.%d" % (4 + k)])
            tt("dve", hT[64:128, 4 + k, :], psf(pb_)[64:128, :], uT[64:128, k, :], ALU.mult, ["ps%d" % pb_, "uT"], ["hT.%d" % (4 + k)])
        creads = ["hT.%d" % k for k in range(8)]
        for t in range(4):
            o = 32 + 4 * (t % 2)
            sm, smr = "stM%d" % (t % 2), "stMr%d" % (t % 2)
            for half in range(2):
                b = 2 + 2 * (t % 2) + half
                kb.group("pe", [mm(psf(b), hT[:, kc, t * 128:(t + 1) * 128], w_o_bf[:, kc, half * 512:(half + 1) * 512], kc == 0, kc == 7) for kc in range(8)],
                         reads=creads + ["w_o"], writes=["ps%d" % b])
                act(junk[:, 0:512], psf(b), AF.Square, ["ps%d" % b], ["junk", sm], accum_out=stat[:, o + half:o + half + 1])
            tt("dve", stat[:, o + 2:o + 3], stat[:, o:o + 1], stat[:, o + 1:o + 2], ALU.add, [sm], [smr])
            rstd_from_ss(stat[:, o + 3:o + 4], stat[:, o + 2:o + 3], D, [smr], [smr])
            s = slots[t]
            for half in range(2):
                b = 2 + 2 * (t % 2) + half
                sl = slice(half * 512, (half + 1) * 512)
                sc = 1 + half
                stt("dve", gv[:, sc, :], psf(b), stat[:, o + 3:o + 4], G_a[:, c, sl], ALU.mult, ALU.mult,
                    ["ps%d" % b, smr, "G_a%d" % c], ["gv.%d" % sc])
                tt("dve" if half == 0 else "pool", x1[:, s, sl], x1[:, s, sl], gv[:, sc, :], ALU.add, ["gv.%d" % sc, "x1.%d" % s], ["x1.%d" % s])

    kb.dma("pool", cc_ld[:], cckv.rearrange("(t p) d -> p t d", p=128), "cc_ld", True)
    kb.op("pool", lambda e: e.memset(ck_ld[:], 0.0), writes=["ck_ld"])
    kb.dma("pool", ck_ld[:, :, 64:96], ckr.rearrange("(t p) d -> p t d", p=128), "ck_ld", True)
    kb.group("pe", [tr(psh(0)[:, t * 128:(t + 1) * 128], cc_ld[:, t, :], ident_bf[:, :]) for t in range(2)], reads=["cc_ld", "ident_bf"], writes=["ps0"])
    cp("dve", ckvT[:, 0:256], psh(0)[:, 0:256], ["ps0"], ["ckvT.0"])
    kb.group("pe", [tr(psh(1)[0:96, t * 128:(t + 1) * 128], ck_ld[:, t, :], ident_bf[:, :]) for t in range(2)], reads=["ck_ld", "ident_bf"], writes=["ps1"])
    cp("dve", krT[64:96, 0:256], psh(1)[64:96, 0:256], ["ps1"], ["krT.0"])
    late_dma(4)
    front_A(xs[512:1024, :], [4, 5, 6, 7])
    front_B([4, 5, 6, 7], 1, False, 768, 512, None)
    late_step()
    front_A(xs[0:512, :], [0, 1, 2, 3])
    stage_hooks.append(late_step)
    stage_hooks.append(late_step)
    front_B([0, 1, 2, 3], 1, True, 256, 0, None)
    del stage_hooks[:]
    kb.dma("pool", w_o_bf[:], w_o.rearrange("(kc p) n -> p kc n", p=128), "w_o", True)
    build_V(list(range(10)), lambda kt: kt * 128)
    for i_ in range(8):
        head_hooks.append(late_step if i_ in (2, 5) else (lambda: None))
    attention(0, 512, 0, 1280, 0, 0)
    del head_hooks[:]
    front_A(xp[0:512, :], [4, 5, 6, 7])
    sgu_and_out([0, 1, 2, 3], 1, None)
    late_step()
    for g in range(2):
        slots = [4 + 4 * g + t for t in range(4)]
        front_B(slots, 0, True, 0, None, g * 512)
        if g == 0:
            late_step()
        build_V([0, 1, 2, 3], lambda kt: kt * 128)
        for j in range(2):
            attention(j * 256, 256, j * 256, 256, 2 * j, j * 256)
            if g == 0 and j == 0:
                late_step()
        if g == 0:
            while late_next[0] < 12:
                late_step()
            front_A(xp[512:1024, :], [8, 9, 10, 11])
        sgu_and_out(slots, 0, None)

    kb.barrier()
    W1.close()
    P1.close()

    P2 = contextlib.ExitStack()
    G_f = kb.sb(P2, "G_f", [128, 2, D], F32)
    w1_bf = kb.sb(P2, "w1_bf", [128, 8, 4 * D], BF16)
    w2_bf = kb.sb(P2, "w2_bf", [128, 32, D], BF16)
    for c in range(2):
        kb.dma("sp", G_f[:, c, :], g_ffn_post.partition_broadcast(128), "G_f%d" % c, True)
    def wgroup_dma(g):
        kb.dma("pool", w1_bf[:, :, g * 512:(g + 1) * 512], w_ff1[:, g * 512:(g + 1) * 512].rearrange("(kc p) n -> p kc n", p=128), "w1.%d" % g, True)
        kb.dma("pool", w2_bf[:, g * 4:(g + 1) * 4, :], w_ff2[g * 512:(g + 1) * 512, :].rearrange("(kc p) n -> p kc n", p=128), "w2.%d" % g, True)

    for g in range(8):
        wgroup_dma(g)
    T2 = contextlib.ExitStack()
    diagT = kb.sb(T2, "diagT", [128, 4, 128], F32)
    ones_f = kb.sb(T2, "ones_f", [128, 128], F32)
    kb.op("dve", lambda e: e.memset(ones_f[:], 1.0), writes=["ones_f"])
    for c in range(2):
        for half in range(2):
            pb = 4 + half
            for kq in range(4):
                ts("dve", diagT[:, kq, :], ident_f[:, :], gcolF[:, c, half * 4 + kq:half * 4 + kq + 1], None, ALU.mult, None,
                   ["ident_f", "gcolF"], ["diagT"])
            kb.group("pe", [mm(psf(pb)[:, kq * 128:(kq + 1) * 128], ones_f[:, :], diagT[:, kq, :], True, True) for kq in range(4)],
                     reads=["ones_f", "diagT"], writes=["ps%d" % pb])
            sl = slice(half * 512, (half + 1) * 512)
            tt("dve", G_f[:, c, sl], G_f[:, c, sl], psf(pb), ALU.mult, ["G_f%d" % c, "ps%d" % pb], ["G_f%d" % c])
    kb.barrier(exclude=("w1.", "w2."))
    T2.close()

    W2 = contextlib.ExitStack()
    h2T = [kb.sb(W2, "h2T%d" % i, [128, 8, 256], BF16) for i in range(2)]
    xn2 = kb.sb(W2, "xn2", [128, 2, D], BF16)
    NH = 6
    hid = [kb.sb(W2, "hid%d" % i, [128, 256], BF16) for i in range(NH)]
    rl = [kb.sb(W2, "rl%d" % i, [128, 256], F32) for i in range(2)]
    junk2 = kb.sb(W2, "junk2", [128, 512], BF16)
    xn2f = xn2[:, :, :].rearrange("p t d -> p (t d)").bitcast(F32)
    ytmp = [xn2f[:, 0:512], xn2f[:, 512:1024]]

    def prep_stats(sg):
        tiles = [2 * sg, 2 * sg + 1]
        for i, s_ in enumerate(tiles):
            act(xn2[:, i, :], x1[:, s_, :], AF.Square, ["x1.%d" % s_], ["xn2", "yt0", "yt1", "stH"], accum_out=stat[:, 40 + i:41 + i])
        rstd_from_ss(stat[:, 42:44], stat[:, 40:42], D, ["stH"], ["stHr"])
        for i, s_ in enumerate(tiles):
            ts("dve", xn2[:, i, :], x1[:, s_, :], stat[:, 42 + i:43 + i], None, ALU.mult, None, ["x1.%d" % s_, "stHr", "xn2"], ["xn2"])

    def prep_tr(sg, kc):
        c_ = 1 if sg < 2 else 0
        b = kc % 2
        hb = h2T[sg % 2]
        kb.group("pe", [tr(psh(b)[:, i * 128:(i + 1) * 128], xn2[:, i, kc * 128:(kc + 1) * 128], ident_bf[:, :]) for i in range(2)],
                 reads=["xn2", "ident_bf"], writes=["ps%d" % b])
        act(hb[:, kc, :], psh(b)[:, 0:256], AF.Identity, ["ps%d" % b, "modA", "modB"], ["h2T%d.%d" % (sg % 2, kc)],
            scale=modA[:, 1, c_, kc:kc + 1], bias=modB[:, 1, c_, kc:kc + 1])

    prep_stats(0)
    for kc in range(8):
        prep_tr(0, kc)

    LA = NH - 1
    NFC = 32

    def ff1(sg, fc):
        hb_ = h2T[sg % 2]
        b = 2 + fc % 2
        r = fc % 2
        r4 = (sg * NFC + fc) % NH
        kb.group("pe", [mm(psf(b)[:, 0:256], w1_bf[:, kc, fc * 128:(fc + 1) * 128], hb_[:, kc, :], kc == 0, kc == 7) for kc in range(8)],
                 reads=["h2T%d.%d" % (sg % 2, kc) for kc in range(8)] + ["w1.%d" % (fc // 4)], writes=["ps%d" % b])
        act(rl[r][:, :], psf(b)[:, 0:256], AF.Relu, ["ps%d" % b], ["rl%d" % r])
        tt("dve", hid[r4][:, :], rl[r][:, :], rl[r][:, :], ALU.mult, ["rl%d" % r], ["hid%d" % r4])

    def ff2(sg, fc):
        r = (sg * NFC + fc) % NH
        for i in range(2):
            fns = [mm(psf(4 + 2 * i + half), hid[r][:, i * 128:(i + 1) * 128], w2_bf[:, fc, half * 512:(half + 1) * 512], fc == 0, fc == NFC - 1)
                   for half in range(2)]
            kb.group("pe", fns, reads=["hid%d" % r, "w2.%d" % (fc // 4)],
                     writes=(["ps%d" % (4 + 2 * i), "ps%d" % (5 + 2 * i)] if fc in (0, NFC - 1) else []))

    def tail(sg):
        tiles = [2 * sg, 2 * sg + 1]
        c = 1 if sg < 2 else 0
        hdead = h2T[sg % 2][:, :, :].rearrange("p k n -> p (k n)").bitcast(F32)
        hnames = ["h2T%d.%d" % (sg % 2, kc) for kc in range(8)]
        tmps = [(ytmp[0], ["xn2", "yt0"], ["yt0"]), (ytmp[1], ["xn2", "yt1"], ["yt1"]),
                (hdead[:, 0:512], hnames, hnames), (hdead[:, 512:1024], hnames, hnames)]
        for i in range(2):
            for half in range(2):
                b = 4 + 2 * i + half
                sl = slice(half * 512, (half + 1) * 512)
                tb, wn, rn = tmps[2 * i + half]
                act(junk2[:, :], psf(b), AF.Square, ["ps%d" % b], ["junk2", "stJ", "pslock%d" % b], accum_out=stat[:, 44 + 2 * i + half:45 + 2 * i + half])
                tt("dve", tb, psf(b), G_f[:, c, sl], ALU.mult, ["ps%d" % b, "pslock%d" % b, "G_f%d" % c], wn)
        kb.op("dve", lambda e: e.tensor_reduce(out=stat[:, 48:50], in_=stat[:, 44:48].rearrange("p (i h) -> p i h", h=2), axis=AX.X, op=ALU.add),
              reads=["stJ"], writes=["stJr"])
        rstd_from_ss(stat[:, 50:52], stat[:, 48:50], D, ["stJr"], ["stJr"])
        for i, s_ in enumerate(tiles):
            for half in range(2):
                sl = slice(half * 512, (half + 1) * 512)
                tb, wn, rn = tmps[2 * i + half]
                stt("dve", x1[:, s_, sl], tb, stat[:, 50 + i:51 + i], x1[:, s_, sl], ALU.mult, ALU.add,
                    rn + ["stJr", "x1.%d" % s_], ["x1.%d" % s_])
            if s_ < 4:
                kb.dma("sp", ys[s_ * 128:(s_ + 1) * 128, :], x1[:, s_, :], "x1.%d" % s_, False)
            else:
                kb.dma("sp", yp[(s_ - 4) * 128:(s_ - 3) * 128, :], x1[:, s_, :], "x1.%d" % s_, False)

    NS = 6 * NFC
    for n in range(NS + LA):
        if n < NS:
            sg, fc = divmod(n, NFC)
            ff1(sg, fc)
            if sg + 1 < 6:
                if fc == 10:
                    prep_stats(sg + 1)
                if 14 <= fc < 22:
                    prep_tr(sg + 1, fc - 14)
        m = n - LA
        if m >= 0:
            sg2, fc2 = divmod(m, NFC)
            ff2(sg2, fc2)
            if fc2 == NFC - 1:
                tail(sg2)

    kb.finish("sp")
    W2.close()
    P2.close()
    kb.close()
    return nc, dbg_outs


_CACHE = {}


def _get_program():
    if "nc" not in _CACHE:
        _CACHE["nc"] = build_program(DEBUG)
    return _CACHE["nc"]


def make_in_maps(inputs):
    f = lambda a: np.ascontiguousarray(np.asarray(a, dtype=np.float32))
    x_prompt = f(inputs["x_prompt"])
    x_sample = f(inputs["x_sample"])
    cache_ckv = f(inputs["cache_ckv"])
    cache_krope = f(inputs["cache_krope"])
    c = f(inputs["c"])
    c_ctx = f(inputs["c_ctx"])
    shared = {
        "w_mod": f(inputs["w_mod"])[0], "b_mod": f(inputs["b_mod"])[0],
        "g_attn_pre": f(inputs["g_attn_pre"])[0], "g_attn_post": f(inputs["g_attn_post"])[0],
        "w_in": f(inputs["w_in"])[0], "g_q": f(inputs["g_q"])[0], "w_uq": f(inputs["w_uq"])[0],
        "g_kv": f(inputs["g_kv"])[0], "w_ukv": f(inputs["w_ukv"])[0], "w_sgu": f(inputs["w_sgu"])[0],
        "b_sgu": f(inputs["b_sgu"])[0], "g_sgu": f(inputs["g_sgu"])[0], "beta_sgu": f(inputs["beta_sgu"])[0],
        "w_o": f(inputs["w_o"])[0], "g_ffn_pre": f(inputs["g_ffn_pre"])[0], "g_ffn_post": f(inputs["g_ffn_post"])[0],
        "w_ff1": f(inputs["w_ff1"])[0], "w_ff2": f(inputs["w_ff2"])[0],
    }
    in_maps = []
    for i in range(8):
        b, hf = i // 2, i % 2
        own = x_sample[b, hf * 512:(hf + 1) * 512]
        oth = x_sample[b, (1 - hf) * 512:(2 - hf) * 512]
        m = dict(shared)
        m["xp"] = np.ascontiguousarray(x_prompt[4 * i:4 * i + 4].reshape(1024, D))
        m["xs"] = np.ascontiguousarray(np.concatenate([own, oth], axis=0))
        m["cckv"] = np.ascontiguousarray(cache_ckv[b, 0])
        m["ckr"] = np.ascontiguousarray(cache_krope[b, 0])
        m["cond"] = np.ascontiguousarray(np.stack([c_ctx, c[b]], axis=0))
        m["meta"] = np.array([8.0 * hf, 8.0 * (1 - hf)], dtype=np.float32)
        in_maps.append(m)
    return in_maps


def kernel(**inputs):
    nc, _ = _get_program()
    in_maps = make_in_maps(inputs)
    res = run_bass_kernel_spmd(nc, in_maps, core_ids=list(range(8)))
    y_prompt = np.zeros((32, 256, D), np.float32)
    y_sample = np.zeros((4, 1024, D), np.float32)
    new_ckv = np.zeros((32, 1, 256, 128), np.float32)
    new_kr = np.zeros((32, 1, 256, 32), np.float32)
    for i in range(8):
        r = res.results[i]
        b, hf = i // 2, i % 2
        y_prompt[4 * i:4 * i + 4] = np.asarray(r["yp"]).reshape(4, 256, D)
        y_sample[b, hf * 512:(hf + 1) * 512] = np.asarray(r["ys"])
        new_ckv[4 * i:4 * i + 4, 0] = np.asarray(r["nckv"]).reshape(4, 256, 128)
        new_kr[4 * i:4 * i + 4, 0] = np.asarray(r["nkr"]).reshape(4, 256, 32)
    return (y_prompt, y_sample, new_ckv, new_kr)
```
